# Optimizing a Trainium2 kernel written in Bass

```python
import math
import jax, jax.numpy as jnp
from jax import lax
import numpy as np

D_MODEL = 1024
BATCH = 8
SEQ = 4096
DEPTH = 2

N_BRANCH = 4
BRANCH_W = D_MODEL // 4
HEAD_DIM = 64
SB_HEADS = BRANCH_W // HEAD_DIM
SB_BLOCK = 128
SG_CHUNK = 128
SG_GROUPS = 4
SG_GD = BRANCH_W // SG_GROUPS
POOL_WINDOWS = (2, 4, 8, 16)
POOL_GD = BRANCH_W // len(POOL_WINDOWS)
CONV_W = 31
D_FF = 2816
LN_EPS = 1e-5
DN_ALPHA = (2.0 * DEPTH) ** 0.25
DN_BETA = (8.0 * DEPTH) ** -0.25

A_Q0, A_K0, A_V0 = 0, BRANCH_W, 2 * BRANCH_W
B0 = 3 * BRANCH_W
C0 = B0 + 2 * BRANCH_W
D0 = C0 + BRANCH_W
IN_COLS = D0 + 2 * BRANCH_W

kernel_name = "hybrid_gated_sb_gmlp_pool_conv_block"


def layer_norm(x, g, b):
    xf = x.astype(jnp.float32)
    mu = jnp.mean(xf, axis=-1, keepdims=True)
    var = jnp.mean(jnp.square(xf - mu), axis=-1, keepdims=True)
    y = (xf - mu) * lax.rsqrt(var + LN_EPS) * g.astype(jnp.float32) + b.astype(jnp.float32)
    return y.astype(x.dtype)


def swiglu_ffn(x, w_in, w_out):
    a, u = jnp.split(x @ w_in, 2, axis=-1)
    return (jax.nn.silu(a) * u) @ w_out


def stick_breaking_attention(q, k, v):
    S = q.shape[1]
    scale = HEAD_DIM ** -0.5
    outs = []
    for i in range(S // SB_BLOCK):
        q0 = i * SB_BLOCK
        kend = q0 + SB_BLOCK
        qb = q[:, q0:kend]
        kb = k[:, :kend]
        vb = v[:, :kend]
        z = jnp.einsum('bqhd,bkhd->bhqk', qb, kb).astype(jnp.float32) * scale
        t_pos = q0 + jnp.arange(SB_BLOCK)[:, None]
        s_pos = jnp.arange(kend)[None, :]
        causal = s_pos < t_pos
        log_keep = jnp.where(causal, jax.nn.log_sigmoid(-z), 0.0)
        after = lax.cumsum(log_keep, axis=3, reverse=True) - log_keep
        w = jnp.where(causal, jnp.exp(jax.nn.log_sigmoid(z) + after), 0.0)
        outs.append(jnp.einsum('bhqk,bkhd->bqhd', w.astype(vb.dtype), vb))
    return jnp.concatenate(outs, axis=1)


def chunked_spatial_gating(uv, ln_g, ln_b, w_s, b_s):
    B, S, _ = uv.shape
    u, v = jnp.split(uv, 2, axis=-1)
    v = layer_norm(v, ln_g, ln_b)
    vc = v.reshape(B, S // SG_CHUNK, SG_CHUNK, SG_GROUPS, SG_GD)
    mask = jnp.tril(jnp.ones((SG_CHUNK, SG_CHUNK), dtype=bool))
    ws = jnp.where(mask, w_s, jnp.zeros_like(w_s))
    mixed = jnp.einsum('gts,bcsgd->bctgd', ws, vc) + jnp.transpose(b_s)[None, None, :, :, None]
    return u * mixed.reshape(B, S, BRANCH_W)


def multiscale_pool(p, w_grp, scale):
    B, S, _ = p.shape
    pf = p.astype(jnp.float32).reshape(B, S, len(POOL_WINDOWS), POOL_GD)
    cs = jnp.cumsum(pf, axis=1)
    pos = jnp.arange(S, dtype=jnp.float32)
    outs = []
    for gi, win in enumerate(POOL_WINDOWS):
        c = cs[:, :, gi]
        prev = jnp.pad(c, ((0, 0), (win, 0), (0, 0)))[:, :S]
        cnt = jnp.minimum(pos + 1.0, float(win))[None, :, None]
        outs.append((c - prev) / cnt - pf[:, :, gi])
    pooled = jnp.stack(outs, axis=2).astype(p.dtype)
    y = jnp.einsum('bsgc,gcd->bsgd', pooled, w_grp).reshape(B, S, BRANCH_W)
    return y * scale


def conformer_conv(h, w_dw, b_dw, ln_g, ln_b):
    a, g = jnp.split(h, 2, axis=-1)
    y = a * jax.nn.sigmoid(g)
    y = lax.conv_general_dilated(
        y, w_dw[:, None, :], window_strides=(1,), padding=[(CONV_W - 1, 0)],
        dimension_numbers=('NWC', 'WIO', 'NWC'), feature_group_count=BRANCH_W) + b_dw
    y = layer_norm(y, ln_g, ln_b)
    return jax.nn.silu(y)


def hybrid_mixer(x, w_in, gate_w, gate_b, branch_w, out_w, sg_ln_g, sg_ln_b, sg_w, sg_b,
                 pool_w, pool_scale, conv_w, conv_b, conv_ln_g, conv_ln_b):
    B, S, _ = x.shape
    h = x @ w_in
    q = h[..., A_Q0:A_K0].reshape(B, S, SB_HEADS, HEAD_DIM)
    k = h[..., A_K0:A_V0].reshape(B, S, SB_HEADS, HEAD_DIM)
    v = h[..., A_V0:B0].reshape(B, S, SB_HEADS, HEAD_DIM)
    y_a = stick_breaking_attention(q, k, v).reshape(B, S, BRANCH_W)
    y_b = chunked_spatial_gating(jax.nn.gelu(h[..., B0:C0]), sg_ln_g, sg_ln_b, sg_w, sg_b)
    y_c = multiscale_pool(h[..., C0:D0], pool_w, pool_scale)
    y_d = conformer_conv(h[..., D0:IN_COLS], conv_w, conv_b, conv_ln_g, conv_ln_b)
    merged = jnp.zeros_like(x)
    for n, y in enumerate((y_a, y_b, y_c, y_d)):
        gate = jax.nn.sigmoid(x @ gate_w[n] + gate_b[n])
        merged = merged + gate * (y @ branch_w[n])
    return merged @ out_w


def setup_inputs(seed: int = 0) -> dict:
    key = jax.random.key(seed)
    ks = jax.random.split(key, 24)
    f32 = jnp.float32
    L, D, BW = DEPTH, D_MODEL, BRANCH_W
    nrm = lambda k, shape, s: jax.random.normal(k, shape, f32) * s
    return {
        "x": jax.random.normal(ks[0], (BATCH, SEQ, D), f32),
        "ln_g": 1.0 + nrm(ks[1], (L, 3, D), 0.02),
        "ln_b": nrm(ks[2], (L, 3, D), 0.02),
        "ffn_w_in": nrm(ks[3], (L, 2, D, 2 * D_FF), D ** -0.5),
        "ffn_w_out": nrm(ks[4], (L, 2, D_FF, D), DN_BETA * D_FF ** -0.5),
        "mix_w_in": nrm(ks[5], (L, D, IN_COLS), D ** -0.5),
        "gate_w": nrm(ks[6], (L, N_BRANCH, D, D), D ** -0.5),
        "gate_b": nrm(ks[7], (L, N_BRANCH, D), 0.02),
        "branch_w": nrm(ks[8], (L, N_BRANCH, BW, D), BW ** -0.5),
        "out_w": nrm(ks[9], (L, D, D), DN_BETA * D ** -0.5),
        "sg_ln_g": 1.0 + nrm(ks[10], (L, BW), 0.02),
        "sg_ln_b": nrm(ks[11], (L, BW), 0.02),
        "sg_w": nrm(ks[12], (L, SG_GROUPS, SG_CHUNK, SG_CHUNK), SG_CHUNK ** -0.5),
        "sg_b": 1.0 + nrm(ks[13], (L, SG_GROUPS, SG_CHUNK), 0.02),
        "pool_w": nrm(ks[14], (L, len(POOL_WINDOWS), POOL_GD, POOL_GD), POOL_GD ** -0.5),
        "pool_scale": 1.0 + nrm(ks[15], (L, BW), 0.02),
        "conv_w": nrm(ks[16], (L, CONV_W, BW), CONV_W ** -0.5),
        "conv_b": nrm(ks[17], (L, BW), 0.02),
        "conv_ln_g": 1.0 + nrm(ks[18], (L, BW), 0.02),
        "conv_ln_b": nrm(ks[19], (L, BW), 0.02),
    }


def reference(x, ln_g, ln_b, ffn_w_in, ffn_w_out, mix_w_in, gate_w, gate_b, branch_w, out_w,
              sg_ln_g, sg_ln_b, sg_w, sg_b, pool_w, pool_scale, conv_w, conv_b, conv_ln_g, conv_ln_b):
    for l in range(DEPTH):
        x = layer_norm(DN_ALPHA * x + 0.5 * swiglu_ffn(x, ffn_w_in[l, 0], ffn_w_out[l, 0]),
                       ln_g[l, 0], ln_b[l, 0])
        m = hybrid_mixer(x, mix_w_in[l], gate_w[l], gate_b[l], branch_w[l], out_w[l],
                         sg_ln_g[l], sg_ln_b[l], sg_w[l], sg_b[l], pool_w[l], pool_scale[l],
                         conv_w[l], conv_b[l], conv_ln_g[l], conv_ln_b[l])
        x = layer_norm(DN_ALPHA * x + m, ln_g[l, 1], ln_b[l, 1])
        x = layer_norm(DN_ALPHA * x + 0.5 * swiglu_ffn(x, ffn_w_in[l, 1], ffn_w_out[l, 1]),
                       ln_g[l, 2], ln_b[l, 2])
    return x
```

```python
import math
from contextlib import ExitStack

import numpy as np

import concourse.bass as bass
import concourse.mybir as mybir
from concourse.bass_utils import run_bass_kernel_spmd

F32 = mybir.dt.float32
BF16 = mybir.dt.bfloat16
AF = mybir.ActivationFunctionType
ALU = mybir.AluOpType

D_MODEL = 1024
SEQ = 4096
DEPTH = 2
BW = 256
HEAD_DIM = 64
D_FF = 2816
CONV_W = 31
LN_EPS = 1e-5
DN_ALPHA = (2.0 * DEPTH) ** 0.25
POOL_WINDOWS = (2, 4, 8, 16)

TB = 512
NSUB = TB // 128
NKC = D_MODEL // 128
NFC = D_FF // 128
SLOT = 4096
NSLOT = 5


class Buf:
    __slots__ = ("name", "w", "r", "excl")

    def __init__(self, name, excl=False):
        self.name = name
        self.w = None
        self.r = {}
        self.excl = excl


class Op:
    __slots__ = ("eng", "fn", "deps", "signal", "sigval", "dsem", "key", "tag")


class Prog:
    ENGS = ("pe", "act", "dve", "pool", "sp")

    def __init__(self):
        self.ops = {e: [] for e in self.ENGS}
        self.dsem_count = {}
        self.last_dma = {}
        self.tag = ""

    def op(self, eng, fn, reads=(), writes=(), dsem=None):
        o = Op()
        o.eng = eng
        o.fn = fn
        o.signal = False
        o.sigval = None
        o.dsem = dsem
        o.tag = self.tag
        o.key = ("d", dsem) if dsem is not None else ("e", eng)
        deps = {}
        for b in reads:
            if b.w is not None:
                deps[id(b.w)] = b.w
            if b.excl:
                for r in b.r.values():
                    if r.key != o.key:
                        deps[id(r)] = r
        for b in writes:
            if b.w is not None:
                deps[id(b.w)] = b.w
            for r in b.r.values():
                deps[id(r)] = r
        if dsem is not None:
            pd = self.last_dma.get(dsem)
            if pd is not None:
                deps[id(pd)] = pd
            self.last_dma[dsem] = o
        dl = []
        for d in deps.values():
            if d is o:
                continue
            if d.dsem is None and d.eng == "pe" and eng == "pe" and dsem is None:
                continue
            d.signal = True
            dl.append(d)
        o.deps = dl
        for b in reads:
            b.r[o.key] = o
        for b in writes:
            b.w = o
            b.r = {}
        self.ops[eng].append(o)
        return o

    def finalize(self):
        cnt = {e: 0 for e in self.ENGS}
        for e in self.ENGS:
            for o in self.ops[e]:
                if o.dsem is not None:
                    c = self.dsem_count.get(o.dsem, 0) + 16
                    self.dsem_count[o.dsem] = c
                    o.sigval = c
                elif o.signal:
                    cnt[e] += 1
                    o.sigval = cnt[e]
        return cnt

    def emit(self, eng, handle, esems, dsems):
        seen = {}
        dbg = getattr(self, "dbg", None)
        for o in self.ops[eng]:
            if dbg is not None:
                dbg.append((eng, o.fn.__code__.co_firstlineno, [(d.key, d.sigval) for d in o.deps], o.key, o.sigval))
            need = {}
            for d in o.deps:
                k = d.key
                if d.sigval > need.get(k, 0):
                    need[k] = d.sigval
            for k, v in need.items():
                if v > seen.get(k, 0):
                    seen[k] = v
                    sem = dsems[k[1]] if k[0] == "d" else esems[k[1]]
                    handle.wait_ge(sem, v)
            ins = o.fn(handle)
            if o.dsem is not None:
                ins.then_inc(dsems[o.dsem], 16)
            elif o.signal:
                ins.then_inc(esems[eng], 1)


def _slab_list():
    sl = []
    for i in (0,):
        pass
    def ffn(i):
        for j2 in range(NFC // 2):
            sl.append((f"f{i}_in{j2}", 4096))
        for d in range(NKC):
            sl.append((f"f{i}_out{d}", NFC * 128))
    ffn(0)
    sl.append(("m_qk", 4096))
    sl.append(("m_tok", 4096))
    sl.append(("m_pool", 2048))
    sl.append(("m_poolw", 256))
    sl.append(("m_ua", 4096))
    sl.append(("m_g", 2048))
    for d in range(NKC):
        sl.append((f"m_gate{d}", 4096))
        if d % 4 == 0:
            sl.append((f"m_br{d // 4}", 4096))
    sl.append(("m_out0", 4096))
    sl.append(("m_out1", 4096))
    ffn(1)
    return sl


SLABS = _slab_list()
SLAB_OFF = {}
_o = 0
for _n, _sz in SLABS:
    SLAB_OFF[_n] = (_o, _sz)
    _o += _sz
LAYER_W = _o


def _kc(w, cols):
    k = w.shape[0] // 128
    nch = len(cols) // 128
    a = w[:, cols].reshape(k, 128, nch, 128)
    return a.transpose(1, 2, 0, 3)


def pack_weights(inp):
    W = np.empty((128, DEPTH * LAYER_W), np.float32)
    for l in range(DEPTH):
        def put(name, arr):
            off, sz = SLAB_OFF[name]
            a = np.ascontiguousarray(arr).reshape(128, -1)
            assert a.shape[1] == sz, (name, a.shape, sz)
            W[:, l * LAYER_W + off: l * LAYER_W + off + sz] = a
        for i in range(2):
            w_in = inp["ffn_w_in"][l, i]
            w4 = w_in.reshape(NKC, 128, 2, NFC, 128)
            for j2 in range(NFC // 2):
                a = w4[:, :, :, 2 * j2:2 * j2 + 2, :]
                put(f"f{i}_in{j2}", a.transpose(1, 3, 2, 0, 4))
            w_out = inp["ffn_w_out"][l, i].reshape(NFC, 128, NKC, 128)
            for d in range(NKC):
                put(f"f{i}_out{d}", w_out[:, :, d, :].transpose(1, 0, 2))
        mw = inp["mix_w_in"][l]
        put("m_qk", _kc(mw, np.arange(0, 512)))
        tokc = np.concatenate([np.arange(512, 768), np.arange(1024, 1280)])
        put("m_tok", mw[:, tokc].reshape(NKC, 128, 512).transpose(1, 0, 2))
        put("m_pool", mw[:, 1280:1536].reshape(NKC, 128, 256).transpose(1, 0, 2))
        pw = np.zeros((128, 2, 128), np.float32)
        for g in range(4):
            r = (g % 2) * 64
            pw[r:r + 64, g // 2, r:r + 64] = inp["pool_w"][l, g]
        put("m_poolw", pw)
        put("m_ua", _kc(mw, np.concatenate([np.arange(768, 1024), np.arange(1536, 1792)])))
        put("m_g", _kc(mw, np.arange(1792, 2048)))
        gw = inp["gate_w"][l].reshape(4, NKC, 128, NKC, 128)
        for d in range(NKC):
            put(f"m_gate{d}", gw[:, :, :, d, :].transpose(2, 0, 1, 3))
        bw = inp["branch_w"][l].reshape(4, 2, 128, NKC, 128)
        for h in range(2):
            put(f"m_br{h}", bw[:, :, :, 4 * h:4 * h + 4, :].transpose(2, 3, 0, 1, 4))
        ow = inp["out_w"][l].reshape(NKC, 128, NKC, 128)
        for h in range(2):
            put(f"m_out{h}", ow[:, :, 4 * h:4 * h + 4, :].transpose(1, 2, 0, 3))
    return W


VEC_COLS = {}
_c = 0
def _vc(name, n):
    global _c
    VEC_COLS[name] = _c
    _c += n
for _i in range(3):
    _vc(f"ln_g{_i}", 8); _vc(f"ln_b{_i}", 8)
for _n in range(4):
    _vc(f"gate_b{_n}", 8)
_vc("pool_scale", 2); _vc("conv_b", 2); _vc("conv_ln_g", 2); _vc("conv_ln_b", 2)
_vc("conv_w", 62)
NVEC = _c


def pack_vecs(inp):
    V = np.zeros((128, DEPTH * NVEC), np.float32)
    for l in range(DEPTH):
        def put(name, v):
            c0 = l * NVEC + VEC_COLS[name]
            a = np.asarray(v).reshape(-1, 128).T
            V[:, c0:c0 + a.shape[1]] = a
        for i in range(3):
            put(f"ln_g{i}", inp["ln_g"][l, i]); put(f"ln_b{i}", inp["ln_b"][l, i])
        for n in range(4):
            put(f"gate_b{n}", inp["gate_b"][l, n])
        put("pool_scale", inp["pool_scale"][l]); put("conv_b", inp["conv_b"][l])
        put("conv_ln_g", inp["conv_ln_g"][l]); put("conv_ln_b", inp["conv_ln_b"][l])
        cw = inp["conv_w"][l]
        c0 = l * NVEC + VEC_COLS["conv_w"]
        for c in range(2):
            V[:, c0 + c * 31: c0 + (c + 1) * 31] = cw[:, c * 128:(c + 1) * 128].T
    return V


CB_NTRI, CB_NUSTR, CB_TRILE = 0, 128, 256
CB_BAND, CB_BANDP, CB_BAND0 = 384, 896, 1408
CB_MASK = 1920
NCB = CB_MASK + 2048
CF_SGW, CF_SGB, CF_SGG, CF_SGBE = 0, 512, 1536, 1792
NCF = 2048


def pack_consts(inp=None):
    c = {}
    c["ident"] = np.eye(128, dtype=np.float32)
    j = np.arange(128)[:, None]
    t = np.arange(128)[None, :]
    cb = np.zeros((128, NCB), np.float32)
    cb[:, CB_NTRI:CB_NTRI + 128] = -1.0 * (j >= t)
    cb[:, CB_NUSTR:CB_NUSTR + 128] = -1.0 * (j < t)
    cb[:, CB_TRILE:CB_TRILE + 128] = (j <= t)
    for g, win in enumerate(POOL_WINDOWS):
        band = ((j <= t) & (j > t - win)).astype(np.float32) / win - (j == t)
        cnt = np.minimum(t + 1, win).astype(np.float32)
        band0 = ((j <= t) & (j > t - win)).astype(np.float32) / cnt - (j == t)
        bandp = ((j + 0 - 128 > t - win)).astype(np.float32) / win
        cb[:, CB_BAND + g * 128:CB_BAND + (g + 1) * 128] = band
        cb[:, CB_BANDP + g * 128:CB_BANDP + (g + 1) * 128] = bandp
        cb[:, CB_BAND0 + g * 128:CB_BAND0 + (g + 1) * 128] = band0
    col = np.arange(512)[None, :]
    for d in range(4):
        cb[:, CB_MASK + d * 512:CB_MASK + (d + 1) * 512] = (col > 128 * d + j)
    c["cb"] = cb
    if inp is not None:
        cf = np.zeros((128, DEPTH * NCF), np.float32)
        for l in range(DEPTH):
            o = l * NCF
            sgw = inp["sg_w"][l]
            cf[:, o + CF_SGW:o + CF_SGW + 512] = sgw.transpose(2, 0, 1).reshape(128, 512)
            sgb = inp["sg_b"][l]
            rep = np.empty((128, 2, 4, 128), np.float32)
            for cc in range(2):
                rep[0:64, cc] = sgb[2 * cc][None, None, :]
                rep[64:128, cc] = sgb[2 * cc + 1][None, None, :]
            cf[:, o + CF_SGB:o + CF_SGB + 1024] = rep.reshape(128, 1024)
            cf[:, o + CF_SGG:o + CF_SGG + 256] = inp["sg_ln_g"][l][None, :]
            cf[:, o + CF_SGBE:o + CF_SGBE + 256] = inp["sg_ln_b"][l][None, :]
        c["cf"] = cf
    return c


class Cfg:
    def __init__(self, seq=SEQ, depth=DEPTH, phases=("f0", "mix", "f1"), dbg=None):
        self.seq = seq
        self.depth = depth
        self.phases = phases
        self.nblk = seq // TB
        self.dbg = dbg


def build_program(cfg):
    nc = bass.Bass("TRN2", target_bir_lowering=False)
    S = cfg.seq
    NBLK = cfg.nblk
    x_in = nc.dram_tensor("x", [S, D_MODEL], F32, kind="ExternalInput").ap()
    w_in = nc.dram_tensor("w", [128, DEPTH * LAYER_W], F32, kind="ExternalInput").ap()
    vec_in = nc.dram_tensor("vecs", [128, DEPTH * NVEC], F32, kind="ExternalInput").ap()
    ident_in = nc.dram_tensor("ident", [128, 128], F32, kind="ExternalInput").ap()
    cb_in = nc.dram_tensor("cb", [128, NCB], F32, kind="ExternalInput").ap()
    cf_in = nc.dram_tensor("cf", [128, DEPTH * NCF], F32, kind="ExternalInput").ap()
    out = nc.dram_tensor("out", [S, D_MODEL], F32, kind="ExternalOutput").ap()
    wb = nc.dram_tensor("wb", [128, DEPTH * LAYER_W], BF16, kind="Internal").ap()
    xs = nc.dram_tensor("xs", [NBLK, 128, NKC, TB], F32, kind="Internal").ap()

    P = Prog()
    es = ExitStack()

    def sb(name, shape, dt):
        return es.enter_context(nc.sbuf_tensor("s_" + name, shape, dt))

    def bufs(name, n):
        return [Buf(f"{name}{i}") for i in range(n)]

    with es:
        ident = sb("ident", [128, 128], F32); b_ident = Buf("ident")
        vecs = sb("vecs", [128, DEPTH * NVEC], F32); b_vecs = Buf("vecs")
        ring = sb("ring", [128, NSLOT, SLOT], BF16); b_ring = bufs("ring", NSLOT)
        xT = sb("xT", [128, NKC, TB], F32); b_xT = bufs("xT", NKC)
        xTb = sb("xTb", [128, NKC, TB], BF16); b_xTb = bufs("xTb", NKC)
        xn = sb("xn", [128, NSUB, D_MODEL], F32); b_xn = bufs("xn", NSUB)
        gT = sb("gT", [128, NFC, TB], BF16); b_gT = bufs("gT", NFC)
        S32 = sb("S32", [128, 8, TB], F32); b_S32 = bufs("S32", 8)
        stt = sb("stt", [128, 4, 12], F32); b_stt = bufs("stt", 4)
        mv = sb("mv", [128, 4, 2], F32); b_mv = bufs("mv", 4)
        rs = sb("rs", [128, 4, 4], F32); b_rs = bufs("rs", 4); b_rs0 = bufs("rs0_", 4); b_rs1 = bufs("rs1_", 4)
        psum = [es.enter_context(nc.psum_tensor(f"ps{i}", [128, 512], F32)) for i in range(8)]
        b_ps = [Buf(f"ps{i}", excl=True) for i in range(8)]
        ps_rr = [0]

        def next_ps(subset=range(8)):
            subset = list(subset)
            i = subset[ps_rr[0] % len(subset)]
            ps_rr[0] += 1
            return psum[i], b_ps[i]

        DS_CONST = "const"
        DS_RING = [f"ring{i}" for i in range(NSLOT)]
        DS_PRE = [f"pre{i}" for i in range(8)]
        DS_XIO = ["xio0", "xio1", "xio2", "xio3"]
        DS_XS = "xs"

        P.op("pool", lambda e: e.dma_start(out=ident[:], in_=ident_in[:]), writes=[b_ident], dsem=DS_CONST)
        P.op("pool", lambda e: e.dma_start(out=vecs[:], in_=vec_in[:]), writes=[b_vecs], dsem=DS_CONST)

        b_wb = {}
        npre = [0]
        pre_list = [(l, name, sz) for l in range(cfg.depth) for (name, sz) in SLABS]

        def emit_prepass(n):
            if getattr(cfg, "no_prepass", False):
                return
            for _ in range(n):
                if not pre_list:
                    return
                l, name, sz = pre_list.pop(0)
                off = l * LAYER_W + SLAB_OFF[name][0]
                b = Buf(f"wb{l}{name}")
                b_wb[(l, name)] = b
                P.op("pool",
                     lambda e, off=off, sz=sz: e.dma_start(out=wb[:, off:off + sz], in_=w_in[:, off:off + sz],
                                                            max_dma_last_dim=8192),
                     writes=[b], dsem=DS_PRE[npre[0] % 8])
                npre[0] += 1


        slab_ctr = [0]

        def load_slab(l, name):
            off, sz = SLAB_OFF[name]
            off += l * LAYER_W
            k = slab_ctr[0] % NSLOT
            slab_ctr[0] += 1
            P.op("sp", lambda e, k=k, off=off, sz=sz: e.dma_start(out=ring[:, k, 0:sz], in_=wb[:, off:off + sz]),
                 reads=[b_wb[(l, name)]], writes=[b_ring[k]], dsem=DS_RING[k])
            return ring[:, k, :], b_ring[k]

        def vcol(l, name, j):
            c = l * NVEC + VEC_COLS[name] + j
            return vecs[:, c:c + 1]

        def mm(ps, bps, lhsT, rhs, rd, start, stop, skip=False):
            P.op("pe", lambda e: e.matmul(ps, lhsT=lhsT, rhs=rhs, start=start, stop=stop, skip_group_check=skip),
                 reads=rd, writes=[bps])

        def tr(ps, bps, in_, rd):
            P.op("pe", lambda e: e.transpose(ps, in_, ident[:]), reads=rd + [b_ident], writes=[bps])

        ln_ctr = [0]

        def layer_norm_T(l, gname, bname):
            P.tag = "ln"
            pzs = {}
            for s in range(NSUB + 1):
                if s < NSUB:
                    q = s % 2
                    pz = [next_ps(), next_ps()]
                    pzs[s] = pz
                    for d in range(NKC):
                        ps, bps = pz[d // 4]
                        tr(ps[:, (d % 4) * 128:(d % 4 + 1) * 128], bps, xT[:, d, s * 128:(s + 1) * 128], [b_xT[d]])
                    for h in range(2):
                        ps, bps = pz[h]
                        P.op("dve", lambda e, ps=ps, q=q, h=h: e.bn_stats(out=stt[:, q, h * 6:(h + 1) * 6], in_=ps[:]),
                             reads=[bps], writes=[b_stt[q]])
                    P.op("dve", lambda e, q=q: e.bn_aggr(out=mv[:, q, :], in_=stt[:, q, :]),
                         reads=[b_stt[q]], writes=[b_mv[q]])
                    P.op("act", lambda e, q=q: e.activation(out=rs[:, q, 0:1], in_=mv[:, q, 1:2], func=AF.Sqrt,
                                                            bias=eps_t[:, 0:1], scale=1.0),
                         reads=[b_mv[q], b_eps], writes=[b_rs0[q]])
                    P.op("dve", lambda e, q=q: e.reciprocal(out=rs[:, q, 1:2], in_=rs[:, q, 0:1]),
                         reads=[b_rs0[q]], writes=[b_rs1[q]])
                if s >= 1:
                    sp_ = s - 1
                    q = sp_ % 2
                    P.op("dve", lambda e, q=q: e.scalar_tensor_tensor(out=rs[:, q, 2:3], in0=mv[:, q, 0:1], scalar=-1.0,
                                                                       in1=rs[:, q, 1:2], op0=ALU.mult, op1=ALU.mult),
                         reads=[b_rs1[q], b_mv[q]], writes=[b_rs[q]])
                    for h in range(2):
                        ps, bps = pzs[sp_][h]
                        P.op("act", lambda e, ps=ps, q=q, h=h, sp_=sp_: e.activation(
                            out=xn[:, sp_, h * 512:(h + 1) * 512], in_=ps[:], func=AF.Identity,
                            bias=rs[:, q, 2:3], scale=rs[:, q, 1:2]),
                            reads=[bps, b_rs[q], b_rs1[q]], writes=[b_xn[sp_]])
            for d in range(NKC):
                ps, bps = next_ps()
                for s in range(NSUB):
                    tr(ps[:, s * 128:(s + 1) * 128], bps, xn[:, s, d * 128:(d + 1) * 128], [b_xn[s]])
                g = vcol(l, gname, d)
                b = vcol(l, bname, d)
                if d % 2 == 0:
                    P.op("act", lambda e, ps=ps, d=d, g=g, b=b: e.activation(out=xT[:, d, :], in_=ps[:], func=AF.Identity,
                                                                             bias=b, scale=g),
                         reads=[bps, b_vecs], writes=[b_xT[d]])
                    P.op("dve", lambda e, d=d: e.tensor_copy(out=xTb[:, d, :], in_=xT[:, d, :]),
                         reads=[b_xT[d]], writes=[b_xTb[d]])
                else:
                    P.op("dve", lambda e, ps=ps, d=d, g=g, b=b: e.tensor_scalar(out=xT[:, d, :], in0=ps[:], scalar1=g,
                                                                               scalar2=b, op0=ALU.mult, op1=ALU.add),
                         reads=[bps, b_vecs], writes=[b_xT[d]])
                    P.op("act", lambda e, d=d: e.activation(out=xTb[:, d, :], in_=xT[:, d, :], func=AF.Identity),
                         reads=[b_xT[d]], writes=[b_xTb[d]])

        def ffn_phase(l, i, ln_idx):
            c_res = 0.5 / DN_ALPHA
            P.tag = "ffn_in"
            b_sil = [Buf("silh0"), Buf("silh1")]
            P.op("dve", lambda e: e.memset(xn[:, 0, 0:2], 0.0), writes=[b_xn[0]] + b_sil)
            for j2 in range(NFC // 2):
                slab, bsl = load_slab(l, f"f{i}_in{j2}")
                for jj in range(2):
                    j = 2 * j2 + jj
                    pa = next_ps()
                    pu = next_ps()
                    for t, (ps, bps) in enumerate((pa, pu)):
                        for k in range(NKC):
                            o = ((jj * 2 + t) * NKC + k) * 128
                            mm(ps[:], bps, slab[:, o:o + 128], xTb[:, k, :], [bsl, b_xTb[k]], k == 0, k == NKC - 1)
                    q = j % 2
                    P.op("act", lambda e, ps=pa[0], q=q: e.activation(out=xn[:, 0, q * 512:(q + 1) * 512], in_=ps[:], func=AF.Silu),
                         reads=[pa[1]], writes=[b_sil[q]])
                    P.op("dve", lambda e, ps=pu[0], q=q, j=j: e.tensor_tensor(out=gT[:, j, :], in0=ps[:],
                                                                             in1=xn[:, 0, q * 512:(q + 1) * 512], op=ALU.mult),
                         reads=[pu[1], b_sil[q]], writes=[b_gT[j]])
            P.op("dve", lambda e: e.memset(xn[:, 0, 0:2], 0.0), writes=[b_xn[0]] + b_sil)
            P.tag = "ffn_out"
            for d in range(NKC):
                slab, bsl = load_slab(l, f"f{i}_out{d}")
                ps, bps = next_ps()
                for j in range(NFC):
                    mm(ps[:], bps, slab[:, j * 128:(j + 1) * 128], gT[:, j, :], [bsl, b_gT[j]], j == 0, j == NFC - 1)
                P.op("dve", lambda e, ps=ps, d=d: e.scalar_tensor_tensor(out=xT[:, d, :], in0=ps[:], scalar=c_res,
                                                                         in1=xT[:, d, :], op0=ALU.mult, op1=ALU.add),
                     reads=[bps, b_xT[d]], writes=[b_xT[d]])
            layer_norm_T(l, f"ln_g{ln_idx}", f"ln_b{ln_idx}")

        eps_t = sb("eps_t", [128, 1], F32); b_eps = Buf("eps")
        P.op("dve", lambda e: e.memset(eps_t[:], LN_EPS / (DN_ALPHA * DN_ALPHA)), writes=[b_eps])
        eps1 = sb("eps1", [128, 1], F32)
        one1 = sb("one1", [128, 1], F32)
        mhalf = sb("mhalf", [128, 1], F32)
        P.op("dve", lambda e: e.memset(mhalf[:], -0.5), writes=[b_eps])
        P.op("dve", lambda e: e.memset(eps1[:], LN_EPS), writes=[b_eps])
        P.op("dve", lambda e: e.memset(one1[:], 1.0), writes=[b_eps])

        DS_PF = ["pf0", "pf1", "pf2", "pf3"]

        def prefetch_x(blk):
            for s_ in range(NSUB):
                r0 = blk * TB + s_ * 128
                P.op("pool", lambda e, s_=s_, r0=r0: e.dma_start(
                    out=S32[:, 2 * s_:2 * s_ + 2, :], in_=x_in[r0:r0 + 128, :].rearrange("p (a b) -> p a b", a=2)),
                    writes=[b_S32[2 * s_], b_S32[2 * s_ + 1]], dsem=DS_PF[s_])

        def consume_x(blk):
            P.tag = "io"
            for d in range(NKC):
                ps, bps = next_ps()
                for s_ in range(NSUB):
                    sl = 2 * s_ + d // 4
                    tr(ps[:, s_ * 128:(s_ + 1) * 128], bps, S32[:, sl, (d % 4) * 128:(d % 4 + 1) * 128], [b_S32[sl]])
                if d % 2 == 0:
                    P.op("act", lambda e, ps=ps, d=d: e.activation(out=xT[:, d, :], in_=ps[:], func=AF.Identity),
                         reads=[bps], writes=[b_xT[d]])
                    P.op("dve", lambda e, d=d: e.tensor_copy(out=xTb[:, d, :], in_=xT[:, d, :]),
                         reads=[b_xT[d]], writes=[b_xTb[d]])
                else:
                    P.op("dve", lambda e, ps=ps, d=d: e.tensor_copy(out=xT[:, d, :], in_=ps[:]),
                         reads=[bps], writes=[b_xT[d]])
                    P.op("act", lambda e, d=d: e.activation(out=xTb[:, d, :], in_=xT[:, d, :], func=AF.Identity),
                         reads=[b_xT[d]], writes=[b_xTb[d]])

        def prefetch_xs(blk):
            P.op("pool", lambda e, blk=blk: e.dma_start(out=S32[:], in_=xs[blk]),
                 reads=[b_xs[blk]], writes=b_S32, dsem=DS_PF[0])

        def consume_xs(blk):
            P.tag = "io"
            for d in range(NKC):
                if d % 2 == 0:
                    P.op("act", lambda e, d=d: e.activation(out=xT[:, d, :], in_=S32[:, d, :], func=AF.Identity),
                         reads=[b_S32[d]], writes=[b_xT[d]])
                    P.op("dve", lambda e, d=d: e.tensor_copy(out=xTb[:, d, :], in_=S32[:, d, :]),
                         reads=[b_S32[d]], writes=[b_xTb[d]])
                else:
                    P.op("dve", lambda e, d=d: e.tensor_copy(out=xT[:, d, :], in_=S32[:, d, :]),
                         reads=[b_S32[d]], writes=[b_xT[d]])
                    P.op("act", lambda e, d=d: e.activation(out=xTb[:, d, :], in_=S32[:, d, :], func=AF.Identity),
                         reads=[b_S32[d]], writes=[b_xTb[d]])

        def store_block_to_out(blk):
            P.tag = "io"
            for s in range(NSUB):
                q = s
                for h in range(2):
                    ps, bps = next_ps()
                    for dd in range(4):
                        d = h * 4 + dd
                        tr(ps[:, dd * 128:(dd + 1) * 128], bps, xT[:, d, s * 128:(s + 1) * 128], [b_xT[d]])
                    eng = "act" if h == 0 else "dve"
                    if eng == "act":
                        P.op("act", lambda e, ps=ps, q=q, h=h: e.activation(out=xn[:, q, h * 512:(h + 1) * 512],
                                                                            in_=ps[:], func=AF.Identity),
                             reads=[bps], writes=[b_xn[q]])
                    else:
                        P.op("dve", lambda e, ps=ps, q=q, h=h: e.tensor_copy(out=xn[:, q, h * 512:(h + 1) * 512],
                                                                             in_=ps[:]),
                             reads=[bps], writes=[b_xn[q]])
                r0 = blk * TB + s * 128
                P.op("pool", lambda e, q=q, r0=r0: e.dma_start(out=out[r0:r0 + 128, :], in_=xn[:, q, :]),
                     reads=[b_xn[q]], dsem=DS_XIO[q])

        def store_block_xs(blk):
            P.op("pool", lambda e, blk=blk: e.dma_start(out=xs[blk], in_=xT[:]),
                 reads=b_xT, writes=[b_xs[blk]], dsem=DS_XS)

        cbt = sb("cbt", [128, NCB], BF16); b_cb = Buf("cb")
        cft = sb("cft", [128, NCF], F32); b_cf = Buf("cf")
        wsm = sb("wsm", [128, 4, 128], BF16); b_wsm = Buf("wsm")
        kTc = sb("kTc", [128, 2, S], BF16); b_kTc = Buf("kTc")
        Vc = sb("Vc", [128, S // 128, BW], BF16); b_Vc = Buf("Vc")
        B16 = sb("B16", [128, 8, TB], BF16); b_B16 = bufs("B16", 8)
        yT4 = sb("yT4", [128, 8, TB], BF16); b_yT4 = bufs("yT4", 8)
        qT = sb("qT", [128, 2, TB], BF16); b_qT = bufs("qT", 2)
        vln = sb("vln", [128, NSUB, BW], BF16); b_vln = bufs("vln", NSUB)
        ptok = sb("ptok", [128, NSUB + 1, BW], BF16); b_ptok = bufs("ptok", NSUB + 1)
        ybuf = sb("ybuf", [128, 2, 30 + TB], BF16); b_ybuf = bufs("ybuf", 2)
        dg = sb("dg", [128, 4, 128], BF16); b_dg = bufs("dg", 4)
        P.op("pool", lambda e: e.dma_start(out=cbt[:], in_=cb_in[:], max_dma_last_dim=8192), writes=[b_cb], dsem="cb")
        ntri = cbt[:, CB_NTRI:CB_NTRI + 128]
        nustr = cbt[:, CB_NUSTR:CB_NUSTR + 128]
        mergedT = gT
        b_merged = b_gT
        dg_ctr = [0]
        st_ctr = [0]

        def tok_stats_a(src_ap, src_buf, eps_ap):
            q = st_ctr[0] % 4
            st_ctr[0] += 1
            P.op("dve", lambda e: e.bn_stats(out=stt[:, q, 0:6], in_=src_ap), reads=[src_buf], writes=[b_stt[q]])
            P.op("dve", lambda e: e.bn_aggr(out=mv[:, q, :], in_=stt[:, q, 0:6]), reads=[b_stt[q]], writes=[b_mv[q]])
            P.op("act", lambda e: e.activation(out=rs[:, q, 0:1], in_=mv[:, q, 1:2], func=AF.Sqrt, bias=eps_ap, scale=1.0),
                 reads=[b_mv[q], b_eps], writes=[b_rs0[q]])
            P.op("dve", lambda e: e.reciprocal(out=rs[:, q, 1:2], in_=rs[:, q, 0:1]), reads=[b_rs0[q]], writes=[b_rs1[q]])
            return q

        def tok_stats_b(q):
            P.op("dve", lambda e: e.scalar_tensor_tensor(out=rs[:, q, 2:3], in0=mv[:, q, 0:1], scalar=-1.0,
                                                          in1=rs[:, q, 1:2], op0=ALU.mult, op1=ALU.mult),
                 reads=[b_rs1[q], b_mv[q]], writes=[b_rs[q]])
            return rs[:, q, 1:2], rs[:, q, 2:3], [b_rs[q], b_rs1[q]]

        def layer_setup(l):
            o = l * NCF
            P.op("pool", lambda e: e.dma_start(out=cft[:], in_=cf_in[:, o:o + NCF]), writes=[b_cf], dsem="cf")
            for g in range(4):
                P.op("dve", lambda e, g=g: e.tensor_tensor(out=wsm[:, g, :], in0=cft[:, CF_SGW + g * 128:CF_SGW + (g + 1) * 128],
                                                          in1=cbt[:, CB_TRILE:CB_TRILE + 128], op=ALU.mult),
                     reads=[b_cf, b_cb], writes=[b_wsm])
            for c in range(2):
                P.op("dve", lambda e, c=c: e.memset(ybuf[:, c, 0:30], 0.0), writes=[b_ybuf[c]])

        def mixer_phase(l, blk):
            t0 = blk * TB
            nt0 = blk * NSUB
            if blk == 0 and l > 0:
                layer_setup(l)
            P.tag = "mix_qk"
            slab, bsl = load_slab(l, "m_qk")
            for cc in range(4):
                ps, bps = next_ps()
                for k in range(NKC):
                    o = (cc * NKC + k) * 128
                    mm(ps[:], bps, slab[:, o:o + 128], xTb[:, k, :], [bsl, b_xTb[k]], k == 0, k == NKC - 1)
                if cc < 2:
                    P.op("act", lambda e, ps=ps, cc=cc: e.activation(out=qT[:, cc, :], in_=ps[:], func=AF.Identity),
                         reads=[bps], writes=[b_qT[cc]])
                else:
                    P.op("dve", lambda e, ps=ps, cc=cc: e.tensor_copy(out=kTc[:, cc - 2, t0:t0 + TB], in_=ps[:]),
                         reads=[bps], writes=[b_kTc])
            P.tag = "mix_tok"
            slab, bsl = load_slab(l, "m_tok")
            slabp, bslp = load_slab(l, "m_pool")
            slabw, bslw = load_slab(l, "m_poolw")
            for s in range(NSUB):
                ps, bps = next_ps()
                for k in range(NKC):
                    mm(ps[:], bps, xTb[:, k, s * 128:(s + 1) * 128], slab[:, k * 512:(k + 1) * 512],
                       [bsl, b_xTb[k]], k == 0, k == NKC - 1)
                P.op("dve", lambda e, ps=ps, s=s: e.tensor_copy(out=Vc[:, nt0 + s, :], in_=ps[:, 0:256]),
                     reads=[bps], writes=[b_Vc])
                P.op("act", lambda e, ps=ps, s=s: e.activation(out=xn[:, s, 256:512], in_=ps[:, 256:512],
                                                               func=AF.Gelu_apprx_tanh),
                     reads=[bps], writes=[b_xn[s]])
                ps2, bps2 = next_ps()
                for k in range(NKC):
                    mm(ps2[:, 0:256], bps2, xTb[:, k, s * 128:(s + 1) * 128], slabp[:, k * 256:(k + 1) * 256],
                       [bslp, b_xTb[k]], k == 0, k == NKC - 1)
                P.op("act", lambda e, ps2=ps2, s=s: e.activation(out=ptok[:, s + 1, :], in_=ps2[:, 0:256], func=AF.Identity),
                     reads=[bps2], writes=[b_ptok[s + 1]])
            P.tag = "mix_sgln"
            qs = [tok_stats_a(xn[:, s, 256:512], b_xn[s], eps1[:, 0:1]) for s in range(NSUB)]
            for s in range(NSUB):
                rstd, nmr, brs = tok_stats_b(qs[s])
                P.op("act", lambda e, s=s, rstd=rstd, nmr=nmr: e.activation(out=xn[:, s, 512:768], in_=xn[:, s, 256:512],
                                                                            func=AF.Identity, bias=nmr, scale=rstd),
                     reads=[b_xn[s]] + brs, writes=[b_xn[s]])
                P.op("dve", lambda e, s=s: e.tensor_tensor(out=xn[:, s, 512:768], in0=xn[:, s, 512:768],
                                                           in1=cft[:, CF_SGG:CF_SGG + 256], op=ALU.mult),
                     reads=[b_xn[s], b_cf], writes=[b_xn[s]])
                P.op("dve", lambda e, s=s: e.tensor_tensor(out=vln[:, s, :], in0=xn[:, s, 512:768],
                                                           in1=cft[:, CF_SGBE:CF_SGBE + 256], op=ALU.add),
                     reads=[b_xn[s], b_cf], writes=[b_vln[s]])
            P.tag = "mix_u"
            slab, bsl = load_slab(l, "m_ua")
            slabg, bslg = load_slab(l, "m_g")
            for c in range(2):
                ps, bps = next_ps()
                for k in range(NKC):
                    o = (c * NKC + k) * 128
                    mm(ps[:], bps, slab[:, o:o + 128], xTb[:, k, :], [bsl, b_xTb[k]], k == 0, k == NKC - 1)
                P.op("act", lambda e, ps=ps, c=c: e.activation(out=S32[:, 2 + c, :], in_=ps[:], func=AF.Gelu_apprx_tanh),
                     reads=[bps], writes=[b_S32[2 + c]])
            P.tag = "mix_conv"
            for c in range(2):
                psa, bpsa = next_ps()
                for k in range(NKC):
                    o = ((2 + c) * NKC + k) * 128
                    mm(psa[:], bpsa, slab[:, o:o + 128], xTb[:, k, :], [bsl, b_xTb[k]], k == 0, k == NKC - 1)
                psg, bpsg = next_ps()
                for k in range(NKC):
                    o = (c * NKC + k) * 128
                    mm(psg[:], bpsg, slabg[:, o:o + 128], xTb[:, k, :], [bslg, b_xTb[k]], k == 0, k == NKC - 1)
                P.op("act", lambda e, psg=psg: e.activation(out=S32[:, 0, :], in_=psg[:], func=AF.Sigmoid),
                     reads=[bpsg], writes=[b_S32[0]])
                P.op("dve", lambda e, psa=psa, c=c: e.tensor_tensor(out=ybuf[:, c, 30:30 + TB], in0=psa[:], in1=S32[:, 0, :],
                                                                    op=ALU.mult),
                     reads=[bpsa, b_S32[0]], writes=[b_ybuf[c]])
            for c in range(2):
                ps, bps = next_ps()
                for k in range(CONV_W):
                    q = dg_ctr[0] % 4
                    dg_ctr[0] += 1
                    wc = vcol(l, "conv_w", c * 31 + k)
                    P.op("dve", lambda e, q=q, wc=wc: e.tensor_scalar(out=dg[:, q, :], in0=ident[:], scalar1=wc, scalar2=None,
                                                                      op0=ALU.mult),
                         reads=[b_ident, b_vecs], writes=[b_dg[q]])
                    mm(ps[:], bps, dg[:, q, :], ybuf[:, c, k:k + TB], [b_dg[q], b_ybuf[c]], k == 0, k == CONV_W - 1)
                cb_ = vcol(l, "conv_b", c)
                P.op("act", lambda e, ps=ps, c=c, cb_=cb_: e.activation(out=S32[:, 6 + c, :], in_=ps[:], func=AF.Identity,
                                                                         bias=cb_, scale=1.0),
                     reads=[bps, b_vecs], writes=[b_S32[6 + c]])
                P.op("dve", lambda e, c=c: e.tensor_copy(out=ybuf[:, c, 0:30], in_=ybuf[:, c, TB:TB + 30]),
                     reads=[b_ybuf[c]], writes=[b_ybuf[c]])
            cps = []
            for s in range(NSUB):
                ps, bps = next_ps(range(4, 8))
                for c in range(2):
                    tr(ps[:, c * 128:(c + 1) * 128], bps, S32[:, 6 + c, s * 128:(s + 1) * 128], [b_S32[6 + c]])
                cps.append((ps, bps, tok_stats_a(ps[:, 0:256], bps, eps1[:, 0:1])))
            for s in range(NSUB):
                ps, bps, q = cps[s]
                rstd, nmr, brs = tok_stats_b(q)
                P.op("act", lambda e, ps=ps, s=s, rstd=rstd, nmr=nmr: e.activation(out=xn[:, s, 0:256], in_=ps[:, 0:256],
                                                                                   func=AF.Identity, bias=nmr, scale=rstd),
                     reads=[bps] + brs, writes=[b_xn[s]])
            P.tag = "mix_pool"
            for c in range(2):
                for par in range(2):
                    g = 2 * c + par
                    r0 = par * 64
                    ps, bps = next_ps(range(4))
                    for s in range(NSUB):
                        osl = ps[:, s * 128:(s + 1) * 128]
                        cur = ptok[:, s + 1, c * 128:(c + 1) * 128]
                        if blk == 0 and s == 0:
                            mm(osl, bps, cur, cbt[:, CB_BAND0 + g * 128:CB_BAND0 + (g + 1) * 128],
                               [b_ptok[s + 1], b_cb], True, True)
                        else:
                            mm(osl, bps, cur, cbt[:, CB_BAND + g * 128:CB_BAND + (g + 1) * 128],
                               [b_ptok[s + 1], b_cb], True, False)
                            mm(osl, bps, ptok[:, s, c * 128:(c + 1) * 128],
                               cbt[:, CB_BANDP + g * 128:CB_BANDP + (g + 1) * 128], [b_ptok[s], b_cb], False, True)
                    P.op("act", lambda e, ps=ps, c=c, r0=r0: e.activation(out=B16[r0:r0 + 64, 4 + c, :], in_=ps[r0:r0 + 64, :],
                                                                          func=AF.Identity),
                         reads=[bps], writes=[b_B16[4 + c]])
            P.op("dve", lambda e: e.tensor_copy(out=ptok[:, 0, :], in_=ptok[:, NSUB, :]), reads=[b_ptok[NSUB]],
                 writes=[b_ptok[0]])
            for c in range(2):
                ps, bps = next_ps(range(4))
                mm(ps[:], bps, slabw[:, c * 128:(c + 1) * 128], B16[:, 4 + c, :], [bslw, b_B16[4 + c]], True, True)
                sc_ = vcol(l, "pool_scale", c)
                P.op("act", lambda e, ps=ps, c=c, sc_=sc_: e.activation(out=yT4[:, 4 + c, :], in_=ps[:], func=AF.Identity,
                                                                        scale=sc_),
                     reads=[bps, b_vecs], writes=[b_yT4[4 + c]])
            P.tag = "mix_sgmix"
            for c in range(2):
                for par in range(2):
                    g = 2 * c + par
                    r0 = par * 64
                    ps, bps = next_ps(range(4))
                    for s in range(NSUB):
                        mm(ps[:, s * 128:(s + 1) * 128], bps, vln[:, s, c * 128:(c + 1) * 128], wsm[:, g, :],
                           [b_vln[s], b_wsm], True, True)
                    P.op("dve", lambda e, ps=ps, c=c, r0=r0: e.tensor_tensor(
                        out=S32[r0:r0 + 64, 4, :], in0=ps[r0:r0 + 64, :],
                        in1=cft[r0:r0 + 64, CF_SGB + c * 512:CF_SGB + (c + 1) * 512], op=ALU.add),
                        reads=[bps, b_cf], writes=[b_S32[4]])
                    P.op("dve", lambda e, c=c, r0=r0: e.tensor_tensor(
                        out=yT4[r0:r0 + 64, 2 + c, :], in0=S32[r0:r0 + 64, 4, :], in1=S32[r0:r0 + 64, 2 + c, :], op=ALU.mult),
                        reads=[b_S32[4], b_S32[2 + c]], writes=[b_yT4[2 + c]])
            P.tag = "mix_conv"
            for c in range(2):
                ps, bps = next_ps(range(4))
                for s in range(NSUB):
                    tr(ps[:, s * 128:(s + 1) * 128], bps, xn[:, s, c * 128:(c + 1) * 128], [b_xn[s]])
                g_ = vcol(l, "conv_ln_g", c)
                b_ = vcol(l, "conv_ln_b", c)
                P.op("act", lambda e, ps=ps, c=c, g_=g_, b_=b_: e.activation(out=yT4[:, 6 + c, :], in_=ps[:], func=AF.Silu,
                                                                             bias=b_, scale=g_),
                     reads=[bps, b_vecs], writes=[b_yT4[6 + c]])
            P.tag = "mix_attn"
            npair = nt0 + NSUB
            a_hi = npair - 1
            b_xnh = [[Buf(f"xnh{s_}{hh}") for hh in range(2)] for s_ in range(NSUB)]
            for s_ in range(NSUB):
                P.op("dve", lambda e, s_=s_: e.memset(xn[:, s_, 0:2], 0.0), writes=[b_xn[s_]] + b_xnh[s_])
            ebuf = []
            for h in range(4):
                ebuf.append([(xn[:, h, 0:512], b_xnh[h][0]), (xn[:, h, 512:1024], b_xnh[h][1]), (S32[:, h, :], b_S32[h])])
            Ebuf = [(S32[:, 4 + h, :], b_S32[4 + h]) for h in range(4)]
            spbuf = [[(gT[:, 8 + h * 3 + i, :], b_gT[8 + h * 3 + i]) for i in range(3)] for h in range(4)]
            wbuf = [[(B16[:, h * 2 + i, :], b_B16[h * 2 + i]) for i in range(2)] for h in range(4)]
            actr = [0]

            def pinfo(p):
                a = a_hi - p
                dd = a - nt0
                c0 = 128 * max(dd, 0)
                return a, dd, c0

            for t in range(npair + 2):
                if t < npair:
                    a, dd, c0 = pinfo(t)
                    pend_ln = None
                    for h in range(4):
                        c = h // 2
                        r0 = (h % 2) * 64
                        psA, bA = psum[actr[0] % 2], b_ps[actr[0] % 2]
                        actr[0] += 1
                        mm(psA[:, c0:TB], bA, kTc[r0:r0 + 64, c, a * 128:(a + 1) * 128], qT[r0:r0 + 64, c, c0:TB],
                           [b_kTc, b_qT[c]], True, True)
                        eap, eb = ebuf[h][t % 3]
                        P.op("act", lambda e, psA=psA, eap=eap, c0=c0: e.activation(out=eap[:, c0:TB], in_=psA[:, c0:TB],
                                                                                  func=AF.Exp, scale=0.125),
                             reads=[bA], writes=[eb])
                        if dd >= 0:
                            P.op("dve", lambda e, eap=eap, dd=dd, c0=c0: e.tensor_tensor(
                                out=eap[:, c0:TB], in0=eap[:, c0:TB], in1=cbt[:, CB_MASK + dd * 512 + c0:CB_MASK + (dd + 1) * 512],
                                op=ALU.mult), reads=[eb, b_cb], writes=[eb])
                        if pend_ln is not None:
                            pend_ln()
                        sap, sbf = spbuf[h][t % 3]

                        def _ln(eap=eap, eb=eb, sap=sap, sbf=sbf, c0=c0):
                            P.op("act", lambda e: e.activation(out=sap[:, c0:TB], in_=eap[:, c0:TB], func=AF.Ln, bias=1.0, scale=1.0),
                                 reads=[eb], writes=[sbf])
                        pend_ln = _ln
                    pend_ln()
                if t >= 2:
                    p = t - 2
                    a, dd, c0 = pinfo(p)
                    for h in range(4):
                        c = h // 2
                        r0 = (h % 2) * 64
                        sap, sbf = spbuf[h][p % 3]
                        wap, wbf = wbuf[h][p % 2]
                        if p < npair - 1:
                            mm(psum[2 + h][:, c0:TB], b_ps[2 + h], nustr, sap[:, c0:TB], [b_cb, sbf], False, True, skip=True)
                        mm(psum[6 + c][r0:r0 + 64, c0:TB], b_ps[6 + c], Vc[:, a, h * 64:(h + 1) * 64], wap[:, c0:TB],
                           [b_Vc, wbf], p == 0, True, skip=(p > 0))
                if 1 <= t <= npair:
                    p = t - 1
                    a, dd, c0 = pinfo(p)
                    for h in range(4):
                        sap, sbf = spbuf[h][p % 3]
                        eap, eb = ebuf[h][p % 3]
                        Eap, Ebf = Ebuf[h]
                        wap, wbf = wbuf[h][p % 2]
                        psB, bB = psum[2 + h], b_ps[2 + h]
                        mm(psB[:, c0:TB], bB, ntri, sap[:, c0:TB], [b_cb, sbf], p == 0, True, skip=(p > 0))
                        P.op("act", lambda e, psB=psB, Eap=Eap, c0=c0: e.activation(out=Eap[:, c0:TB], in_=psB[:, c0:TB], func=AF.Exp),
                             reads=[bB], writes=[Ebf])
                        P.op("dve", lambda e, wap=wap, eap=eap, Eap=Eap, c0=c0: e.tensor_tensor(
                            out=wap[:, c0:TB], in0=eap[:, c0:TB], in1=Eap[:, c0:TB], op=ALU.mult),
                            reads=[eb, Ebf], writes=[wbf])
            for c in range(2):
                P.op("act", lambda e, c=c: e.activation(out=yT4[:, c, :], in_=psum[6 + c][:], func=AF.Identity),
                     reads=[b_ps[6 + c]], writes=[b_yT4[c]])
            for s_ in range(NSUB):
                P.op("dve", lambda e, s_=s_: e.memset(xn[:, s_, 0:2], 0.0), writes=[b_xn[s_]] + b_xnh[s_])
            P.tag = "mix_gates"
            for d in range(NKC):
                slabg_, bslg_ = load_slab(l, f"m_gate{d}")
                if d % 4 == 0:
                    slabb, bslb = load_slab(l, f"m_br{d // 4}")
                for ni, n in enumerate((1, 2, 3, 0)):
                    psg, bpsg = next_ps()
                    for k in range(NKC):
                        o = (n * NKC + k) * 128
                        mm(psg[:], bpsg, slabg_[:, o:o + 128], xTb[:, k, :], [bslg_, b_xTb[k]], k == 0, k == NKC - 1)
                    gb = vcol(l, f"gate_b{n}", d)
                    gi = n % 2
                    P.op("act", lambda e, psg=psg, gi=gi, gb=gb: e.activation(out=S32[:, gi, :], in_=psg[:], func=AF.Sigmoid,
                                                                             bias=gb, scale=1.0),
                         reads=[bpsg, b_vecs], writes=[b_S32[gi]])
                    psb, bpsb = next_ps()
                    for kk in range(2):
                        o = (((d % 4) * 4 + n) * 2 + kk) * 128
                        mm(psb[:], bpsb, slabb[:, o:o + 128], yT4[:, 2 * n + kk, :], [bslb, b_yT4[2 * n + kk]], kk == 0, kk == 1)
                    ai = 4 + d % 2
                    if ni == 0:
                        P.op("dve", lambda e, psb=psb, gi=gi, ai=ai: e.tensor_tensor(out=S32[:, ai, :], in0=psb[:], in1=S32[:, gi, :],
                                                                                    op=ALU.mult),
                             reads=[bpsb, b_S32[gi]], writes=[b_S32[ai]])
                    else:
                        ti = 2 + n % 2
                        P.op("dve", lambda e, psb=psb, gi=gi, ti=ti: e.tensor_tensor(out=S32[:, ti, :], in0=psb[:], in1=S32[:, gi, :],
                                                                                    op=ALU.mult),
                             reads=[bpsb, b_S32[gi]], writes=[b_S32[ti]])
                        if ni < 3:
                            P.op("dve", lambda e, ti=ti, ai=ai: e.tensor_tensor(out=S32[:, ai, :], in0=S32[:, ai, :], in1=S32[:, ti, :],
                                                                               op=ALU.add),
                                 reads=[b_S32[ai], b_S32[ti]], writes=[b_S32[ai]])
                        else:
                            P.op("dve", lambda e, ti=ti, ai=ai, d=d: e.tensor_tensor(out=mergedT[:, d, :], in0=S32[:, ai, :],
                                                                                    in1=S32[:, ti, :], op=ALU.add),
                                 reads=[b_S32[ai], b_S32[ti]], writes=[b_merged[d]])
            P.tag = "mix_out"
            c_res = 1.0 / DN_ALPHA
            for d in range(NKC):
                if d % 4 == 0:
                    slabo, bslo = load_slab(l, f"m_out{d // 4}")
                ps, bps = next_ps()
                for k in range(NKC):
                    o = ((d % 4) * NKC + k) * 128
                    mm(ps[:], bps, slabo[:, o:o + 128], mergedT[:, k, :], [bslo, b_merged[k]], k == 0, k == NKC - 1)
                P.op("dve", lambda e, ps=ps, d=d: e.scalar_tensor_tensor(out=xT[:, d, :], in0=ps[:], scalar=c_res,
                                                                         in1=xT[:, d, :], op0=ALU.mult, op1=ALU.add),
                     reads=[bps, b_xT[d]], writes=[b_xT[d]])
            layer_norm_T(l, "ln_g1", "ln_b1")

        def names_of(ph):
            if ph == "f0":
                return [n for n, _ in SLABS if n.startswith("f0_")]
            if ph == "f1":
                return [n for n, _ in SLABS if n.startswith("f1_")]
            return [n for n, _ in SLABS if n.startswith("m_")]

        def prepass_names(l, names):
            for nm in names:
                for i, (ll, name, sz) in enumerate(pre_list):
                    if ll == l and name == nm:
                        pre_list.insert(0, pre_list.pop(i))
                        emit_prepass(1)
                        break

        phase_fn = {"f0": lambda l, blk: ffn_phase(l, 0, 0), "mix": mixer_phase, "f1": lambda l, blk: ffn_phase(l, 1, 2)}
        order = [ph for ph in ("f0", "mix", "f1") if ph in cfg.phases]
        seq = [(l, blk) for l in range(cfg.depth) for blk in range(NBLK)]

        def prefetch(l, blk):
            if l == 0:
                prefetch_x(blk)
            else:
                prefetch_xs(blk)

        b_xs = bufs("xs", NBLK)
        for idx, (l, blk) in enumerate(seq):
            P.tag = "io"
            if idx == 0:
                if order:
                    prepass_names(0, names_of(order[0])[:4])
                prefetch(l, blk)
                if "mix" in cfg.phases:
                    layer_setup(0)
            if l > 0 and blk == 0:
                emit_prepass(10 ** 6)
            if l == 0:
                consume_x(blk)
            else:
                consume_xs(blk)
            for pi, ph in enumerate(order):
                P.tag = "io"
                if l == 0 and blk == 0:
                    prepass_names(0, names_of(ph))
                elif l == 0 and cfg.depth > 1:
                    emit_prepass(3)
                if pi == len(order) - 1 and ph != "mix" and idx + 1 < len(seq):
                    prefetch(*seq[idx + 1])
                phase_fn[ph](l, blk)
            if not (order and order[-1] != "mix") and idx + 1 < len(seq):
                prefetch(*seq[idx + 1])
            if l == cfg.depth - 1:
                store_block_to_out(blk)
            else:
                store_block_xs(blk)

        fin = sb("fin", [128, 1], F32)
        b_fin = Buf("fin")
        last_out_ops = [o for o in P.ops["pool"] if o.dsem in DS_XIO]
        tail = P.op("pool", lambda e: e.memset(fin[:], 0.0), writes=[b_fin])
        for q in range(4):
            lo = [o for o in last_out_ops if o.dsem == DS_XIO[q]]
            if lo:
                tail.deps.append(lo[-1])

        P.finalize()
        if getattr(cfg, "dbg", None) is not None:
            P.dbg = cfg.dbg
        if getattr(cfg, "pe_tags", None) is not None:
            cfg.pe_tags.extend(o.tag for o in P.ops["pe"])
        dsem_names = [DS_CONST, "cb", "cf"] + DS_RING + DS_PRE + DS_XIO + [DS_XS] + DS_PF
        esems = {e: es.enter_context(nc.semaphore(f"e_{e}")) for e in Prog.ENGS}
        dsems = {n: es.enter_context(nc.semaphore(f"d_{n}")) for n in dsem_names}
        block = es.enter_context(nc.Block())

        @block.tensor
        def _(e):
            P.emit("pe", e, esems, dsems)

        @block.scalar
        def _(e):
            P.emit("act", e, esems, dsems)

        @block.vector
        def _(e):
            P.emit("dve", e, esems, dsems)

        @block.gpsimd
        def _(e):
            P.emit("pool", e, esems, dsems)

        @block.sync
        def _(e):
            P.emit("sp", e, esems, dsems)

    return nc


_PROGRAM_CACHE = {}


def kernel(**inputs):
    inp = {k: np.asarray(v) for k, v in inputs.items()}
    x = inp["x"].astype(np.float32, copy=False)
    B = x.shape[0]
    W = pack_weights(inp)
    V = pack_vecs(inp)
    C = pack_consts(inp)
    cfg = Cfg()
    nc = build_program(cfg)
    in_maps = [dict(x=np.ascontiguousarray(x[b]), w=W, vecs=V, **C) for b in range(B)]
    res = run_bass_kernel_spmd(nc, in_maps, core_ids=list(range(B)))
    return np.stack([np.asarray(r["out"]) for r in res.results], axis=0).astype(np.float32)
```

```python
import math
from contextlib import ExitStack

import numpy as np

import concourse.bass as bass
import concourse.mybir as mybir
from concourse.bass_utils import run_bass_kernel_spmd

F32 = mybir.dt.float32
BF16 = mybir.dt.bfloat16
AF = mybir.ActivationFunctionType
ALU = mybir.AluOpType

D_MODEL = 1024
SEQ = 4096
DEPTH = 2
BW = 256
HEAD_DIM = 64
D_FF = 2816
CONV_W = 31
LN_EPS = 1e-5
DN_ALPHA = (2.0 * DEPTH) ** 0.25
POOL_WINDOWS = (2, 4, 8, 16)

TB = 512
NSUB = TB // 128
NKC = D_MODEL // 128
NFC = D_FF // 128
SLOT = 4096
NSLOT = 5


class Buf:
    __slots__ = ("name", "w", "r", "excl")

    def __init__(self, name, excl=False):
        self.name = name
        self.w = None
        self.r = {}
        self.excl = excl


class Op:
    __slots__ = ("eng", "fn", "deps", "signal", "sigval", "dsem", "key", "tag")


class Prog:
    ENGS = ("pe", "act", "dve", "pool", "sp")

    def __init__(self):
        self.ops = {e: [] for e in self.ENGS}
        self.dsem_count = {}
        self.last_dma = {}
        self.tag = ""

    def op(self, eng, fn, reads=(), writes=(), dsem=None):
        o = Op()
        o.eng = eng
        o.fn = fn
        o.signal = False
        o.sigval = None
        o.dsem = dsem
        o.tag = self.tag
        o.key = ("d", dsem) if dsem is not None else ("e", eng)
        deps = {}
        for b in reads:
            if b.w is not None:
                deps[id(b.w)] = b.w
            if b.excl:
                for r in b.r.values():
                    if r.key != o.key:
                        deps[id(r)] = r
        for b in writes:
            if b.w is not None:
                deps[id(b.w)] = b.w
            for r in b.r.values():
                deps[id(r)] = r
        if dsem is not None:
            pd = self.last_dma.get(dsem)
            if pd is not None:
                deps[id(pd)] = pd
            self.last_dma[dsem] = o
        dl = []
        for d in deps.values():
            if d is o:
                continue
            if d.dsem is None and d.eng == "pe" and eng == "pe" and dsem is None:
                continue
            d.signal = True
            dl.append(d)
        o.deps = dl
        for b in reads:
            b.r[o.key] = o
        for b in writes:
            b.w = o
            b.r = {}
        self.ops[eng].append(o)
        return o

    def finalize(self):
        cnt = {e: 0 for e in self.ENGS}
        for e in self.ENGS:
            for o in self.ops[e]:
                if o.dsem is not None:
                    c = self.dsem_count.get(o.dsem, 0) + 16
                    self.dsem_count[o.dsem] = c
                    o.sigval = c
                elif o.signal:
                    cnt[e] += 1
                    o.sigval = cnt[e]
        return cnt

    def emit(self, eng, handle, esems, dsems):
        seen = {}
        dbg = getattr(self, "dbg", None)
        for o in self.ops[eng]:
            if dbg is not None:
                dbg.append((eng, o.fn.__code__.co_firstlineno, [(d.key, d.sigval) for d in o.deps], o.key, o.sigval))
            need = {}
            for d in o.deps:
                k = d.key
                if d.sigval > need.get(k, 0):
                    need[k] = d.sigval
            for k, v in need.items():
                if v > seen.get(k, 0):
                    seen[k] = v
                    sem = dsems[k[1]] if k[0] == "d" else esems[k[1]]
                    handle.wait_ge(sem, v)
            ins = o.fn(handle)
            if o.dsem is not None:
                ins.then_inc(dsems[o.dsem], 16)
            elif o.signal:
                ins.then_inc(esems[eng], 1)


def _slab_list():
    sl = []
    for i in (0,):
        pass
    def ffn(i):
        for j2 in range(NFC // 2):
            sl.append((f"f{i}_in{j2}", 4096))
        for d in range(NKC):
            sl.append((f"f{i}_out{d}", NFC * 128))
    ffn(0)
    sl.append(("m_qk", 4096))
    sl.append(("m_tok", 4096))
    sl.append(("m_pool", 2048))
    sl.append(("m_poolw", 256))
    sl.append(("m_ua", 4096))
    sl.append(("m_g", 2048))
    for d in range(NKC):
        sl.append((f"m_gate{d}", 4096))
        if d % 4 == 0:
            sl.append((f"m_br{d // 4}", 4096))
    sl.append(("m_out0", 4096))
    sl.append(("m_out1", 4096))
    ffn(1)
    return sl


SLABS = _slab_list()
SLAB_OFF = {}
_o = 0
for _n, _sz in SLABS:
    SLAB_OFF[_n] = (_o, _sz)
    _o += _sz
LAYER_W = _o


def _kc(w, cols):
    k = w.shape[0] // 128
    nch = len(cols) // 128
    a = w[:, cols].reshape(k, 128, nch, 128)
    return a.transpose(1, 2, 0, 3)


def pack_weights(inp):
    W = np.empty((128, DEPTH * LAYER_W), np.float32)
    for l in range(DEPTH):
        def put(name, arr):
            off, sz = SLAB_OFF[name]
            a = np.ascontiguousarray(arr).reshape(128, -1)
            assert a.shape[1] == sz, (name, a.shape, sz)
            W[:, l * LAYER_W + off: l * LAYER_W + off + sz] = a
        for i in range(2):
            w_in = inp["ffn_w_in"][l, i]
            w4 = w_in.reshape(NKC, 128, 2, NFC, 128)
            for j2 in range(NFC // 2):
                a = w4[:, :, :, 2 * j2:2 * j2 + 2, :]
                put(f"f{i}_in{j2}", a.transpose(1, 3, 2, 0, 4))
            w_out = inp["ffn_w_out"][l, i].reshape(NFC, 128, NKC, 128)
            for d in range(NKC):
                put(f"f{i}_out{d}", w_out[:, :, d, :].transpose(1, 0, 2))
        mw = inp["mix_w_in"][l]
        put("m_qk", _kc(mw, np.arange(0, 512)))
        tokc = np.concatenate([np.arange(512, 768), np.arange(1024, 1280)])
        put("m_tok", mw[:, tokc].reshape(NKC, 128, 512).transpose(1, 0, 2))
        put("m_pool", mw[:, 1280:1536].reshape(NKC, 128, 256).transpose(1, 0, 2))
        pw = np.zeros((128, 2, 128), np.float32)
        for g in range(4):
            r = (g % 2) * 64
            pw[r:r + 64, g // 2, r:r + 64] = inp["pool_w"][l, g]
        put("m_poolw", pw)
        put("m_ua", _kc(mw, np.concatenate([np.arange(768, 1024), np.arange(1536, 1792)])))
        put("m_g", _kc(mw, np.arange(1792, 2048)))
        gw = inp["gate_w"][l].reshape(4, NKC, 128, NKC, 128)
        for d in range(NKC):
            put(f"m_gate{d}", gw[:, :, :, d, :].transpose(2, 0, 1, 3))
        bw = inp["branch_w"][l].reshape(4, 2, 128, NKC, 128)
        for h in range(2):
            put(f"m_br{h}", bw[:, :, :, 4 * h:4 * h + 4, :].transpose(2, 3, 0, 1, 4))
        ow = inp["out_w"][l].reshape(NKC, 128, NKC, 128)
        for h in range(2):
            put(f"m_out{h}", ow[:, :, 4 * h:4 * h + 4, :].transpose(1, 2, 0, 3))
    return W


VEC_COLS = {}
_c = 0
def _vc(name, n):
    global _c
    VEC_COLS[name] = _c
    _c += n
for _i in range(3):
    _vc(f"ln_g{_i}", 8); _vc(f"ln_b{_i}", 8)
for _n in range(4):
    _vc(f"gate_b{_n}", 8)
_vc("pool_scale", 2); _vc("conv_b", 2); _vc("conv_ln_g", 2); _vc("conv_ln_b", 2)
_vc("conv_w", 62)
NVEC = _c


def pack_vecs(inp):
    V = np.zeros((128, DEPTH * NVEC), np.float32)
    for l in range(DEPTH):
        def put(name, v):
            c0 = l * NVEC + VEC_COLS[name]
            a = np.asarray(v).reshape(-1, 128).T
            V[:, c0:c0 + a.shape[1]] = a
        for i in range(3):
            put(f"ln_g{i}", inp["ln_g"][l, i]); put(f"ln_b{i}", inp["ln_b"][l, i])
        for n in range(4):
            put(f"gate_b{n}", inp["gate_b"][l, n])
        put("pool_scale", inp["pool_scale"][l]); put("conv_b", inp["conv_b"][l])
        put("conv_ln_g", inp["conv_ln_g"][l]); put("conv_ln_b", inp["conv_ln_b"][l])
        cw = inp["conv_w"][l]
        c0 = l * NVEC + VEC_COLS["conv_w"]
        for c in range(2):
            V[:, c0 + c * 31: c0 + (c + 1) * 31] = cw[:, c * 128:(c + 1) * 128].T
    return V


CB_NTRI, CB_NUSTR, CB_TRILE = 0, 128, 256
CB_BAND, CB_BANDP, CB_BAND0 = 384, 896, 1408
CB_MASK = 1920
NCB = CB_MASK + 2048
CF_SGW, CF_SGB, CF_SGG, CF_SGBE = 0, 512, 1536, 1792
NCF = 2048


def pack_consts(inp=None):
    c = {}
    c["ident"] = np.eye(128, dtype=np.float32)
    j = np.arange(128)[:, None]
    t = np.arange(128)[None, :]
    cb = np.zeros((128, NCB), np.float32)
    cb[:, CB_NTRI:CB_NTRI + 128] = -1.0 * (j >= t)
    cb[:, CB_NUSTR:CB_NUSTR + 128] = -1.0 * (j < t)
    cb[:, CB_TRILE:CB_TRILE + 128] = (j <= t)
    for g, win in enumerate(POOL_WINDOWS):
        band = ((j <= t) & (j > t - win)).astype(np.float32) / win - (j == t)
        cnt = np.minimum(t + 1, win).astype(np.float32)
        band0 = ((j <= t) & (j > t - win)).astype(np.float32) / cnt - (j == t)
        bandp = ((j + 0 - 128 > t - win)).astype(np.float32) / win
        cb[:, CB_BAND + g * 128:CB_BAND + (g + 1) * 128] = band
        cb[:, CB_BANDP + g * 128:CB_BANDP + (g + 1) * 128] = bandp
        cb[:, CB_BAND0 + g * 128:CB_BAND0 + (g + 1) * 128] = band0
    col = np.arange(512)[None, :]
    for d in range(4):
        cb[:, CB_MASK + d * 512:CB_MASK + (d + 1) * 512] = (col > 128 * d + j)
    c["cb"] = cb
    if inp is not None:
        cf = np.zeros((128, DEPTH * NCF), np.float32)
        for l in range(DEPTH):
            o = l * NCF
            sgw = inp["sg_w"][l]
            cf[:, o + CF_SGW:o + CF_SGW + 512] = sgw.transpose(2, 0, 1).reshape(128, 512)
            sgb = inp["sg_b"][l]
            rep = np.empty((128, 2, 4, 128), np.float32)
            for cc in range(2):
                rep[0:64, cc] = sgb[2 * cc][None, None, :]
                rep[64:128, cc] = sgb[2 * cc + 1][None, None, :]
            cf[:, o + CF_SGB:o + CF_SGB + 1024] = rep.reshape(128, 1024)
            cf[:, o + CF_SGG:o + CF_SGG + 256] = inp["sg_ln_g"][l][None, :]
            cf[:, o + CF_SGBE:o + CF_SGBE + 256] = inp["sg_ln_b"][l][None, :]
        c["cf"] = cf
    return c


class Cfg:
    def __init__(self, seq=SEQ, depth=DEPTH, phases=("f0", "mix", "f1"), dbg=None):
        self.seq = seq
        self.depth = depth
        self.phases = phases
        self.nblk = seq // TB
        self.dbg = dbg


def build_program(cfg):
    nc = bass.Bass("TRN2", target_bir_lowering=False)
    S = cfg.seq
    NBLK = cfg.nblk
    x_in = nc.dram_tensor("x", [S, D_MODEL], F32, kind="ExternalInput").ap()
    w_in = nc.dram_tensor("w", [128, DEPTH * LAYER_W], F32, kind="ExternalInput").ap()
    vec_in = nc.dram_tensor("vecs", [128, DEPTH * NVEC], F32, kind="ExternalInput").ap()
    ident_in = nc.dram_tensor("ident", [128, 128], F32, kind="ExternalInput").ap()
    cb_in = nc.dram_tensor("cb", [128, NCB], F32, kind="ExternalInput").ap()
    cf_in = nc.dram_tensor("cf", [128, DEPTH * NCF], F32, kind="ExternalInput").ap()
    out = nc.dram_tensor("out", [S, D_MODEL], F32, kind="ExternalOutput").ap()
    wb = nc.dram_tensor("wb", [128, DEPTH * LAYER_W], BF16, kind="Internal").ap()
    xs = nc.dram_tensor("xs", [NBLK, 128, NKC, TB], F32, kind="Internal").ap()

    P = Prog()
    es = ExitStack()

    def sb(name, shape, dt):
        return es.enter_context(nc.sbuf_tensor("s_" + name, shape, dt))

    def bufs(name, n):
        return [Buf(f"{name}{i}") for i in range(n)]

    with es:
        ident = sb("ident", [128, 128], F32); b_ident = Buf("ident")
        vecs = sb("vecs", [128, DEPTH * NVEC], F32); b_vecs = Buf("vecs")
        ring = sb("ring", [128, NSLOT, SLOT], BF16); b_ring = bufs("ring", NSLOT)
        xT = sb("xT", [128, NKC, TB], F32); b_xT = bufs("xT", NKC)
        xTb = sb("xTb", [128, NKC, TB], BF16); b_xTb = bufs("xTb", NKC)
        xn = sb("xn", [128, NSUB, D_MODEL], F32); b_xn = bufs("xn", NSUB)
        gT = sb("gT", [128, NFC, TB], BF16); b_gT = bufs("gT", NFC)
        S32 = sb("S32", [128, 8, TB], F32); b_S32 = bufs("S32", 8)
        stt = sb("stt", [128, 4, 12], F32); b_stt = bufs("stt", 4)
        mv = sb("mv", [128, 4, 2], F32); b_mv = bufs("mv", 4)
        rs = sb("rs", [128, 4, 4], F32); b_rs = bufs("rs", 4); b_rs0 = bufs("rs0_", 4); b_rs1 = bufs("rs1_", 4)
        psum = [es.enter_context(nc.psum_tensor(f"ps{i}", [128, 512], F32)) for i in range(8)]
        b_ps = [Buf(f"ps{i}", excl=True) for i in range(8)]
        ps_rr = [0]

        def next_ps(subset=range(8)):
            subset = list(subset)
            i = subset[ps_rr[0] % len(subset)]
            ps_rr[0] += 1
            return psum[i], b_ps[i]

        DS_CONST = "const"
        DS_RING = [f"ring{i}" for i in range(NSLOT)]
        DS_PRE = [f"pre{i}" for i in range(8)]
        DS_XIO = ["xio0", "xio1", "xio2", "xio3"]
        DS_XS = "xs"

        P.op("pool", lambda e: e.dma_start(out=ident[:], in_=ident_in[:]), writes=[b_ident], dsem=DS_CONST)
        P.op("pool", lambda e: e.dma_start(out=vecs[:], in_=vec_in[:]), writes=[b_vecs], dsem=DS_CONST)

        b_wb = {}
        npre = [0]
        pre_list = [(l, name, sz) for l in range(cfg.depth) for (name, sz) in SLABS]

        def emit_prepass(n):
            if getattr(cfg, "no_prepass", False):
                return
            for _ in range(n):
                if not pre_list:
                    return
                l, name, sz = pre_list.pop(0)
                off = l * LAYER_W + SLAB_OFF[name][0]
                b = Buf(f"wb{l}{name}")
                b_wb[(l, name)] = b
                P.op("pool",
                     lambda e, off=off, sz=sz: e.dma_start(out=wb[:, off:off + sz], in_=w_in[:, off:off + sz],
                                                            max_dma_last_dim=8192),
                     writes=[b], dsem=DS_PRE[npre[0] % 8])
                npre[0] += 1


        slab_ctr = [0]

        def load_slab(l, name):
            off, sz = SLAB_OFF[name]
            off += l * LAYER_W
            k = slab_ctr[0] % NSLOT
            slab_ctr[0] += 1
            P.op("sp", lambda e, k=k, off=off, sz=sz: e.dma_start(out=ring[:, k, 0:sz], in_=wb[:, off:off + sz]),
                 reads=[b_wb[(l, name)]], writes=[b_ring[k]], dsem=DS_RING[k])
            return ring[:, k, :], b_ring[k]

        def vcol(l, name, j):
            c = l * NVEC + VEC_COLS[name] + j
            return vecs[:, c:c + 1]

        def mm(ps, bps, lhsT, rhs, rd, start, stop, skip=False):
            P.op("pe", lambda e: e.matmul(ps, lhsT=lhsT, rhs=rhs, start=start, stop=stop, skip_group_check=skip),
                 reads=rd, writes=[bps])

        def tr(ps, bps, in_, rd):
            P.op("pe", lambda e: e.transpose(ps, in_, ident[:]), reads=rd + [b_ident], writes=[bps])

        ln_ctr = [0]

        def layer_norm_T(l, gname, bname):
            P.tag = "ln"
            pzs = {}
            for s in range(NSUB + 1):
                if s < NSUB:
                    q = s % 2
                    pz = [next_ps(), next_ps()]
                    pzs[s] = pz
                    for d in range(NKC):
                        ps, bps = pz[d // 4]
                        tr(ps[:, (d % 4) * 128:(d % 4 + 1) * 128], bps, xT[:, d, s * 128:(s + 1) * 128], [b_xT[d]])
                    for h in range(2):
                        ps, bps = pz[h]
                        P.op("dve", lambda e, ps=ps, q=q, h=h: e.bn_stats(out=stt[:, q, h * 6:(h + 1) * 6], in_=ps[:]),
                             reads=[bps], writes=[b_stt[q]])
                    P.op("dve", lambda e, q=q: e.bn_aggr(out=mv[:, q, :], in_=stt[:, q, :]),
                         reads=[b_stt[q]], writes=[b_mv[q]])
                    P.op("act", lambda e, q=q: e.activation(out=rs[:, q, 0:1], in_=mv[:, q, 1:2], func=AF.Sqrt,
                                                            bias=eps_t[:, 0:1], scale=1.0),
                         reads=[b_mv[q], b_eps], writes=[b_rs0[q]])
                if s >= 1:
                    sp_ = s - 1
                    q = sp_ % 2
                    P.op("dve", lambda e, q=q: e.reciprocal(out=rs[:, q, 1:2], in_=rs[:, q, 0:1]),
                         reads=[b_rs0[q]], writes=[b_rs1[q]])
                    P.op("dve", lambda e, q=q: e.scalar_tensor_tensor(out=rs[:, q, 2:3], in0=mv[:, q, 0:1], scalar=-1.0,
                                                                       in1=rs[:, q, 1:2], op0=ALU.mult, op1=ALU.mult),
                         reads=[b_rs1[q], b_mv[q]], writes=[b_rs[q]])
                    for h in range(2):
                        ps, bps = pzs[sp_][h]
                        P.op("act", lambda e, ps=ps, q=q, h=h, sp_=sp_: e.activation(
                            out=xn[:, sp_, h * 512:(h + 1) * 512], in_=ps[:], func=AF.Identity,
                            bias=rs[:, q, 2:3], scale=rs[:, q, 1:2]),
                            reads=[bps, b_rs[q], b_rs1[q]], writes=[b_xn[sp_]])
            for d in range(NKC):
                ps, bps = next_ps()
                for s in range(NSUB):
                    tr(ps[:, s * 128:(s + 1) * 128], bps, xn[:, s, d * 128:(d + 1) * 128], [b_xn[s]])
                g = vcol(l, gname, d)
                b = vcol(l, bname, d)
                if d % 2 == 0:
                    P.op("act", lambda e, ps=ps, d=d, g=g, b=b: e.activation(out=xT[:, d, :], in_=ps[:], func=AF.Identity,
                                                                             bias=b, scale=g),
                         reads=[bps, b_vecs], writes=[b_xT[d]])
                    P.op("dve", lambda e, d=d: e.tensor_copy(out=xTb[:, d, :], in_=xT[:, d, :]),
                         reads=[b_xT[d]], writes=[b_xTb[d]])
                else:
                    P.op("dve", lambda e, ps=ps, d=d, g=g, b=b: e.tensor_scalar(out=xT[:, d, :], in0=ps[:], scalar1=g,
                                                                               scalar2=b, op0=ALU.mult, op1=ALU.add),
                         reads=[bps, b_vecs], writes=[b_xT[d]])
                    P.op("act", lambda e, d=d: e.activation(out=xTb[:, d, :], in_=xT[:, d, :], func=AF.Identity),
                         reads=[b_xT[d]], writes=[b_xTb[d]])

        def ffn_phase(l, i, ln_idx):
            c_res = 0.5 / DN_ALPHA
            P.tag = "ffn_in"
            b_sil = [Buf("silh0"), Buf("silh1")]
            P.op("dve", lambda e: e.memset(xn[:, 0, 0:2], 0.0), writes=[b_xn[0]] + b_sil)
            for j2 in range(NFC // 2):
                slab, bsl = load_slab(l, f"f{i}_in{j2}")
                for jj in range(2):
                    j = 2 * j2 + jj
                    pa = next_ps()
                    pu = next_ps()
                    for t, (ps, bps) in enumerate((pa, pu)):
                        for k in range(NKC):
                            o = ((jj * 2 + t) * NKC + k) * 128
                            mm(ps[:], bps, slab[:, o:o + 128], xTb[:, k, :], [bsl, b_xTb[k]], k == 0, k == NKC - 1)
                    q = j % 2
                    P.op("act", lambda e, ps=pa[0], q=q: e.activation(out=xn[:, 0, q * 512:(q + 1) * 512], in_=ps[:], func=AF.Silu),
                         reads=[pa[1]], writes=[b_sil[q]])
                    P.op("dve", lambda e, ps=pu[0], q=q, j=j: e.tensor_tensor(out=gT[:, j, :], in0=ps[:],
                                                                             in1=xn[:, 0, q * 512:(q + 1) * 512], op=ALU.mult),
                         reads=[pu[1], b_sil[q]], writes=[b_gT[j]])
            P.op("dve", lambda e: e.memset(xn[:, 0, 0:2], 0.0), writes=[b_xn[0]] + b_sil)
            P.tag = "ffn_out"
            for d in range(NKC):
                slab, bsl = load_slab(l, f"f{i}_out{d}")
                ps, bps = next_ps()
                for j in range(NFC):
                    mm(ps[:], bps, slab[:, j * 128:(j + 1) * 128], gT[:, j, :], [bsl, b_gT[j]], j == 0, j == NFC - 1)
                P.op("dve", lambda e, ps=ps, d=d: e.scalar_tensor_tensor(out=xT[:, d, :], in0=ps[:], scalar=c_res,
                                                                         in1=xT[:, d, :], op0=ALU.mult, op1=ALU.add),
                     reads=[bps, b_xT[d]], writes=[b_xT[d]])
            layer_norm_T(l, f"ln_g{ln_idx}", f"ln_b{ln_idx}")

        eps_t = sb("eps_t", [128, 1], F32); b_eps = Buf("eps")
        P.op("dve", lambda e: e.memset(eps_t[:], LN_EPS / (DN_ALPHA * DN_ALPHA)), writes=[b_eps])
        eps1 = sb("eps1", [128, 1], F32)
        one1 = sb("one1", [128, 1], F32)
        mhalf = sb("mhalf", [128, 1], F32)
        P.op("dve", lambda e: e.memset(mhalf[:], -0.5), writes=[b_eps])
        P.op("dve", lambda e: e.memset(eps1[:], LN_EPS), writes=[b_eps])
        P.op("dve", lambda e: e.memset(one1[:], 1.0), writes=[b_eps])

        DS_PF = ["pf0", "pf1", "pf2", "pf3"]

        def prefetch_x(blk):
            for s_ in range(NSUB):
                r0 = blk * TB + s_ * 128
                P.op("pool", lambda e, s_=s_, r0=r0: e.dma_start(
                    out=S32[:, 2 * s_:2 * s_ + 2, :], in_=x_in[r0:r0 + 128, :].rearrange("p (a b) -> p a b", a=2)),
                    writes=[b_S32[2 * s_], b_S32[2 * s_ + 1]], dsem=DS_PF[s_])

        def consume_x(blk):
            P.tag = "io"
            for d in range(NKC):
                ps, bps = next_ps()
                for s_ in range(NSUB):
                    sl = 2 * s_ + d // 4
                    tr(ps[:, s_ * 128:(s_ + 1) * 128], bps, S32[:, sl, (d % 4) * 128:(d % 4 + 1) * 128], [b_S32[sl]])
                if d % 2 == 0:
                    P.op("act", lambda e, ps=ps, d=d: e.activation(out=xT[:, d, :], in_=ps[:], func=AF.Identity),
                         reads=[bps], writes=[b_xT[d]])
                    P.op("dve", lambda e, d=d: e.tensor_copy(out=xTb[:, d, :], in_=xT[:, d, :]),
                         reads=[b_xT[d]], writes=[b_xTb[d]])
                else:
                    P.op("dve", lambda e, ps=ps, d=d: e.tensor_copy(out=xT[:, d, :], in_=ps[:]),
                         reads=[bps], writes=[b_xT[d]])
                    P.op("act", lambda e, d=d: e.activation(out=xTb[:, d, :], in_=xT[:, d, :], func=AF.Identity),
                         reads=[b_xT[d]], writes=[b_xTb[d]])

        def prefetch_xs(blk):
            P.op("pool", lambda e, blk=blk: e.dma_start(out=S32[:], in_=xs[blk]),
                 reads=[b_xs[blk]], writes=b_S32, dsem=DS_PF[0])

        def consume_xs(blk):
            P.tag = "io"
            for d in range(NKC):
                if d % 2 == 0:
                    P.op("act", lambda e, d=d: e.activation(out=xT[:, d, :], in_=S32[:, d, :], func=AF.Identity),
                         reads=[b_S32[d]], writes=[b_xT[d]])
                    P.op("dve", lambda e, d=d: e.tensor_copy(out=xTb[:, d, :], in_=S32[:, d, :]),
                         reads=[b_S32[d]], writes=[b_xTb[d]])
                else:
                    P.op("dve", lambda e, d=d: e.tensor_copy(out=xT[:, d, :], in_=S32[:, d, :]),
                         reads=[b_S32[d]], writes=[b_xT[d]])
                    P.op("act", lambda e, d=d: e.activation(out=xTb[:, d, :], in_=S32[:, d, :], func=AF.Identity),
                         reads=[b_S32[d]], writes=[b_xTb[d]])

        def store_block_to_out(blk):
            P.tag = "io"
            for s in range(NSUB):
                q = s
                for h in range(2):
                    ps, bps = next_ps()
                    for dd in range(4):
                        d = h * 4 + dd
                        tr(ps[:, dd * 128:(dd + 1) * 128], bps, xT[:, d, s * 128:(s + 1) * 128], [b_xT[d]])
                    eng = "act" if h == 0 else "dve"
                    if eng == "act":
                        P.op("act", lambda e, ps=ps, q=q, h=h: e.activation(out=xn[:, q, h * 512:(h + 1) * 512],
                                                                            in_=ps[:], func=AF.Identity),
                             reads=[bps], writes=[b_xn[q]])
                    else:
                        P.op("dve", lambda e, ps=ps, q=q, h=h: e.tensor_copy(out=xn[:, q, h * 512:(h + 1) * 512],
                                                                             in_=ps[:]),
                             reads=[bps], writes=[b_xn[q]])
                r0 = blk * TB + s * 128
                P.op("pool", lambda e, q=q, r0=r0: e.dma_start(out=out[r0:r0 + 128, :], in_=xn[:, q, :]),
                     reads=[b_xn[q]], dsem=DS_XIO[q])

        def store_block_xs(blk):
            P.op("pool", lambda e, blk=blk: e.dma_start(out=xs[blk], in_=xT[:]),
                 reads=b_xT, writes=[b_xs[blk]], dsem=DS_XS)

        cbt = sb("cbt", [128, NCB], BF16); b_cb = Buf("cb")
        cft = sb("cft", [128, NCF], F32); b_cf = Buf("cf")
        wsm = sb("wsm", [128, 4, 128], BF16); b_wsm = Buf("wsm")
        kTc = sb("kTc", [128, 2, S], BF16); b_kTc = Buf("kTc")
        Vc = sb("Vc", [128, S // 128, BW], BF16); b_Vc = Buf("Vc")
        B16 = sb("B16", [128, 8, TB], BF16); b_B16 = bufs("B16", 8)
        yT4 = sb("yT4", [128, 8, TB], BF16); b_yT4 = bufs("yT4", 8)
        qT = sb("qT", [128, 2, TB], BF16); b_qT = bufs("qT", 2)
        vln = sb("vln", [128, NSUB, BW], BF16); b_vln = bufs("vln", NSUB)
        ptok = sb("ptok", [128, NSUB + 1, BW], BF16); b_ptok = bufs("ptok", NSUB + 1)
        ybuf = sb("ybuf", [128, 2, 30 + TB], BF16); b_ybuf = bufs("ybuf", 2)
        dg = sb("dg", [128, 4, 128], BF16); b_dg = bufs("dg", 4)
        P.op("pool", lambda e: e.dma_start(out=cbt[:], in_=cb_in[:], max_dma_last_dim=8192), writes=[b_cb], dsem="cb")
        ntri = cbt[:, CB_NTRI:CB_NTRI + 128]
        nustr = cbt[:, CB_NUSTR:CB_NUSTR + 128]
        mergedT = gT
        b_merged = b_gT
        dg_ctr = [0]
        st_ctr = [0]

        def tok_stats_a(src_ap, src_buf, eps_ap):
            q = st_ctr[0] % 4
            st_ctr[0] += 1
            P.op("dve", lambda e: e.bn_stats(out=stt[:, q, 0:6], in_=src_ap), reads=[src_buf], writes=[b_stt[q]])
            P.op("dve", lambda e: e.bn_aggr(out=mv[:, q, :], in_=stt[:, q, 0:6]), reads=[b_stt[q]], writes=[b_mv[q]])
            P.op("act", lambda e: e.activation(out=rs[:, q, 0:1], in_=mv[:, q, 1:2], func=AF.Sqrt, bias=eps_ap, scale=1.0),
                 reads=[b_mv[q], b_eps], writes=[b_rs0[q]])
            return q

        def tok_stats_b(q):
            P.op("dve", lambda e: e.reciprocal(out=rs[:, q, 1:2], in_=rs[:, q, 0:1]), reads=[b_rs0[q]], writes=[b_rs1[q]])
            P.op("dve", lambda e: e.scalar_tensor_tensor(out=rs[:, q, 2:3], in0=mv[:, q, 0:1], scalar=-1.0,
                                                          in1=rs[:, q, 1:2], op0=ALU.mult, op1=ALU.mult),
                 reads=[b_rs1[q], b_mv[q]], writes=[b_rs[q]])
            return rs[:, q, 1:2], rs[:, q, 2:3], [b_rs[q], b_rs1[q]]

        def layer_setup(l):
            o = l * NCF
            P.op("pool", lambda e: e.dma_start(out=cft[:], in_=cf_in[:, o:o + NCF]), writes=[b_cf], dsem="cf")
            for g in range(4):
                P.op("dve", lambda e, g=g: e.tensor_tensor(out=wsm[:, g, :], in0=cft[:, CF_SGW + g * 128:CF_SGW + (g + 1) * 128],
                                                          in1=cbt[:, CB_TRILE:CB_TRILE + 128], op=ALU.mult),
                     reads=[b_cf, b_cb], writes=[b_wsm])
            for c in range(2):
                P.op("dve", lambda e, c=c: e.memset(ybuf[:, c, 0:30], 0.0), writes=[b_ybuf[c]])

        def mixer_phase(l, blk):
            t0 = blk * TB
            nt0 = blk * NSUB
            if blk == 0 and l > 0:
                layer_setup(l)
            P.tag = "mix_qk"
            slab, bsl = load_slab(l, "m_qk")
            for cc in range(4):
                ps, bps = next_ps()
                for k in range(NKC):
                    o = (cc * NKC + k) * 128
                    mm(ps[:], bps, slab[:, o:o + 128], xTb[:, k, :], [bsl, b_xTb[k]], k == 0, k == NKC - 1)
                if cc < 2:
                    P.op("act", lambda e, ps=ps, cc=cc: e.activation(out=qT[:, cc, :], in_=ps[:], func=AF.Identity),
                         reads=[bps], writes=[b_qT[cc]])
                else:
                    P.op("dve", lambda e, ps=ps, cc=cc: e.tensor_copy(out=kTc[:, cc - 2, t0:t0 + TB], in_=ps[:]),
                         reads=[bps], writes=[b_kTc])
            P.tag = "mix_tok"
            slab, bsl = load_slab(l, "m_tok")
            slabp, bslp = load_slab(l, "m_pool")
            slabw, bslw = load_slab(l, "m_poolw")
            for s in range(NSUB):
                ps, bps = next_ps()
                for k in range(NKC):
                    mm(ps[:], bps, xTb[:, k, s * 128:(s + 1) * 128], slab[:, k * 512:(k + 1) * 512],
                       [bsl, b_xTb[k]], k == 0, k == NKC - 1)
                P.op("dve", lambda e, ps=ps, s=s: e.tensor_copy(out=Vc[:, nt0 + s, :], in_=ps[:, 0:256]),
                     reads=[bps], writes=[b_Vc])
                P.op("act", lambda e, ps=ps, s=s: e.activation(out=xn[:, s, 256:512], in_=ps[:, 256:512],
                                                               func=AF.Gelu_apprx_tanh),
                     reads=[bps], writes=[b_xn[s]])
                ps2, bps2 = next_ps()
                for k in range(NKC):
                    mm(ps2[:, 0:256], bps2, xTb[:, k, s * 128:(s + 1) * 128], slabp[:, k * 256:(k + 1) * 256],
                       [bslp, b_xTb[k]], k == 0, k == NKC - 1)
                P.op("act", lambda e, ps2=ps2, s=s: e.activation(out=ptok[:, s + 1, :], in_=ps2[:, 0:256], func=AF.Identity),
                     reads=[bps2], writes=[b_ptok[s + 1]])
            P.tag = "mix_sgln"
            qs = [tok_stats_a(xn[:, s, 256:512], b_xn[s], eps1[:, 0:1]) for s in range(NSUB)]
            for s in range(NSUB):
                rstd, nmr, brs = tok_stats_b(qs[s])
                P.op("act", lambda e, s=s, rstd=rstd, nmr=nmr: e.activation(out=xn[:, s, 512:768], in_=xn[:, s, 256:512],
                                                                            func=AF.Identity, bias=nmr, scale=rstd),
                     reads=[b_xn[s]] + brs, writes=[b_xn[s]])
                P.op("dve", lambda e, s=s: e.tensor_tensor(out=xn[:, s, 512:768], in0=xn[:, s, 512:768],
                                                           in1=cft[:, CF_SGG:CF_SGG + 256], op=ALU.mult),
                     reads=[b_xn[s], b_cf], writes=[b_xn[s]])
                P.op("dve", lambda e, s=s: e.tensor_tensor(out=vln[:, s, :], in0=xn[:, s, 512:768],
                                                           in1=cft[:, CF_SGBE:CF_SGBE + 256], op=ALU.add),
                     reads=[b_xn[s], b_cf], writes=[b_vln[s]])
            P.tag = "mix_u"
            slab, bsl = load_slab(l, "m_ua")
            slabg, bslg = load_slab(l, "m_g")
            for c in range(2):
                ps, bps = next_ps()
                for k in range(NKC):
                    o = (c * NKC + k) * 128
                    mm(ps[:], bps, slab[:, o:o + 128], xTb[:, k, :], [bsl, b_xTb[k]], k == 0, k == NKC - 1)
                P.op("act", lambda e, ps=ps, c=c: e.activation(out=S32[:, 2 + c, :], in_=ps[:], func=AF.Gelu_apprx_tanh),
                     reads=[bps], writes=[b_S32[2 + c]])
            P.tag = "mix_conv"
            for c in range(2):
                psa, bpsa = next_ps()
                for k in range(NKC):
                    o = ((2 + c) * NKC + k) * 128
                    mm(psa[:], bpsa, slab[:, o:o + 128], xTb[:, k, :], [bsl, b_xTb[k]], k == 0, k == NKC - 1)
                psg, bpsg = next_ps()
                for k in range(NKC):
                    o = (c * NKC + k) * 128
                    mm(psg[:], bpsg, slabg[:, o:o + 128], xTb[:, k, :], [bslg, b_xTb[k]], k == 0, k == NKC - 1)
                P.op("act", lambda e, psg=psg: e.activation(out=S32[:, 0, :], in_=psg[:], func=AF.Sigmoid),
                     reads=[bpsg], writes=[b_S32[0]])
                P.op("dve", lambda e, psa=psa, c=c: e.tensor_tensor(out=ybuf[:, c, 30:30 + TB], in0=psa[:], in1=S32[:, 0, :],
                                                                    op=ALU.mult),
                     reads=[bpsa, b_S32[0]], writes=[b_ybuf[c]])
            for c in range(2):
                ps, bps = next_ps()
                for k in range(CONV_W):
                    q = dg_ctr[0] % 4
                    dg_ctr[0] += 1
                    wc = vcol(l, "conv_w", c * 31 + k)
                    P.op("dve", lambda e, q=q, wc=wc: e.tensor_scalar(out=dg[:, q, :], in0=ident[:], scalar1=wc, scalar2=None,
                                                                      op0=ALU.mult),
                         reads=[b_ident, b_vecs], writes=[b_dg[q]])
                    mm(ps[:], bps, dg[:, q, :], ybuf[:, c, k:k + TB], [b_dg[q], b_ybuf[c]], k == 0, k == CONV_W - 1)
                cb_ = vcol(l, "conv_b", c)
                P.op("act", lambda e, ps=ps, c=c, cb_=cb_: e.activation(out=S32[:, 6 + c, :], in_=ps[:], func=AF.Identity,
                                                                         bias=cb_, scale=1.0),
                     reads=[bps, b_vecs], writes=[b_S32[6 + c]])
                P.op("dve", lambda e, c=c: e.tensor_copy(out=ybuf[:, c, 0:30], in_=ybuf[:, c, TB:TB + 30]),
                     reads=[b_ybuf[c]], writes=[b_ybuf[c]])
            cps = []
            for s in range(NSUB):
                ps, bps = next_ps(range(4, 8))
                for c in range(2):
                    tr(ps[:, c * 128:(c + 1) * 128], bps, S32[:, 6 + c, s * 128:(s + 1) * 128], [b_S32[6 + c]])
                cps.append((ps, bps, tok_stats_a(ps[:, 0:256], bps, eps1[:, 0:1])))
            for s in range(NSUB):
                ps, bps, q = cps[s]
                rstd, nmr, brs = tok_stats_b(q)
                P.op("act", lambda e, ps=ps, s=s, rstd=rstd, nmr=nmr: e.activation(out=xn[:, s, 0:256], in_=ps[:, 0:256],
                                                                                   func=AF.Identity, bias=nmr, scale=rstd),
                     reads=[bps] + brs, writes=[b_xn[s]])
            P.tag = "mix_pool"
            for c in range(2):
                for par in range(2):
                    g = 2 * c + par
                    r0 = par * 64
                    ps, bps = next_ps(range(4))
                    for s in range(NSUB):
                        osl = ps[:, s * 128:(s + 1) * 128]
                        cur = ptok[:, s + 1, c * 128:(c + 1) * 128]
                        if blk == 0 and s == 0:
                            mm(osl, bps, cur, cbt[:, CB_BAND0 + g * 128:CB_BAND0 + (g + 1) * 128],
                               [b_ptok[s + 1], b_cb], True, True)
                        else:
                            mm(osl, bps, cur, cbt[:, CB_BAND + g * 128:CB_BAND + (g + 1) * 128],
                               [b_ptok[s + 1], b_cb], True, False)
                            mm(osl, bps, ptok[:, s, c * 128:(c + 1) * 128],
                               cbt[:, CB_BANDP + g * 128:CB_BANDP + (g + 1) * 128], [b_ptok[s], b_cb], False, True)
                    P.op("act", lambda e, ps=ps, c=c, r0=r0: e.activation(out=B16[r0:r0 + 64, 4 + c, :], in_=ps[r0:r0 + 64, :],
                                                                          func=AF.Identity),
                         reads=[bps], writes=[b_B16[4 + c]])
            P.op("dve", lambda e: e.tensor_copy(out=ptok[:, 0, :], in_=ptok[:, NSUB, :]), reads=[b_ptok[NSUB]],
                 writes=[b_ptok[0]])
            for c in range(2):
                ps, bps = next_ps(range(4))
                mm(ps[:], bps, slabw[:, c * 128:(c + 1) * 128], B16[:, 4 + c, :], [bslw, b_B16[4 + c]], True, True)
                sc_ = vcol(l, "pool_scale", c)
                P.op("act", lambda e, ps=ps, c=c, sc_=sc_: e.activation(out=yT4[:, 4 + c, :], in_=ps[:], func=AF.Identity,
                                                                        scale=sc_),
                     reads=[bps, b_vecs], writes=[b_yT4[4 + c]])
            P.tag = "mix_sgmix"
            for c in range(2):
                for par in range(2):
                    g = 2 * c + par
                    r0 = par * 64
                    ps, bps = next_ps(range(4))
                    for s in range(NSUB):
                        mm(ps[:, s * 128:(s + 1) * 128], bps, vln[:, s, c * 128:(c + 1) * 128], wsm[:, g, :],
                           [b_vln[s], b_wsm], True, True)
                    P.op("dve", lambda e, ps=ps, c=c, r0=r0: e.tensor_tensor(
                        out=S32[r0:r0 + 64, 4, :], in0=ps[r0:r0 + 64, :],
                        in1=cft[r0:r0 + 64, CF_SGB + c * 512:CF_SGB + (c + 1) * 512], op=ALU.add),
                        reads=[bps, b_cf], writes=[b_S32[4]])
                    P.op("dve", lambda e, c=c, r0=r0: e.tensor_tensor(
                        out=yT4[r0:r0 + 64, 2 + c, :], in0=S32[r0:r0 + 64, 4, :], in1=S32[r0:r0 + 64, 2 + c, :], op=ALU.mult),
                        reads=[b_S32[4], b_S32[2 + c]], writes=[b_yT4[2 + c]])
            P.tag = "mix_conv"
            for c in range(2):
                ps, bps = next_ps(range(4))
                for s in range(NSUB):
                    tr(ps[:, s * 128:(s + 1) * 128], bps, xn[:, s, c * 128:(c + 1) * 128], [b_xn[s]])
                g_ = vcol(l, "conv_ln_g", c)
                b_ = vcol(l, "conv_ln_b", c)
                P.op("act", lambda e, ps=ps, c=c, g_=g_, b_=b_: e.activation(out=yT4[:, 6 + c, :], in_=ps[:], func=AF.Silu,
                                                                             bias=b_, scale=g_),
                     reads=[bps, b_vecs], writes=[b_yT4[6 + c]])
            P.tag = "mix_attn"
            npair = nt0 + NSUB
            a_hi = npair - 1
            b_xnh = [[Buf(f"xnh{s_}{hh}") for hh in range(2)] for s_ in range(NSUB)]
            for s_ in range(NSUB):
                P.op("dve", lambda e, s_=s_: e.memset(xn[:, s_, 0:2], 0.0), writes=[b_xn[s_]] + b_xnh[s_])
            ebuf = []
            for h in range(4):
                ebuf.append([(xn[:, h, 0:512], b_xnh[h][0]), (xn[:, h, 512:1024], b_xnh[h][1]), (S32[:, h, :], b_S32[h])])
            Ebuf = [(S32[:, 4 + h, :], b_S32[4 + h]) for h in range(4)]
            spbuf = [[(gT[:, 8 + h * 3 + i, :], b_gT[8 + h * 3 + i]) for i in range(3)] for h in range(4)]
            wbuf = [[(B16[:, h * 2 + i, :], b_B16[h * 2 + i]) for i in range(2)] for h in range(4)]
            actr = [0]

            def pinfo(p):
                a = a_hi - p
                dd = a - nt0
                c0 = 128 * max(dd, 0)
                return a, dd, c0

            for t in range(npair + 2):
                if t < npair:
                    a, dd, c0 = pinfo(t)
                    pend_ln = None
                    for h in range(4):
                        c = h // 2
                        r0 = (h % 2) * 64
                        psA, bA = psum[actr[0] % 2], b_ps[actr[0] % 2]
                        actr[0] += 1
                        mm(psA[:, c0:TB], bA, kTc[r0:r0 + 64, c, a * 128:(a + 1) * 128], qT[r0:r0 + 64, c, c0:TB],
                           [b_kTc, b_qT[c]], True, True)
                        eap, eb = ebuf[h][t % 3]
                        P.op("act", lambda e, psA=psA, eap=eap, c0=c0: e.activation(out=eap[:, c0:TB], in_=psA[:, c0:TB],
                                                                                  func=AF.Exp, scale=0.125),
                             reads=[bA], writes=[eb])
                        if dd >= 0:
                            P.op("dve", lambda e, eap=eap, dd=dd, c0=c0: e.tensor_tensor(
                                out=eap[:, c0:TB], in0=eap[:, c0:TB], in1=cbt[:, CB_MASK + dd * 512 + c0:CB_MASK + (dd + 1) * 512],
                                op=ALU.mult), reads=[eb, b_cb], writes=[eb])
                        if pend_ln is not None:
                            pend_ln()
                        sap, sbf = spbuf[h][t % 3]

                        def _ln(eap=eap, eb=eb, sap=sap, sbf=sbf, c0=c0):
                            P.op("act", lambda e: e.activation(out=sap[:, c0:TB], in_=eap[:, c0:TB], func=AF.Ln, bias=1.0, scale=1.0),
                                 reads=[eb], writes=[sbf])
                        pend_ln = _ln
                    pend_ln()
                if t >= 2:
                    p = t - 2
                    a, dd, c0 = pinfo(p)
                    for h in range(4):
                        c = h // 2
                        r0 = (h % 2) * 64
                        sap, sbf = spbuf[h][p % 3]
                        wap, wbf = wbuf[h][p % 2]
                        if p < npair - 1:
                            mm(psum[2 + h][:, c0:TB], b_ps[2 + h], nustr, sap[:, c0:TB], [b_cb, sbf], False, True, skip=True)
                        mm(psum[6 + c][r0:r0 + 64, c0:TB], b_ps[6 + c], Vc[:, a, h * 64:(h + 1) * 64], wap[:, c0:TB],
                           [b_Vc, wbf], p == 0, True, skip=(p > 0))
                if 1 <= t <= npair:
                    p = t - 1
                    a, dd, c0 = pinfo(p)
                    for h in range(4):
                        sap, sbf = spbuf[h][p % 3]
                        eap, eb = ebuf[h][p % 3]
                        Eap, Ebf = Ebuf[h]
                        wap, wbf = wbuf[h][p % 2]
                        psB, bB = psum[2 + h], b_ps[2 + h]
                        mm(psB[:, c0:TB], bB, ntri, sap[:, c0:TB], [b_cb, sbf], p == 0, True, skip=(p > 0))
                        P.op("act", lambda e, psB=psB, Eap=Eap, c0=c0: e.activation(out=Eap[:, c0:TB], in_=psB[:, c0:TB], func=AF.Exp),
                             reads=[bB], writes=[Ebf])
                        P.op("dve", lambda e, wap=wap, eap=eap, Eap=Eap, c0=c0: e.tensor_tensor(
                            out=wap[:, c0:TB], in0=eap[:, c0:TB], in1=Eap[:, c0:TB], op=ALU.mult),
                            reads=[eb, Ebf], writes=[wbf])
            for c in range(2):
                P.op("act", lambda e, c=c: e.activation(out=yT4[:, c, :], in_=psum[6 + c][:], func=AF.Identity),
                     reads=[b_ps[6 + c]], writes=[b_yT4[c]])
            for s_ in range(NSUB):
                P.op("dve", lambda e, s_=s_: e.memset(xn[:, s_, 0:2], 0.0), writes=[b_xn[s_]] + b_xnh[s_])
            P.tag = "mix_gates"
            for d in range(NKC):
                slabg_, bslg_ = load_slab(l, f"m_gate{d}")
                if d % 4 == 0:
                    slabb, bslb = load_slab(l, f"m_br{d // 4}")
                for ni, n in enumerate((1, 2, 3, 0)):
                    psg, bpsg = next_ps()
                    for k in range(NKC):
                        o = (n * NKC + k) * 128
                        mm(psg[:], bpsg, slabg_[:, o:o + 128], xTb[:, k, :], [bslg_, b_xTb[k]], k == 0, k == NKC - 1)
                    gb = vcol(l, f"gate_b{n}", d)
                    gi = n % 2
                    P.op("act", lambda e, psg=psg, gi=gi, gb=gb: e.activation(out=S32[:, gi, :], in_=psg[:], func=AF.Sigmoid,
                                                                             bias=gb, scale=1.0),
                         reads=[bpsg, b_vecs], writes=[b_S32[gi]])
                    psb, bpsb = next_ps()
                    for kk in range(2):
                        o = (((d % 4) * 4 + n) * 2 + kk) * 128
                        mm(psb[:], bpsb, slabb[:, o:o + 128], yT4[:, 2 * n + kk, :], [bslb, b_yT4[2 * n + kk]], kk == 0, kk == 1)
                    ai = 4 + d % 2
                    if ni == 0:
                        P.op("dve", lambda e, psb=psb, gi=gi, ai=ai: e.tensor_tensor(out=S32[:, ai, :], in0=psb[:], in1=S32[:, gi, :],
                                                                                    op=ALU.mult),
                             reads=[bpsb, b_S32[gi]], writes=[b_S32[ai]])
                    else:
                        ti = 2 + n % 2
                        P.op("dve", lambda e, psb=psb, gi=gi, ti=ti: e.tensor_tensor(out=S32[:, ti, :], in0=psb[:], in1=S32[:, gi, :],
                                                                                    op=ALU.mult),
                             reads=[bpsb, b_S32[gi]], writes=[b_S32[ti]])
                        if ni < 3:
                            P.op("dve", lambda e, ti=ti, ai=ai: e.tensor_tensor(out=S32[:, ai, :], in0=S32[:, ai, :], in1=S32[:, ti, :],
                                                                               op=ALU.add),
                                 reads=[b_S32[ai], b_S32[ti]], writes=[b_S32[ai]])
                        else:
                            P.op("dve", lambda e, ti=ti, ai=ai, d=d: e.tensor_tensor(out=mergedT[:, d, :], in0=S32[:, ai, :],
                                                                                    in1=S32[:, ti, :], op=ALU.add),
                                 reads=[b_S32[ai], b_S32[ti]], writes=[b_merged[d]])
            P.tag = "mix_out"
            c_res = 1.0 / DN_ALPHA
            for d in range(NKC):
                if d % 4 == 0:
                    slabo, bslo = load_slab(l, f"m_out{d // 4}")
                ps, bps = next_ps()
                for k in range(NKC):
                    o = ((d % 4) * NKC + k) * 128
                    mm(ps[:], bps, slabo[:, o:o + 128], mergedT[:, k, :], [bslo, b_merged[k]], k == 0, k == NKC - 1)
                P.op("dve", lambda e, ps=ps, d=d: e.scalar_tensor_tensor(out=xT[:, d, :], in0=ps[:], scalar=c_res,
                                                                         in1=xT[:, d, :], op0=ALU.mult, op1=ALU.add),
                     reads=[bps, b_xT[d]], writes=[b_xT[d]])
            layer_norm_T(l, "ln_g1", "ln_b1")

        def names_of(ph):
            if ph == "f0":
                return [n for n, _ in SLABS if n.startswith("f0_")]
            if ph == "f1":
                return [n for n, _ in SLABS if n.startswith("f1_")]
            return [n for n, _ in SLABS if n.startswith("m_")]

        def prepass_names(l, names):
            for nm in names:
                for i, (ll, name, sz) in enumerate(pre_list):
                    if ll == l and name == nm:
                        pre_list.insert(0, pre_list.pop(i))
                        emit_prepass(1)
                        break

        phase_fn = {"f0": lambda l, blk: ffn_phase(l, 0, 0), "mix": mixer_phase, "f1": lambda l, blk: ffn_phase(l, 1, 2)}
        order = [ph for ph in ("f0", "mix", "f1") if ph in cfg.phases]
        seq = [(l, blk) for l in range(cfg.depth) for blk in range(NBLK)]

        def prefetch(l, blk):
            if l == 0:
                prefetch_x(blk)
            else:
                prefetch_xs(blk)

        b_xs = bufs("xs", NBLK)
        for idx, (l, blk) in enumerate(seq):
            P.tag = "io"
            if idx == 0:
                if order:
                    prepass_names(0, names_of(order[0])[:4])
                prefetch(l, blk)
                if "mix" in cfg.phases:
                    layer_setup(0)
                emit_prepass(sum(1 for (ll, _n, _s) in pre_list if ll == 0))
            if l > 0 and blk == 0:
                emit_prepass(10 ** 6)
            if l == 0:
                consume_x(blk)
            else:
                consume_xs(blk)
            for pi, ph in enumerate(order):
                P.tag = "io"
                if l == 0 and blk == 0:
                    prepass_names(0, names_of(ph))
                elif l == 0 and cfg.depth > 1:
                    emit_prepass(3)
                if pi == len(order) - 1 and ph != "mix" and idx + 1 < len(seq):
                    prefetch(*seq[idx + 1])
                phase_fn[ph](l, blk)
            if not (order and order[-1] != "mix") and idx + 1 < len(seq):
                prefetch(*seq[idx + 1])
            if l == cfg.depth - 1:
                store_block_to_out(blk)
            else:
                store_block_xs(blk)

        fin = sb("fin", [128, 1], F32)
        b_fin = Buf("fin")
        last_out_ops = [o for o in P.ops["pool"] if o.dsem in DS_XIO]
        tail = P.op("pool", lambda e: e.memset(fin[:], 0.0), writes=[b_fin])
        for q in range(4):
            lo = [o for o in last_out_ops if o.dsem == DS_XIO[q]]
            if lo:
                tail.deps.append(lo[-1])

        P.finalize()
        if getattr(cfg, "dbg", None) is not None:
            P.dbg = cfg.dbg
        if getattr(cfg, "pe_tags", None) is not None:
            cfg.pe_tags.extend(o.tag for o in P.ops["pe"])
        dsem_names = [DS_CONST, "cb", "cf"] + DS_RING + DS_PRE + DS_XIO + [DS_XS] + DS_PF
        esems = {e: es.enter_context(nc.semaphore(f"e_{e}")) for e in Prog.ENGS}
        dsems = {n: es.enter_context(nc.semaphore(f"d_{n}")) for n in dsem_names}
        block = es.enter_context(nc.Block())

        @block.tensor
        def _(e):
            P.emit("pe", e, esems, dsems)

        @block.scalar
        def _(e):
            P.emit("act", e, esems, dsems)

        @block.vector
        def _(e):
            P.emit("dve", e, esems, dsems)

        @block.gpsimd
        def _(e):
            P.emit("pool", e, esems, dsems)

        @block.sync
        def _(e):
            P.emit("sp", e, esems, dsems)

    return nc


_PROGRAM_CACHE = {}


def kernel(**inputs):
    inp = {k: np.asarray(v) for k, v in inputs.items()}
    x = inp["x"].astype(np.float32, copy=False)
    B = x.shape[0]
    W = pack_weights(inp)
    V = pack_vecs(inp)
    C = pack_consts(inp)
    cfg = Cfg()
    nc = build_program(cfg)
    in_maps = [dict(x=np.ascontiguousarray(x[b]), w=W, vecs=V, **C) for b in range(B)]
    res = run_bass_kernel_spmd(nc, in_maps, core_ids=list(range(B)))
    return np.stack([np.asarray(r["out"]) for r in res.results], axis=0).astype(np.float32)
```

```python
import math
from contextlib import ExitStack

import numpy as np

import concourse.bass as bass
import concourse.mybir as mybir
from concourse.bass_utils import run_bass_kernel_spmd

F32 = mybir.dt.float32
BF16 = mybir.dt.bfloat16
AF = mybir.ActivationFunctionType
ALU = mybir.AluOpType

D_MODEL = 1024
SEQ = 4096
DEPTH = 2
BW = 256
HEAD_DIM = 64
D_FF = 2816
CONV_W = 31
LN_EPS = 1e-5
DN_ALPHA = (2.0 * DEPTH) ** 0.25
POOL_WINDOWS = (2, 4, 8, 16)

TB = 512
NSUB = TB // 128
NKC = D_MODEL // 128
NFC = D_FF // 128
SLOT = 4096
NSLOT = 5


class Buf:
    __slots__ = ("name", "w", "r", "excl")

    def __init__(self, name, excl=False):
        self.name = name
        self.w = None
        self.r = {}
        self.excl = excl


class Op:
    __slots__ = ("eng", "fn", "deps", "signal", "sigval", "dsem", "key", "tag")


class Prog:
    ENGS = ("pe", "act", "dve", "pool", "sp")

    def __init__(self):
        self.ops = {e: [] for e in self.ENGS}
        self.dsem_count = {}
        self.last_dma = {}
        self.tag = ""

    def op(self, eng, fn, reads=(), writes=(), dsem=None):
        o = Op()
        o.eng = eng
        o.fn = fn
        o.signal = False
        o.sigval = None
        o.dsem = dsem
        o.tag = self.tag
        o.key = ("d", dsem) if dsem is not None else ("e", eng)
        deps = {}
        for b in reads:
            if b.w is not None:
                deps[id(b.w)] = b.w
            if b.excl:
                for r in b.r.values():
                    if r.key != o.key:
                        deps[id(r)] = r
        for b in writes:
            if b.w is not None:
                deps[id(b.w)] = b.w
            for r in b.r.values():
                deps[id(r)] = r
        if dsem is not None:
            pd = self.last_dma.get(dsem)
            if pd is not None:
                deps[id(pd)] = pd
            self.last_dma[dsem] = o
        dl = []
        for d in deps.values():
            if d is o:
                continue
            if d.dsem is None and d.eng == "pe" and eng == "pe" and dsem is None:
                continue
            d.signal = True
            dl.append(d)
        o.deps = dl
        for b in reads:
            b.r[o.key] = o
        for b in writes:
            b.w = o
            b.r = {}
        self.ops[eng].append(o)
        return o

    def finalize(self):
        cnt = {e: 0 for e in self.ENGS}
        for e in self.ENGS:
            for o in self.ops[e]:
                if o.dsem is not None:
                    c = self.dsem_count.get(o.dsem, 0) + 16
                    self.dsem_count[o.dsem] = c
                    o.sigval = c
                elif o.signal:
                    cnt[e] += 1
                    o.sigval = cnt[e]
        return cnt

    def emit(self, eng, handle, esems, dsems):
        seen = {}
        dbg = getattr(self, "dbg", None)
        for o in self.ops[eng]:
            if dbg is not None:
                dbg.append((eng, o.fn.__code__.co_firstlineno, [(d.key, d.sigval) for d in o.deps], o.key, o.sigval))
            need = {}
            for d in o.deps:
                k = d.key
                if d.sigval > need.get(k, 0):
                    need[k] = d.sigval
            for k, v in need.items():
                if v > seen.get(k, 0):
                    seen[k] = v
                    sem = dsems[k[1]] if k[0] == "d" else esems[k[1]]
                    handle.wait_ge(sem, v)
            ins = o.fn(handle)
            if o.dsem is not None:
                ins.then_inc(dsems[o.dsem], 16)
            elif o.signal:
                ins.then_inc(esems[eng], 1)


def _slab_list():
    sl = []
    for i in (0,):
        pass
    def ffn(i):
        for j2 in range(NFC // 2):
            sl.append((f"f{i}_in{j2}", 4096))
        for d in range(NKC):
            sl.append((f"f{i}_out{d}", NFC * 128))
    ffn(0)
    sl.append(("m_qk", 4096))
    sl.append(("m_tok", 4096))
    sl.append(("m_pool", 2048))
    sl.append(("m_poolw", 256))
    sl.append(("m_ua", 4096))
    sl.append(("m_g", 2048))
    for d in range(NKC):
        sl.append((f"m_gate{d}", 4096))
        if d % 4 == 0:
            sl.append((f"m_br{d // 4}", 4096))
    sl.append(("m_out0", 4096))
    sl.append(("m_out1", 4096))
    ffn(1)
    return sl


SLABS = _slab_list()
SLAB_OFF = {}
_o = 0
for _n, _sz in SLABS:
    SLAB_OFF[_n] = (_o, _sz)
    _o += _sz
LAYER_W = _o


def _kc(w, cols):
    k = w.shape[0] // 128
    nch = len(cols) // 128
    a = w[:, cols].reshape(k, 128, nch, 128)
    return a.transpose(1, 2, 0, 3)


def pack_weights(inp):
    W = np.empty((128, DEPTH * LAYER_W), np.float32)
    for l in range(DEPTH):
        def put(name, arr):
            off, sz = SLAB_OFF[name]
            a = np.ascontiguousarray(arr).reshape(128, -1)
            assert a.shape[1] == sz, (name, a.shape, sz)
            W[:, l * LAYER_W + off: l * LAYER_W + off + sz] = a
        for i in range(2):
            w_in = inp["ffn_w_in"][l, i]
            w4 = w_in.reshape(NKC, 128, 2, NFC, 128)
            for j2 in range(NFC // 2):
                a = w4[:, :, :, 2 * j2:2 * j2 + 2, :]
                put(f"f{i}_in{j2}", a.transpose(1, 3, 2, 0, 4))
            w_out = inp["ffn_w_out"][l, i].reshape(NFC, 128, NKC, 128)
            for d in range(NKC):
                put(f"f{i}_out{d}", w_out[:, :, d, :].transpose(1, 0, 2))
        mw = inp["mix_w_in"][l]
        put("m_qk", _kc(mw, np.arange(0, 512)))
        tokc = np.concatenate([np.arange(512, 768), np.arange(1024, 1280)])
        put("m_tok", mw[:, tokc].reshape(NKC, 128, 512).transpose(1, 0, 2))
        put("m_pool", mw[:, 1280:1536].reshape(NKC, 128, 256).transpose(1, 0, 2))
        pw = np.zeros((128, 2, 128), np.float32)
        for g in range(4):
            r = (g % 2) * 64
            pw[r:r + 64, g // 2, r:r + 64] = inp["pool_w"][l, g]
        put("m_poolw", pw)
        put("m_ua", _kc(mw, np.concatenate([np.arange(768, 1024), np.arange(1536, 1792)])))
        put("m_g", _kc(mw, np.arange(1792, 2048)))
        gw = inp["gate_w"][l].reshape(4, NKC, 128, NKC, 128)
        for d in range(NKC):
            put(f"m_gate{d}", gw[:, :, :, d, :].transpose(2, 0, 1, 3))
        bw = inp["branch_w"][l].reshape(4, 2, 128, NKC, 128)
        for h in range(2):
            put(f"m_br{h}", bw[:, :, :, 4 * h:4 * h + 4, :].transpose(2, 3, 0, 1, 4))
        ow = inp["out_w"][l].reshape(NKC, 128, NKC, 128)
        for h in range(2):
            put(f"m_out{h}", ow[:, :, 4 * h:4 * h + 4, :].transpose(1, 2, 0, 3))
    return W


VEC_COLS = {}
_c = 0
def _vc(name, n):
    global _c
    VEC_COLS[name] = _c
    _c += n
for _i in range(3):
    _vc(f"ln_g{_i}", 8); _vc(f"ln_b{_i}", 8)
for _n in range(4):
    _vc(f"gate_b{_n}", 8)
_vc("pool_scale", 2); _vc("conv_b", 2); _vc("conv_ln_g", 2); _vc("conv_ln_b", 2)
_vc("conv_w", 62)
NVEC = _c


def pack_vecs(inp):
    V = np.zeros((128, DEPTH * NVEC), np.float32)
    for l in range(DEPTH):
        def put(name, v):
            c0 = l * NVEC + VEC_COLS[name]
            a = np.asarray(v).reshape(-1, 128).T
            V[:, c0:c0 + a.shape[1]] = a
        for i in range(3):
            put(f"ln_g{i}", inp["ln_g"][l, i]); put(f"ln_b{i}", inp["ln_b"][l, i])
        for n in range(4):
            put(f"gate_b{n}", inp["gate_b"][l, n])
        put("pool_scale", inp["pool_scale"][l]); put("conv_b", inp["conv_b"][l])
        put("conv_ln_g", inp["conv_ln_g"][l]); put("conv_ln_b", inp["conv_ln_b"][l])
        cw = inp["conv_w"][l]
        c0 = l * NVEC + VEC_COLS["conv_w"]
        for c in range(2):
            V[:, c0 + c * 31: c0 + (c + 1) * 31] = cw[:, c * 128:(c + 1) * 128].T
    return V


CB_NTRI, CB_NUSTR, CB_TRILE = 0, 128, 256
CB_BAND, CB_BANDP, CB_BAND0 = 384, 896, 1408
CB_MASK = 1920
NCB = CB_MASK + 2048
CF_SGW, CF_SGB, CF_SGG, CF_SGBE = 0, 512, 1536, 1792
NCF = 2048


def pack_consts(inp=None):
    c = {}
    c["ident"] = np.eye(128, dtype=np.float32)
    j = np.arange(128)[:, None]
    t = np.arange(128)[None, :]
    cb = np.zeros((128, NCB), np.float32)
    cb[:, CB_NTRI:CB_NTRI + 128] = -1.0 * (j >= t)
    cb[:, CB_NUSTR:CB_NUSTR + 128] = -1.0 * (j < t)
    cb[:, CB_TRILE:CB_TRILE + 128] = (j <= t)
    for g, win in enumerate(POOL_WINDOWS):
        band = ((j <= t) & (j > t - win)).astype(np.float32) / win - (j == t)
        cnt = np.minimum(t + 1, win).astype(np.float32)
        band0 = ((j <= t) & (j > t - win)).astype(np.float32) / cnt - (j == t)
        bandp = ((j + 0 - 128 > t - win)).astype(np.float32) / win
        cb[:, CB_BAND + g * 128:CB_BAND + (g + 1) * 128] = band
        cb[:, CB_BANDP + g * 128:CB_BANDP + (g + 1) * 128] = bandp
        cb[:, CB_BAND0 + g * 128:CB_BAND0 + (g + 1) * 128] = band0
    col = np.arange(512)[None, :]
    for d in range(4):
        cb[:, CB_MASK + d * 512:CB_MASK + (d + 1) * 512] = (col > 128 * d + j)
    c["cb"] = cb
    if inp is not None:
        cf = np.zeros((128, DEPTH * NCF), np.float32)
        for l in range(DEPTH):
            o = l * NCF
            sgw = inp["sg_w"][l]
            cf[:, o + CF_SGW:o + CF_SGW + 512] = sgw.transpose(2, 0, 1).reshape(128, 512)
            sgb = inp["sg_b"][l]
            rep = np.empty((128, 2, 4, 128), np.float32)
            for cc in range(2):
                rep[0:64, cc] = sgb[2 * cc][None, None, :]
                rep[64:128, cc] = sgb[2 * cc + 1][None, None, :]
            cf[:, o + CF_SGB:o + CF_SGB + 1024] = rep.reshape(128, 1024)
            cf[:, o + CF_SGG:o + CF_SGG + 256] = inp["sg_ln_g"][l][None, :]
            cf[:, o + CF_SGBE:o + CF_SGBE + 256] = inp["sg_ln_b"][l][None, :]
        c["cf"] = cf
    return c


class Cfg:
    def __init__(self, seq=SEQ, depth=DEPTH, phases=("f0", "mix", "f1"), dbg=None):
        self.seq = seq
        self.depth = depth
        self.phases = phases
        self.nblk = seq // TB
        self.dbg = dbg


def build_program(cfg):
    nc = bass.Bass("TRN2", target_bir_lowering=False)
    S = cfg.seq
    NBLK = cfg.nblk
    x_in = nc.dram_tensor("x", [S, D_MODEL], F32, kind="ExternalInput").ap()
    w_in = nc.dram_tensor("w", [128, DEPTH * LAYER_W], F32, kind="ExternalInput").ap()
    vec_in = nc.dram_tensor("vecs", [128, DEPTH * NVEC], F32, kind="ExternalInput").ap()
    ident_in = nc.dram_tensor("ident", [128, 128], F32, kind="ExternalInput").ap()
    cb_in = nc.dram_tensor("cb", [128, NCB], F32, kind="ExternalInput").ap()
    cf_in = nc.dram_tensor("cf", [128, DEPTH * NCF], F32, kind="ExternalInput").ap()
    out = nc.dram_tensor("out", [S, D_MODEL], F32, kind="ExternalOutput").ap()
    wb = nc.dram_tensor("wb", [128, DEPTH * LAYER_W], BF16, kind="Internal").ap()
    xs = nc.dram_tensor("xs", [NBLK, 128, NKC, TB], F32, kind="Internal").ap()

    P = Prog()
    es = ExitStack()

    def sb(name, shape, dt):
        return es.enter_context(nc.sbuf_tensor("s_" + name, shape, dt))

    def bufs(name, n):
        return [Buf(f"{name}{i}") for i in range(n)]

    with es:
        ident = sb("ident", [128, 128], F32); b_ident = Buf("ident")
        vecs = sb("vecs", [128, DEPTH * NVEC], F32); b_vecs = Buf("vecs")
        ring = sb("ring", [128, NSLOT, SLOT], BF16); b_ring = bufs("ring", NSLOT)
        xT = sb("xT", [128, NKC, TB], F32); b_xT = bufs("xT", NKC)
        xTb = sb("xTb", [128, NKC, TB], BF16); b_xTb = bufs("xTb", NKC)
        xn = sb("xn", [128, NSUB, D_MODEL], F32); b_xn = bufs("xn", NSUB)
        gT = sb("gT", [128, NFC, TB], BF16); b_gT = bufs("gT", NFC)
        S32 = sb("S32", [128, 8, TB], F32); b_S32 = bufs("S32", 8)
        stt = sb("stt", [128, 4, 12], F32); b_stt = bufs("stt", 4)
        mv = sb("mv", [128, 4, 2], F32); b_mv = bufs("mv", 4)
        rs = sb("rs", [128, 4, 4], F32); b_rs = bufs("rs", 4); b_rs0 = bufs("rs0_", 4); b_rs1 = bufs("rs1_", 4)
        psum = [es.enter_context(nc.psum_tensor(f"ps{i}", [128, 512], F32)) for i in range(8)]
        b_ps = [Buf(f"ps{i}", excl=True) for i in range(8)]
        ps_rr = [0]

        def next_ps(subset=range(8)):
            subset = list(subset)
            i = subset[ps_rr[0] % len(subset)]
            ps_rr[0] += 1
            return psum[i], b_ps[i]

        DS_CONST = "const"
        DS_RING = [f"ring{i}" for i in range(NSLOT)]
        DS_PRE = [f"pre{i}" for i in range(8)]
        DS_XIO = ["xio0", "xio1", "xio2", "xio3"]
        DS_XS = "xs"

        P.op("pool", lambda e: e.dma_start(out=ident[:], in_=ident_in[:]), writes=[b_ident], dsem=DS_CONST)
        P.op("pool", lambda e: e.dma_start(out=vecs[:], in_=vec_in[:]), writes=[b_vecs], dsem=DS_CONST)

        b_wb = {}
        npre = [0]
        pre_list = [(l, name, sz) for l in range(cfg.depth) for (name, sz) in SLABS]

        def emit_prepass(n):
            if getattr(cfg, "no_prepass", False):
                return
            for _ in range(n):
                if not pre_list:
                    return
                l, name, sz = pre_list.pop(0)
                off = l * LAYER_W + SLAB_OFF[name][0]
                b = Buf(f"wb{l}{name}")
                b_wb[(l, name)] = b
                P.op("pool",
                     lambda e, off=off, sz=sz: e.dma_start(out=wb[:, off:off + sz], in_=w_in[:, off:off + sz],
                                                            max_dma_last_dim=8192),
                     writes=[b], dsem=DS_PRE[npre[0] % 8])
                npre[0] += 1


        slab_ctr = [0]

        def load_slab(l, name):
            off, sz = SLAB_OFF[name]
            off += l * LAYER_W
            k = slab_ctr[0] % NSLOT
            slab_ctr[0] += 1
            P.op("sp", lambda e, k=k, off=off, sz=sz: e.dma_start(out=ring[:, k, 0:sz], in_=wb[:, off:off + sz]),
                 reads=[b_wb[(l, name)]], writes=[b_ring[k]], dsem=DS_RING[k])
            return ring[:, k, :], b_ring[k]

        def vcol(l, name, j):
            c = l * NVEC + VEC_COLS[name] + j
            return vecs[:, c:c + 1]

        def mm(ps, bps, lhsT, rhs, rd, start, stop, skip=False):
            P.op("pe", lambda e: e.matmul(ps, lhsT=lhsT, rhs=rhs, start=start, stop=stop, skip_group_check=skip),
                 reads=rd, writes=[bps])

        def tr(ps, bps, in_, rd):
            P.op("pe", lambda e: e.transpose(ps, in_, ident[:]), reads=rd + [b_ident], writes=[bps])

        ln_ctr = [0]

        def layer_norm_T(l, gname, bname):
            P.tag = "ln"
            pzs = {}
            for s in range(NSUB + 1):
                if s < NSUB:
                    q = s % 2
                    pz = [next_ps(), next_ps()]
                    pzs[s] = pz
                    for d in range(NKC):
                        ps, bps = pz[d // 4]
                        tr(ps[:, (d % 4) * 128:(d % 4 + 1) * 128], bps, xT[:, d, s * 128:(s + 1) * 128], [b_xT[d]])
                    for h in range(2):
                        ps, bps = pz[h]
                        P.op("dve", lambda e, ps=ps, q=q, h=h: e.bn_stats(out=stt[:, q, h * 6:(h + 1) * 6], in_=ps[:]),
                             reads=[bps], writes=[b_stt[q]])
                    P.op("dve", lambda e, q=q: e.bn_aggr(out=mv[:, q, :], in_=stt[:, q, :]),
                         reads=[b_stt[q]], writes=[b_mv[q]])
                    P.op("act", lambda e, q=q: e.activation(out=rs[:, q, 0:1], in_=mv[:, q, 1:2], func=AF.Sqrt,
                                                            bias=eps_t[:, 0:1], scale=1.0),
                         reads=[b_mv[q], b_eps], writes=[b_rs0[q]])
                if s >= 1:
                    sp_ = s - 1
                    q = sp_ % 2
                    P.op("dve", lambda e, q=q: e.reciprocal(out=rs[:, q, 1:2], in_=rs[:, q, 0:1]),
                         reads=[b_rs0[q]], writes=[b_rs1[q]])
                    P.op("dve", lambda e, q=q: e.scalar_tensor_tensor(out=rs[:, q, 2:3], in0=mv[:, q, 0:1], scalar=-1.0,
                                                                       in1=rs[:, q, 1:2], op0=ALU.mult, op1=ALU.mult),
                         reads=[b_rs1[q], b_mv[q]], writes=[b_rs[q]])
                    for h in range(2):
                        ps, bps = pzs[sp_][h]
                        P.op("act", lambda e, ps=ps, q=q, h=h, sp_=sp_: e.activation(
                            out=xn[:, sp_, h * 512:(h + 1) * 512], in_=ps[:], func=AF.Identity,
                            bias=rs[:, q, 2:3], scale=rs[:, q, 1:2]),
                            reads=[bps, b_rs[q], b_rs1[q]], writes=[b_xn[sp_]])
            for d in range(NKC):
                ps, bps = next_ps()
                for s in range(NSUB):
                    tr(ps[:, s * 128:(s + 1) * 128], bps, xn[:, s, d * 128:(d + 1) * 128], [b_xn[s]])
                g = vcol(l, gname, d)
                b = vcol(l, bname, d)
                if d % 2 == 0:
                    P.op("act", lambda e, ps=ps, d=d, g=g, b=b: e.activation(out=xT[:, d, :], in_=ps[:], func=AF.Identity,
                                                                             bias=b, scale=g),
                         reads=[bps, b_vecs], writes=[b_xT[d]])
                    P.op("dve", lambda e, d=d: e.tensor_copy(out=xTb[:, d, :], in_=xT[:, d, :]),
                         reads=[b_xT[d]], writes=[b_xTb[d]])
                else:
                    P.op("dve", lambda e, ps=ps, d=d, g=g, b=b: e.tensor_scalar(out=xT[:, d, :], in0=ps[:], scalar1=g,
                                                                               scalar2=b, op0=ALU.mult, op1=ALU.add),
                         reads=[bps, b_vecs], writes=[b_xT[d]])
                    P.op("act", lambda e, d=d: e.activation(out=xTb[:, d, :], in_=xT[:, d, :], func=AF.Identity),
                         reads=[b_xT[d]], writes=[b_xTb[d]])

        def ffn_phase(l, i, ln_idx):
            c_res = 0.5 / DN_ALPHA
            P.tag = "ffn_in"
            b_sil = [Buf("silh0"), Buf("silh1")]
            P.op("dve", lambda e: e.memset(xn[:, 0, 0:2], 0.0), writes=[b_xn[0]] + b_sil)
            for j2 in range(NFC // 2):
                slab, bsl = load_slab(l, f"f{i}_in{j2}")
                for jj in range(2):
                    j = 2 * j2 + jj
                    pa = next_ps()
                    pu = next_ps()
                    for t, (ps, bps) in enumerate((pa, pu)):
                        for k in range(NKC):
                            o = ((jj * 2 + t) * NKC + k) * 128
                            mm(ps[:], bps, slab[:, o:o + 128], xTb[:, k, :], [bsl, b_xTb[k]], k == 0, k == NKC - 1)
                    q = j % 2
                    P.op("act", lambda e, ps=pa[0], q=q: e.activation(out=xn[:, 0, q * 512:(q + 1) * 512], in_=ps[:], func=AF.Silu),
                         reads=[pa[1]], writes=[b_sil[q]])
                    P.op("dve", lambda e, ps=pu[0], q=q, j=j: e.tensor_tensor(out=gT[:, j, :], in0=ps[:],
                                                                             in1=xn[:, 0, q * 512:(q + 1) * 512], op=ALU.mult),
                         reads=[pu[1], b_sil[q]], writes=[b_gT[j]])
            P.op("dve", lambda e: e.memset(xn[:, 0, 0:2], 0.0), writes=[b_xn[0]] + b_sil)
            P.tag = "ffn_out"
            P.op("act", lambda e: e.activation(out=acts[:, 0:1], in_=one1[:, 0:1], func=AF.Sqrt), reads=[b_eps], writes=[b_acts])
            for d in range(NKC):
                slab, bsl = load_slab(l, f"f{i}_out{d}")
                ps, bps = next_ps()
                for j in range(NFC):
                    mm(ps[:], bps, slab[:, j * 128:(j + 1) * 128], gT[:, j, :], [bsl, b_gT[j]], j == 0, j == NFC - 1)
                P.op("dve", lambda e, ps=ps, d=d: e.scalar_tensor_tensor(out=xT[:, d, :], in0=ps[:], scalar=c_res,
                                                                         in1=xT[:, d, :], op0=ALU.mult, op1=ALU.add),
                     reads=[bps, b_xT[d]], writes=[b_xT[d]])
            layer_norm_T(l, f"ln_g{ln_idx}", f"ln_b{ln_idx}")

        eps_t = sb("eps_t", [128, 1], F32); b_eps = Buf("eps")
        P.op("dve", lambda e: e.memset(eps_t[:], LN_EPS / (DN_ALPHA * DN_ALPHA)), writes=[b_eps])
        eps1 = sb("eps1", [128, 1], F32)
        one1 = sb("one1", [128, 1], F32)
        acts = sb("acts", [128, 1], F32); b_acts = Buf("acts")
        mhalf = sb("mhalf", [128, 1], F32)
        P.op("dve", lambda e: e.memset(mhalf[:], -0.5), writes=[b_eps])
        P.op("dve", lambda e: e.memset(eps1[:], LN_EPS), writes=[b_eps])
        P.op("dve", lambda e: e.memset(one1[:], 1.0), writes=[b_eps])

        DS_PF = ["pf0", "pf1", "pf2", "pf3"]

        def prefetch_x(blk):
            for s_ in range(NSUB):
                r0 = blk * TB + s_ * 128
                P.op("pool", lambda e, s_=s_, r0=r0: e.dma_start(
                    out=S32[:, 2 * s_:2 * s_ + 2, :], in_=x_in[r0:r0 + 128, :].rearrange("p (a b) -> p a b", a=2)),
                    writes=[b_S32[2 * s_], b_S32[2 * s_ + 1]], dsem=DS_PF[s_])

        def consume_x(blk):
            P.tag = "io"
            for d in range(NKC):
                ps, bps = next_ps()
                for s_ in range(NSUB):
                    sl = 2 * s_ + d // 4
                    tr(ps[:, s_ * 128:(s_ + 1) * 128], bps, S32[:, sl, (d % 4) * 128:(d % 4 + 1) * 128], [b_S32[sl]])
                if d % 2 == 0:
                    P.op("act", lambda e, ps=ps, d=d: e.activation(out=xT[:, d, :], in_=ps[:], func=AF.Identity),
                         reads=[bps], writes=[b_xT[d]])
                    P.op("dve", lambda e, d=d: e.tensor_copy(out=xTb[:, d, :], in_=xT[:, d, :]),
                         reads=[b_xT[d]], writes=[b_xTb[d]])
                else:
                    P.op("dve", lambda e, ps=ps, d=d: e.tensor_copy(out=xT[:, d, :], in_=ps[:]),
                         reads=[bps], writes=[b_xT[d]])
                    P.op("act", lambda e, d=d: e.activation(out=xTb[:, d, :], in_=xT[:, d, :], func=AF.Identity),
                         reads=[b_xT[d]], writes=[b_xTb[d]])

        def prefetch_xs(blk):
            P.op("pool", lambda e, blk=blk: e.dma_start(out=S32[:], in_=xs[blk]),
                 reads=[b_xs[blk]], writes=b_S32, dsem=DS_PF[0])

        def consume_xs(blk):
            P.tag = "io"
            for d in range(NKC):
                if d % 2 == 0:
                    P.op("act", lambda e, d=d: e.activation(out=xT[:, d, :], in_=S32[:, d, :], func=AF.Identity),
                         reads=[b_S32[d]], writes=[b_xT[d]])
                    P.op("dve", lambda e, d=d: e.tensor_copy(out=xTb[:, d, :], in_=S32[:, d, :]),
                         reads=[b_S32[d]], writes=[b_xTb[d]])
                else:
                    P.op("dve", lambda e, d=d: e.tensor_copy(out=xT[:, d, :], in_=S32[:, d, :]),
                         reads=[b_S32[d]], writes=[b_xT[d]])
                    P.op("act", lambda e, d=d: e.activation(out=xTb[:, d, :], in_=S32[:, d, :], func=AF.Identity),
                         reads=[b_S32[d]], writes=[b_xTb[d]])

        def store_block_to_out(blk):
            P.tag = "io"
            for s in range(NSUB):
                q = s
                for h in range(2):
                    ps, bps = next_ps()
                    for dd in range(4):
                        d = h * 4 + dd
                        tr(ps[:, dd * 128:(dd + 1) * 128], bps, xT[:, d, s * 128:(s + 1) * 128], [b_xT[d]])
                    eng = "act" if h == 0 else "dve"
                    if eng == "act":
                        P.op("act", lambda e, ps=ps, q=q, h=h: e.activation(out=xn[:, q, h * 512:(h + 1) * 512],
                                                                            in_=ps[:], func=AF.Identity),
                             reads=[bps], writes=[b_xn[q]])
                    else:
                        P.op("dve", lambda e, ps=ps, q=q, h=h: e.tensor_copy(out=xn[:, q, h * 512:(h + 1) * 512],
                                                                             in_=ps[:]),
                             reads=[bps], writes=[b_xn[q]])
                r0 = blk * TB + s * 128
                P.op("pool", lambda e, q=q, r0=r0: e.dma_start(out=out[r0:r0 + 128, :], in_=xn[:, q, :]),
                     reads=[b_xn[q]], dsem=DS_XIO[q])

        def store_block_xs(blk):
            P.op("pool", lambda e, blk=blk: e.dma_start(out=xs[blk], in_=xT[:]),
                 reads=b_xT, writes=[b_xs[blk]], dsem=DS_XS)

        cbt = sb("cbt", [128, NCB], BF16); b_cb = Buf("cb")
        cft = sb("cft", [128, NCF], F32); b_cf = Buf("cf")
        wsm = sb("wsm", [128, 4, 128], BF16); b_wsm = Buf("wsm")
        kTc = sb("kTc", [128, 2, S], BF16); b_kTc = Buf("kTc")
        Vc = sb("Vc", [128, S // 128, BW], BF16); b_Vc = Buf("Vc")
        B16 = sb("B16", [128, 8, TB], BF16); b_B16 = bufs("B16", 8)
        yT4 = sb("yT4", [128, 8, TB], BF16); b_yT4 = bufs("yT4", 8)
        qT = sb("qT", [128, 2, TB], BF16); b_qT = bufs("qT", 2)
        vln = sb("vln", [128, NSUB, BW], BF16); b_vln = bufs("vln", NSUB)
        ptok = sb("ptok", [128, NSUB + 1, BW], BF16); b_ptok = bufs("ptok", NSUB + 1)
        ybuf = sb("ybuf", [128, 2, 30 + TB], BF16); b_ybuf = bufs("ybuf", 2)
        dg = sb("dg", [128, 4, 128], BF16); b_dg = bufs("dg", 4)
        P.op("pool", lambda e: e.dma_start(out=cbt[:], in_=cb_in[:], max_dma_last_dim=8192), writes=[b_cb], dsem="cb")
        ntri = cbt[:, CB_NTRI:CB_NTRI + 128]
        nustr = cbt[:, CB_NUSTR:CB_NUSTR + 128]
        mergedT = gT
        b_merged = b_gT
        dg_ctr = [0]
        st_ctr = [0]

        def tok_stats_a(src_ap, src_buf, eps_ap):
            q = st_ctr[0] % 4
            st_ctr[0] += 1
            P.op("dve", lambda e: e.bn_stats(out=stt[:, q, 0:6], in_=src_ap), reads=[src_buf], writes=[b_stt[q]])
            P.op("dve", lambda e: e.bn_aggr(out=mv[:, q, :], in_=stt[:, q, 0:6]), reads=[b_stt[q]], writes=[b_mv[q]])
            P.op("act", lambda e: e.activation(out=rs[:, q, 0:1], in_=mv[:, q, 1:2], func=AF.Sqrt, bias=eps_ap, scale=1.0),
                 reads=[b_mv[q], b_eps], writes=[b_rs0[q]])
            return q

        def tok_stats_b(q):
            P.op("dve", lambda e: e.reciprocal(out=rs[:, q, 1:2], in_=rs[:, q, 0:1]), reads=[b_rs0[q]], writes=[b_rs1[q]])
            P.op("dve", lambda e: e.scalar_tensor_tensor(out=rs[:, q, 2:3], in0=mv[:, q, 0:1], scalar=-1.0,
                                                          in1=rs[:, q, 1:2], op0=ALU.mult, op1=ALU.mult),
                 reads=[b_rs1[q], b_mv[q]], writes=[b_rs[q]])
            return rs[:, q, 1:2], rs[:, q, 2:3], [b_rs[q], b_rs1[q]]

        def layer_setup(l):
            o = l * NCF
            P.op("pool", lambda e: e.dma_start(out=cft[:], in_=cf_in[:, o:o + NCF]), writes=[b_cf], dsem="cf")
            for g in range(4):
                P.op("dve", lambda e, g=g: e.tensor_tensor(out=wsm[:, g, :], in0=cft[:, CF_SGW + g * 128:CF_SGW + (g + 1) * 128],
                                                          in1=cbt[:, CB_TRILE:CB_TRILE + 128], op=ALU.mult),
                     reads=[b_cf, b_cb], writes=[b_wsm])
            for c in range(2):
                P.op("dve", lambda e, c=c: e.memset(ybuf[:, c, 0:30], 0.0), writes=[b_ybuf[c]])

        def mixer_phase(l, blk):
            t0 = blk * TB
            nt0 = blk * NSUB
            if blk == 0 and l > 0:
                layer_setup(l)
            P.tag = "mix_qk"
            slab, bsl = load_slab(l, "m_qk")
            for cc in range(4):
                ps, bps = next_ps()
                for k in range(NKC):
                    o = (cc * NKC + k) * 128
                    mm(ps[:], bps, slab[:, o:o + 128], xTb[:, k, :], [bsl, b_xTb[k]], k == 0, k == NKC - 1)
                if cc < 2:
                    P.op("act", lambda e, ps=ps, cc=cc: e.activation(out=qT[:, cc, :], in_=ps[:], func=AF.Identity),
                         reads=[bps], writes=[b_qT[cc]])
                else:
                    P.op("dve", lambda e, ps=ps, cc=cc: e.tensor_copy(out=kTc[:, cc - 2, t0:t0 + TB], in_=ps[:]),
                         reads=[bps], writes=[b_kTc])
            P.tag = "mix_tok"
            slab, bsl = load_slab(l, "m_tok")
            slabp, bslp = load_slab(l, "m_pool")
            slabw, bslw = load_slab(l, "m_poolw")
            for s in range(NSUB):
                ps, bps = next_ps()
                for k in range(NKC):
                    mm(ps[:], bps, xTb[:, k, s * 128:(s + 1) * 128], slab[:, k * 512:(k + 1) * 512],
                       [bsl, b_xTb[k]], k == 0, k == NKC - 1)
                P.op("dve", lambda e, ps=ps, s=s: e.tensor_copy(out=Vc[:, nt0 + s, :], in_=ps[:, 0:256]),
                     reads=[bps], writes=[b_Vc])
                P.op("act", lambda e, ps=ps, s=s: e.activation(out=xn[:, s, 256:512], in_=ps[:, 256:512],
                                                               func=AF.Gelu_apprx_tanh),
                     reads=[bps], writes=[b_xn[s]])
                ps2, bps2 = next_ps()
                for k in range(NKC):
                    mm(ps2[:, 0:256], bps2, xTb[:, k, s * 128:(s + 1) * 128], slabp[:, k * 256:(k + 1) * 256],
                       [bslp, b_xTb[k]], k == 0, k == NKC - 1)
                P.op("act", lambda e, ps2=ps2, s=s: e.activation(out=ptok[:, s + 1, :], in_=ps2[:, 0:256], func=AF.Identity),
                     reads=[bps2], writes=[b_ptok[s + 1]])
            P.tag = "mix_sgln"
            qs = [tok_stats_a(xn[:, s, 256:512], b_xn[s], eps1[:, 0:1]) for s in range(NSUB)]
            for s in range(NSUB):
                rstd, nmr, brs = tok_stats_b(qs[s])
                P.op("act", lambda e, s=s, rstd=rstd, nmr=nmr: e.activation(out=xn[:, s, 512:768], in_=xn[:, s, 256:512],
                                                                            func=AF.Identity, bias=nmr, scale=rstd),
                     reads=[b_xn[s]] + brs, writes=[b_xn[s]])
                P.op("dve", lambda e, s=s: e.tensor_tensor(out=xn[:, s, 512:768], in0=xn[:, s, 512:768],
                                                           in1=cft[:, CF_SGG:CF_SGG + 256], op=ALU.mult),
                     reads=[b_xn[s], b_cf], writes=[b_xn[s]])
                P.op("dve", lambda e, s=s: e.tensor_tensor(out=vln[:, s, :], in0=xn[:, s, 512:768],
                                                           in1=cft[:, CF_SGBE:CF_SGBE + 256], op=ALU.add),
                     reads=[b_xn[s], b_cf], writes=[b_vln[s]])
            P.tag = "mix_u"
            slab, bsl = load_slab(l, "m_ua")
            slabg, bslg = load_slab(l, "m_g")
            for c in range(2):
                ps, bps = next_ps()
                for k in range(NKC):
                    o = (c * NKC + k) * 128
                    mm(ps[:], bps, slab[:, o:o + 128], xTb[:, k, :], [bsl, b_xTb[k]], k == 0, k == NKC - 1)
                P.op("act", lambda e, ps=ps, c=c: e.activation(out=S32[:, 2 + c, :], in_=ps[:], func=AF.Gelu_apprx_tanh),
                     reads=[bps], writes=[b_S32[2 + c]])
            P.tag = "mix_conv"
            for c in range(2):
                psa, bpsa = next_ps()
                for k in range(NKC):
                    o = ((2 + c) * NKC + k) * 128
                    mm(psa[:], bpsa, slab[:, o:o + 128], xTb[:, k, :], [bsl, b_xTb[k]], k == 0, k == NKC - 1)
                psg, bpsg = next_ps()
                for k in range(NKC):
                    o = (c * NKC + k) * 128
                    mm(psg[:], bpsg, slabg[:, o:o + 128], xTb[:, k, :], [bslg, b_xTb[k]], k == 0, k == NKC - 1)
                P.op("act", lambda e, psg=psg: e.activation(out=S32[:, 0, :], in_=psg[:], func=AF.Sigmoid),
                     reads=[bpsg], writes=[b_S32[0]])
                P.op("dve", lambda e, psa=psa, c=c: e.tensor_tensor(out=ybuf[:, c, 30:30 + TB], in0=psa[:], in1=S32[:, 0, :],
                                                                    op=ALU.mult),
                     reads=[bpsa, b_S32[0]], writes=[b_ybuf[c]])
            for c in range(2):
                ps, bps = next_ps()
                for k in range(CONV_W):
                    q = dg_ctr[0] % 4
                    dg_ctr[0] += 1
                    wc = vcol(l, "conv_w", c * 31 + k)
                    P.op("dve", lambda e, q=q, wc=wc: e.tensor_scalar(out=dg[:, q, :], in0=ident[:], scalar1=wc, scalar2=None,
                                                                      op0=ALU.mult),
                         reads=[b_ident, b_vecs], writes=[b_dg[q]])
                    mm(ps[:], bps, dg[:, q, :], ybuf[:, c, k:k + TB], [b_dg[q], b_ybuf[c]], k == 0, k == CONV_W - 1)
                cb_ = vcol(l, "conv_b", c)
                P.op("act", lambda e, ps=ps, c=c, cb_=cb_: e.activation(out=S32[:, 6 + c, :], in_=ps[:], func=AF.Identity,
                                                                         bias=cb_, scale=1.0),
                     reads=[bps, b_vecs], writes=[b_S32[6 + c]])
                P.op("dve", lambda e, c=c: e.tensor_copy(out=ybuf[:, c, 0:30], in_=ybuf[:, c, TB:TB + 30]),
                     reads=[b_ybuf[c]], writes=[b_ybuf[c]])
            cps = []
            for s in range(NSUB):
                ps, bps = next_ps(range(4, 8))
                for c in range(2):
                    tr(ps[:, c * 128:(c + 1) * 128], bps, S32[:, 6 + c, s * 128:(s + 1) * 128], [b_S32[6 + c]])
                cps.append((ps, bps, tok_stats_a(ps[:, 0:256], bps, eps1[:, 0:1])))
            for s in range(NSUB):
                ps, bps, q = cps[s]
                rstd, nmr, brs = tok_stats_b(q)
                P.op("act", lambda e, ps=ps, s=s, rstd=rstd, nmr=nmr: e.activation(out=xn[:, s, 0:256], in_=ps[:, 0:256],
                                                                                   func=AF.Identity, bias=nmr, scale=rstd),
                     reads=[bps] + brs, writes=[b_xn[s]])
            P.tag = "mix_pool"
            for c in range(2):
                for par in range(2):
                    g = 2 * c + par
                    r0 = par * 64
                    ps, bps = next_ps(range(4))
                    for s in range(NSUB):
                        osl = ps[:, s * 128:(s + 1) * 128]
                        cur = ptok[:, s + 1, c * 128:(c + 1) * 128]
                        if blk == 0 and s == 0:
                            mm(osl, bps, cur, cbt[:, CB_BAND0 + g * 128:CB_BAND0 + (g + 1) * 128],
                               [b_ptok[s + 1], b_cb], True, True)
                        else:
                            mm(osl, bps, cur, cbt[:, CB_BAND + g * 128:CB_BAND + (g + 1) * 128],
                               [b_ptok[s + 1], b_cb], True, False)
                            mm(osl, bps, ptok[:, s, c * 128:(c + 1) * 128],
                               cbt[:, CB_BANDP + g * 128:CB_BANDP + (g + 1) * 128], [b_ptok[s], b_cb], False, True)
                    P.op("act", lambda e, ps=ps, c=c, r0=r0: e.activation(out=B16[r0:r0 + 64, 4 + c, :], in_=ps[r0:r0 + 64, :],
                                                                          func=AF.Identity),
                         reads=[bps], writes=[b_B16[4 + c]])
            P.op("dve", lambda e: e.tensor_copy(out=ptok[:, 0, :], in_=ptok[:, NSUB, :]), reads=[b_ptok[NSUB]],
                 writes=[b_ptok[0]])
            for c in range(2):
                ps, bps = next_ps(range(4))
                mm(ps[:], bps, slabw[:, c * 128:(c + 1) * 128], B16[:, 4 + c, :], [bslw, b_B16[4 + c]], True, True)
                sc_ = vcol(l, "pool_scale", c)
                P.op("act", lambda e, ps=ps, c=c, sc_=sc_: e.activation(out=yT4[:, 4 + c, :], in_=ps[:], func=AF.Identity,
                                                                        scale=sc_),
                     reads=[bps, b_vecs], writes=[b_yT4[4 + c]])
            P.tag = "mix_sgmix"
            for c in range(2):
                for par in range(2):
                    g = 2 * c + par
                    r0 = par * 64
                    ps, bps = next_ps(range(4))
                    for s in range(NSUB):
                        mm(ps[:, s * 128:(s + 1) * 128], bps, vln[:, s, c * 128:(c + 1) * 128], wsm[:, g, :],
                           [b_vln[s], b_wsm], True, True)
                    P.op("dve", lambda e, ps=ps, c=c, r0=r0: e.tensor_tensor(
                        out=S32[r0:r0 + 64, 4, :], in0=ps[r0:r0 + 64, :],
                        in1=cft[r0:r0 + 64, CF_SGB + c * 512:CF_SGB + (c + 1) * 512], op=ALU.add),
                        reads=[bps, b_cf], writes=[b_S32[4]])
                    P.op("dve", lambda e, c=c, r0=r0: e.tensor_tensor(
                        out=yT4[r0:r0 + 64, 2 + c, :], in0=S32[r0:r0 + 64, 4, :], in1=S32[r0:r0 + 64, 2 + c, :], op=ALU.mult),
                        reads=[b_S32[4], b_S32[2 + c]], writes=[b_yT4[2 + c]])
            P.tag = "mix_conv"
            for c in range(2):
                ps, bps = next_ps(range(4))
                for s in range(NSUB):
                    tr(ps[:, s * 128:(s + 1) * 128], bps, xn[:, s, c * 128:(c + 1) * 128], [b_xn[s]])
                g_ = vcol(l, "conv_ln_g", c)
                b_ = vcol(l, "conv_ln_b", c)
                P.op("act", lambda e, ps=ps, c=c, g_=g_, b_=b_: e.activation(out=yT4[:, 6 + c, :], in_=ps[:], func=AF.Silu,
                                                                             bias=b_, scale=g_),
                     reads=[bps, b_vecs], writes=[b_yT4[6 + c]])
            P.tag = "mix_attn"
            npair = nt0 + NSUB
            a_hi = npair - 1
            b_xnh = [[Buf(f"xnh{s_}{hh}") for hh in range(2)] for s_ in range(NSUB)]
            for s_ in range(NSUB):
                P.op("dve", lambda e, s_=s_: e.memset(xn[:, s_, 0:2], 0.0), writes=[b_xn[s_]] + b_xnh[s_])
            ebuf = []
            for h in range(4):
                ebuf.append([(xn[:, h, 0:512], b_xnh[h][0]), (xn[:, h, 512:1024], b_xnh[h][1]), (S32[:, h, :], b_S32[h])])
            Ebuf = [(S32[:, 4 + h, :], b_S32[4 + h]) for h in range(4)]
            spbuf = [[(gT[:, 8 + h * 3 + i, :], b_gT[8 + h * 3 + i]) for i in range(3)] for h in range(4)]
            wbuf = [[(B16[:, h * 2 + i, :], b_B16[h * 2 + i]) for i in range(2)] for h in range(4)]
            actr = [0]

            def pinfo(p):
                a = a_hi - p
                dd = a - nt0
                c0 = 128 * max(dd, 0)
                return a, dd, c0

            for t in range(npair + 2):
                if t < npair:
                    a, dd, c0 = pinfo(t)
                    pend_ln = None
                    for h in range(4):
                        c = h // 2
                        r0 = (h % 2) * 64
                        psA, bA = psum[actr[0] % 2], b_ps[actr[0] % 2]
                        actr[0] += 1
                        mm(psA[:, c0:TB], bA, kTc[r0:r0 + 64, c, a * 128:(a + 1) * 128], qT[r0:r0 + 64, c, c0:TB],
                           [b_kTc, b_qT[c]], True, True)
                        eap, eb = ebuf[h][t % 3]
                        P.op("act", lambda e, psA=psA, eap=eap, c0=c0: e.activation(out=eap[:, c0:TB], in_=psA[:, c0:TB],
                                                                                  func=AF.Exp, scale=0.125),
                             reads=[bA], writes=[eb])
                        if dd >= 0:
                            P.op("dve", lambda e, eap=eap, dd=dd, c0=c0: e.tensor_tensor(
                                out=eap[:, c0:TB], in0=eap[:, c0:TB], in1=cbt[:, CB_MASK + dd * 512 + c0:CB_MASK + (dd + 1) * 512],
                                op=ALU.mult), reads=[eb, b_cb], writes=[eb])
                        if pend_ln is not None:
                            pend_ln()
                        sap, sbf = spbuf[h][t % 3]

                        def _ln(eap=eap, eb=eb, sap=sap, sbf=sbf, c0=c0):
                            P.op("act", lambda e: e.activation(out=sap[:, c0:TB], in_=eap[:, c0:TB], func=AF.Ln, bias=1.0, scale=1.0),
                                 reads=[eb], writes=[sbf])
                        pend_ln = _ln
                    pend_ln()
                if t >= 2:
                    p = t - 2
                    a, dd, c0 = pinfo(p)
                    for h in range(4):
                        c = h // 2
                        r0 = (h % 2) * 64
                        sap, sbf = spbuf[h][p % 3]
                        wap, wbf = wbuf[h][p % 2]
                        if p < npair - 1:
                            mm(psum[2 + h][:, c0:TB], b_ps[2 + h], nustr, sap[:, c0:TB], [b_cb, sbf], False, True, skip=True)
                        mm(psum[6 + c][r0:r0 + 64, c0:TB], b_ps[6 + c], Vc[:, a, h * 64:(h + 1) * 64], wap[:, c0:TB],
                           [b_Vc, wbf], p == 0, True, skip=(p > 0))
                if 1 <= t <= npair:
                    p = t - 1
                    a, dd, c0 = pinfo(p)
                    for h in range(4):
                        sap, sbf = spbuf[h][p % 3]
                        eap, eb = ebuf[h][p % 3]
                        Eap, Ebf = Ebuf[h]
                        wap, wbf = wbuf[h][p % 2]
                        psB, bB = psum[2 + h], b_ps[2 + h]
                        mm(psB[:, c0:TB], bB, ntri, sap[:, c0:TB], [b_cb, sbf], p == 0, True, skip=(p > 0))
                        P.op("act", lambda e, psB=psB, Eap=Eap, c0=c0: e.activation(out=Eap[:, c0:TB], in_=psB[:, c0:TB], func=AF.Exp),
                             reads=[bB], writes=[Ebf])
                        P.op("dve", lambda e, wap=wap, eap=eap, Eap=Eap, c0=c0: e.tensor_tensor(
                            out=wap[:, c0:TB], in0=eap[:, c0:TB], in1=Eap[:, c0:TB], op=ALU.mult),
                            reads=[eb, Ebf], writes=[wbf])
            for c in range(2):
                P.op("act", lambda e, c=c: e.activation(out=yT4[:, c, :], in_=psum[6 + c][:], func=AF.Identity),
                     reads=[b_ps[6 + c]], writes=[b_yT4[c]])
            for s_ in range(NSUB):
                P.op("dve", lambda e, s_=s_: e.memset(xn[:, s_, 0:2], 0.0), writes=[b_xn[s_]] + b_xnh[s_])
            P.tag = "mix_gates"
            for d in range(NKC):
                slabg_, bslg_ = load_slab(l, f"m_gate{d}")
                if d % 4 == 0:
                    slabb, bslb = load_slab(l, f"m_br{d // 4}")
                for ni, n in enumerate((1, 2, 3, 0)):
                    psg, bpsg = next_ps()
                    for k in range(NKC):
                        o = (n * NKC + k) * 128
                        mm(psg[:], bpsg, slabg_[:, o:o + 128], xTb[:, k, :], [bslg_, b_xTb[k]], k == 0, k == NKC - 1)
                    gb = vcol(l, f"gate_b{n}", d)
                    gi = n % 2
                    P.op("act", lambda e, psg=psg, gi=gi, gb=gb: e.activation(out=S32[:, gi, :], in_=psg[:], func=AF.Sigmoid,
                                                                             bias=gb, scale=1.0),
                         reads=[bpsg, b_vecs], writes=[b_S32[gi]])
                    psb, bpsb = next_ps()
                    for kk in range(2):
                        o = (((d % 4) * 4 + n) * 2 + kk) * 128
                        mm(psb[:], bpsb, slabb[:, o:o + 128], yT4[:, 2 * n + kk, :], [bslb, b_yT4[2 * n + kk]], kk == 0, kk == 1)
                    ai = 4 + d % 2
                    if ni == 0:
                        P.op("dve", lambda e, psb=psb, gi=gi, ai=ai: e.tensor_tensor(out=S32[:, ai, :], in0=psb[:], in1=S32[:, gi, :],
                                                                                    op=ALU.mult),
                             reads=[bpsb, b_S32[gi]], writes=[b_S32[ai]])
                    else:
                        ti = 2 + n % 2
                        P.op("dve", lambda e, psb=psb, gi=gi, ti=ti: e.tensor_tensor(out=S32[:, ti, :], in0=psb[:], in1=S32[:, gi, :],
                                                                                    op=ALU.mult),
                             reads=[bpsb, b_S32[gi]], writes=[b_S32[ti]])
                        if ni < 3:
                            P.op("dve", lambda e, ti=ti, ai=ai: e.tensor_tensor(out=S32[:, ai, :], in0=S32[:, ai, :], in1=S32[:, ti, :],
                                                                               op=ALU.add),
                                 reads=[b_S32[ai], b_S32[ti]], writes=[b_S32[ai]])
                        else:
                            P.op("dve", lambda e, ti=ti, ai=ai, d=d: e.tensor_tensor(out=mergedT[:, d, :], in0=S32[:, ai, :],
                                                                                    in1=S32[:, ti, :], op=ALU.add),
                                 reads=[b_S32[ai], b_S32[ti]], writes=[b_merged[d]])
            P.tag = "mix_out"
            P.op("act", lambda e: e.activation(out=acts[:, 0:1], in_=one1[:, 0:1], func=AF.Sqrt), reads=[b_eps], writes=[b_acts])
            c_res = 1.0 / DN_ALPHA
            for d in range(NKC):
                if d % 4 == 0:
                    slabo, bslo = load_slab(l, f"m_out{d // 4}")
                ps, bps = next_ps()
                for k in range(NKC):
                    o = ((d % 4) * NKC + k) * 128
                    mm(ps[:], bps, slabo[:, o:o + 128], mergedT[:, k, :], [bslo, b_merged[k]], k == 0, k == NKC - 1)
                P.op("dve", lambda e, ps=ps, d=d: e.scalar_tensor_tensor(out=xT[:, d, :], in0=ps[:], scalar=c_res,
                                                                         in1=xT[:, d, :], op0=ALU.mult, op1=ALU.add),
                     reads=[bps, b_xT[d]], writes=[b_xT[d]])
            layer_norm_T(l, "ln_g1", "ln_b1")

        def names_of(ph):
            if ph == "f0":
                return [n for n, _ in SLABS if n.startswith("f0_")]
            if ph == "f1":
                return [n for n, _ in SLABS if n.startswith("f1_")]
            return [n for n, _ in SLABS if n.startswith("m_")]

        def prepass_names(l, names):
            for nm in names:
                for i, (ll, name, sz) in enumerate(pre_list):
                    if ll == l and name == nm:
                        pre_list.insert(0, pre_list.pop(i))
                        emit_prepass(1)
                        break

        phase_fn = {"f0": lambda l, blk: ffn_phase(l, 0, 0), "mix": mixer_phase, "f1": lambda l, blk: ffn_phase(l, 1, 2)}
        order = [ph for ph in ("f0", "mix", "f1") if ph in cfg.phases]
        seq = [(l, blk) for l in range(cfg.depth) for blk in range(NBLK)]

        def prefetch(l, blk):
            if l == 0:
                prefetch_x(blk)
            else:
                prefetch_xs(blk)

        b_xs = bufs("xs", NBLK)
        for idx, (l, blk) in enumerate(seq):
            P.tag = "io"
            if idx == 0:
                if order:
                    prepass_names(0, names_of(order[0])[:4])
                prefetch(l, blk)
                if "mix" in cfg.phases:
                    layer_setup(0)
                emit_prepass(sum(1 for (ll, _n, _s) in pre_list if ll == 0))
            if l > 0 and blk == 0:
                emit_prepass(10 ** 6)
            if l == 0:
                consume_x(blk)
            else:
                consume_xs(blk)
            for pi, ph in enumerate(order):
                P.tag = "io"
                if l == 0 and blk == 0:
                    prepass_names(0, names_of(ph))
                elif l == 0 and cfg.depth > 1:
                    emit_prepass(3)
                if pi == len(order) - 1 and ph != "mix" and idx + 1 < len(seq):
                    prefetch(*seq[idx + 1])
                phase_fn[ph](l, blk)
            if not (order and order[-1] != "mix") and idx + 1 < len(seq):
                prefetch(*seq[idx + 1])
            if l == cfg.depth - 1:
                store_block_to_out(blk)
            else:
                store_block_xs(blk)

        fin = sb("fin", [128, 1], F32)
        b_fin = Buf("fin")
        last_out_ops = [o for o in P.ops["pool"] if o.dsem in DS_XIO]
        tail = P.op("pool", lambda e: e.memset(fin[:], 0.0), writes=[b_fin])
        for q in range(4):
            lo = [o for o in last_out_ops if o.dsem == DS_XIO[q]]
            if lo:
                tail.deps.append(lo[-1])

        P.finalize()
        if getattr(cfg, "dbg", None) is not None:
            P.dbg = cfg.dbg
        if getattr(cfg, "pe_tags", None) is not None:
            cfg.pe_tags.extend(o.tag for o in P.ops["pe"])
        dsem_names = [DS_CONST, "cb", "cf"] + DS_RING + DS_PRE + DS_XIO + [DS_XS] + DS_PF
        esems = {e: es.enter_context(nc.semaphore(f"e_{e}")) for e in Prog.ENGS}
        dsems = {n: es.enter_context(nc.semaphore(f"d_{n}")) for n in dsem_names}
        block = es.enter_context(nc.Block())

        @block.tensor
        def _(e):
            P.emit("pe", e, esems, dsems)

        @block.scalar
        def _(e):
            P.emit("act", e, esems, dsems)

        @block.vector
        def _(e):
            P.emit("dve", e, esems, dsems)

        @block.gpsimd
        def _(e):
            P.emit("pool", e, esems, dsems)

        @block.sync
        def _(e):
            P.emit("sp", e, esems, dsems)

    return nc


_PROGRAM_CACHE = {}


def kernel(**inputs):
    inp = {k: np.asarray(v) for k, v in inputs.items()}
    x = inp["x"].astype(np.float32, copy=False)
    B = x.shape[0]
    W = pack_weights(inp)
    V = pack_vecs(inp)
    C = pack_consts(inp)
    cfg = Cfg()
    nc = build_program(cfg)
    in_maps = [dict(x=np.ascontiguousarray(x[b]), w=W, vecs=V, **C) for b in range(B)]
    res = run_bass_kernel_spmd(nc, in_maps, core_ids=list(range(B)))
    return np.stack([np.asarray(r["out"]) for r in res.results], axis=0).astype(np.float32)
```

```python
import math
from contextlib import ExitStack

import numpy as np

import concourse.bass as bass
import concourse.mybir as mybir
from concourse.bass_utils import run_bass_kernel_spmd

F32 = mybir.dt.float32
BF16 = mybir.dt.bfloat16
AF = mybir.ActivationFunctionType
ALU = mybir.AluOpType

D_MODEL = 1024
SEQ = 4096
DEPTH = 2
BW = 256
HEAD_DIM = 64
D_FF = 2816
CONV_W = 31
LN_EPS = 1e-5
DN_ALPHA = (2.0 * DEPTH) ** 0.25
POOL_WINDOWS = (2, 4, 8, 16)

TB = 512
NSUB = TB // 128
NKC = D_MODEL // 128
NFC = D_FF // 128
SLOT = 4096
NSLOT = 5


class Buf:
    __slots__ = ("name", "w", "r", "excl")

    def __init__(self, name, excl=False):
        self.name = name
        self.w = None
        self.r = {}
        self.excl = excl


class Op:
    __slots__ = ("eng", "fn", "deps", "signal", "sigval", "dsem", "key", "tag")


class Prog:
    ENGS = ("pe", "act", "dve", "pool", "sp")

    def __init__(self):
        self.ops = {e: [] for e in self.ENGS}
        self.dsem_count = {}
        self.last_dma = {}
        self.tag = ""

    def op(self, eng, fn, reads=(), writes=(), dsem=None):
        o = Op()
        o.eng = eng
        o.fn = fn
        o.signal = False
        o.sigval = None
        o.dsem = dsem
        o.tag = self.tag
        o.key = ("d", dsem) if dsem is not None else ("e", eng)
        deps = {}
        for b in reads:
            if b.w is not None:
                deps[id(b.w)] = b.w
            if b.excl:
                for r in b.r.values():
                    if r.key != o.key:
                        deps[id(r)] = r
        for b in writes:
            if b.w is not None:
                deps[id(b.w)] = b.w
            for r in b.r.values():
                deps[id(r)] = r
        if dsem is not None:
            pd = self.last_dma.get(dsem)
            if pd is not None:
                deps[id(pd)] = pd
            self.last_dma[dsem] = o
        dl = []
        for d in deps.values():
            if d is o:
                continue
            if d.dsem is None and d.eng == "pe" and eng == "pe" and dsem is None:
                continue
            d.signal = True
            dl.append(d)
        o.deps = dl
        for b in reads:
            b.r[o.key] = o
        for b in writes:
            b.w = o
            b.r = {}
        self.ops[eng].append(o)
        return o

    def finalize(self):
        cnt = {e: 0 for e in self.ENGS}
        for e in self.ENGS:
            for o in self.ops[e]:
                if o.dsem is not None:
                    c = self.dsem_count.get(o.dsem, 0) + 16
                    self.dsem_count[o.dsem] = c
                    o.sigval = c
                elif o.signal:
                    cnt[e] += 1
                    o.sigval = cnt[e]
        return cnt

    def emit(self, eng, handle, esems, dsems):
        seen = {}
        dbg = getattr(self, "dbg", None)
        for o in self.ops[eng]:
            if dbg is not None:
                dbg.append((eng, o.fn.__code__.co_firstlineno, [(d.key, d.sigval) for d in o.deps], o.key, o.sigval))
            need = {}
            for d in o.deps:
                k = d.key
                if d.sigval > need.get(k, 0):
                    need[k] = d.sigval
            for k, v in need.items():
                if v > seen.get(k, 0):
                    seen[k] = v
                    sem = dsems[k[1]] if k[0] == "d" else esems[k[1]]
                    handle.wait_ge(sem, v)
            ins = o.fn(handle)
            if o.dsem is not None:
                ins.then_inc(dsems[o.dsem], 16)
            elif o.signal:
                ins.then_inc(esems[eng], 1)


def _slab_list():
    sl = []
    for i in (0,):
        pass
    def ffn(i):
        for j2 in range(NFC // 2):
            sl.append((f"f{i}_in{j2}", 4096))
        for d in range(NKC):
            sl.append((f"f{i}_out{d}", NFC * 128))
    ffn(0)
    sl.append(("m_qk", 4096))
    sl.append(("m_tok", 4096))
    sl.append(("m_pool", 2048))
    sl.append(("m_poolw", 256))
    sl.append(("m_ua", 4096))
    sl.append(("m_g", 2048))
    for d in range(NKC):
        sl.append((f"m_gate{d}", 4096))
        if d % 4 == 0:
            sl.append((f"m_br{d // 4}", 4096))
    sl.append(("m_out0", 4096))
    sl.append(("m_out1", 4096))
    ffn(1)
    return sl


SLABS = _slab_list()
SLAB_OFF = {}
_o = 0
for _n, _sz in SLABS:
    SLAB_OFF[_n] = (_o, _sz)
    _o += _sz
LAYER_W = _o


def _kc(w, cols):
    k = w.shape[0] // 128
    nch = len(cols) // 128
    a = w[:, cols].reshape(k, 128, nch, 128)
    return a.transpose(1, 2, 0, 3)


def pack_weights(inp):
    W = np.empty((128, DEPTH * LAYER_W), np.float32)
    for l in range(DEPTH):
        def put(name, arr):
            off, sz = SLAB_OFF[name]
            a = np.ascontiguousarray(arr).reshape(128, -1)
            assert a.shape[1] == sz, (name, a.shape, sz)
            W[:, l * LAYER_W + off: l * LAYER_W + off + sz] = a
        for i in range(2):
            w_in = inp["ffn_w_in"][l, i]
            w4 = w_in.reshape(NKC, 128, 2, NFC, 128)
            for j2 in range(NFC // 2):
                a = w4[:, :, :, 2 * j2:2 * j2 + 2, :]
                put(f"f{i}_in{j2}", a.transpose(1, 3, 2, 0, 4))
            w_out = inp["ffn_w_out"][l, i].reshape(NFC, 128, NKC, 128)
            for d in range(NKC):
                put(f"f{i}_out{d}", w_out[:, :, d, :].transpose(1, 0, 2))
        mw = inp["mix_w_in"][l]
        put("m_qk", _kc(mw, np.arange(0, 512)))
        tokc = np.concatenate([np.arange(512, 768), np.arange(1024, 1280)])
        put("m_tok", mw[:, tokc].reshape(NKC, 128, 512).transpose(1, 0, 2))
        put("m_pool", mw[:, 1280:1536].reshape(NKC, 128, 256).transpose(1, 0, 2))
        pw = np.zeros((128, 2, 128), np.float32)
        for g in range(4):
            r = (g % 2) * 64
            pw[r:r + 64, g // 2, r:r + 64] = inp["pool_w"][l, g]
        put("m_poolw", pw)
        put("m_ua", _kc(mw, np.concatenate([np.arange(768, 1024), np.arange(1536, 1792)])))
        put("m_g", _kc(mw, np.arange(1792, 2048)))
        gw = inp["gate_w"][l].reshape(4, NKC, 128, NKC, 128)
        for d in range(NKC):
            put(f"m_gate{d}", gw[:, :, :, d, :].transpose(2, 0, 1, 3))
        bw = inp["branch_w"][l].reshape(4, 2, 128, NKC, 128)
        for h in range(2):
            put(f"m_br{h}", bw[:, :, :, 4 * h:4 * h + 4, :].transpose(2, 3, 0, 1, 4))
        ow = inp["out_w"][l].reshape(NKC, 128, NKC, 128)
        for h in range(2):
            put(f"m_out{h}", ow[:, :, 4 * h:4 * h + 4, :].transpose(1, 2, 0, 3))
    return W


VEC_COLS = {}
_c = 0
def _vc(name, n):
    global _c
    VEC_COLS[name] = _c
    _c += n
for _i in range(3):
    _vc(f"ln_g{_i}", 8); _vc(f"ln_b{_i}", 8)
for _n in range(4):
    _vc(f"gate_b{_n}", 8)
_vc("pool_scale", 2); _vc("conv_b", 2); _vc("conv_ln_g", 2); _vc("conv_ln_b", 2)
_vc("conv_w", 62)
NVEC = _c


def pack_vecs(inp):
    V = np.zeros((128, DEPTH * NVEC), np.float32)
    for l in range(DEPTH):
        def put(name, v):
            c0 = l * NVEC + VEC_COLS[name]
            a = np.asarray(v).reshape(-1, 128).T
            V[:, c0:c0 + a.shape[1]] = a
        for i in range(3):
            put(f"ln_g{i}", inp["ln_g"][l, i]); put(f"ln_b{i}", inp["ln_b"][l, i])
        for n in range(4):
            put(f"gate_b{n}", inp["gate_b"][l, n])
        put("pool_scale", inp["pool_scale"][l]); put("conv_b", inp["conv_b"][l])
        put("conv_ln_g", inp["conv_ln_g"][l]); put("conv_ln_b", inp["conv_ln_b"][l])
        cw = inp["conv_w"][l]
        c0 = l * NVEC + VEC_COLS["conv_w"]
        for c in range(2):
            V[:, c0 + c * 31: c0 + (c + 1) * 31] = cw[:, c * 128:(c + 1) * 128].T
    return V


CB_NTRI, CB_NUSTR, CB_TRILE = 0, 128, 256
CB_BAND, CB_BANDP, CB_BAND0 = 384, 896, 1408
CB_MASK = 1920
NCB = CB_MASK + 2048
CF_SGW, CF_SGB, CF_SGG, CF_SGBE = 0, 512, 1536, 1792
NCF = 2048


def pack_consts(inp=None):
    c = {}
    c["ident"] = np.eye(128, dtype=np.float32)
    j = np.arange(128)[:, None]
    t = np.arange(128)[None, :]
    cb = np.zeros((128, NCB), np.float32)
    cb[:, CB_NTRI:CB_NTRI + 128] = -1.0 * (j >= t)
    cb[:, CB_NUSTR:CB_NUSTR + 128] = -1.0 * (j < t)
    cb[:, CB_TRILE:CB_TRILE + 128] = (j <= t)
    for g, win in enumerate(POOL_WINDOWS):
        band = ((j <= t) & (j > t - win)).astype(np.float32) / win - (j == t)
        cnt = np.minimum(t + 1, win).astype(np.float32)
        band0 = ((j <= t) & (j > t - win)).astype(np.float32) / cnt - (j == t)
        bandp = ((j + 0 - 128 > t - win)).astype(np.float32) / win
        cb[:, CB_BAND + g * 128:CB_BAND + (g + 1) * 128] = band
        cb[:, CB_BANDP + g * 128:CB_BANDP + (g + 1) * 128] = bandp
        cb[:, CB_BAND0 + g * 128:CB_BAND0 + (g + 1) * 128] = band0
    col = np.arange(512)[None, :]
    for d in range(4):
        cb[:, CB_MASK + d * 512:CB_MASK + (d + 1) * 512] = (col > 128 * d + j)
    c["cb"] = cb
    if inp is not None:
        cf = np.zeros((128, DEPTH * NCF), np.float32)
        for l in range(DEPTH):
            o = l * NCF
            sgw = inp["sg_w"][l]
            cf[:, o + CF_SGW:o + CF_SGW + 512] = sgw.transpose(2, 0, 1).reshape(128, 512)
            sgb = inp["sg_b"][l]
            rep = np.empty((128, 2, 4, 128), np.float32)
            for cc in range(2):
                rep[0:64, cc] = sgb[2 * cc][None, None, :]
                rep[64:128, cc] = sgb[2 * cc + 1][None, None, :]
            cf[:, o + CF_SGB:o + CF_SGB + 1024] = rep.reshape(128, 1024)
            cf[:, o + CF_SGG:o + CF_SGG + 256] = inp["sg_ln_g"][l][None, :]
            cf[:, o + CF_SGBE:o + CF_SGBE + 256] = inp["sg_ln_b"][l][None, :]
        c["cf"] = cf
    return c


class Cfg:
    def __init__(self, seq=SEQ, depth=DEPTH, phases=("f0", "mix", "f1"), dbg=None):
        self.seq = seq
        self.depth = depth
        self.phases = phases
        self.nblk = seq // TB
        self.dbg = dbg


def build_program(cfg):
    nc = bass.Bass("TRN2", target_bir_lowering=False)
    S = cfg.seq
    NBLK = cfg.nblk
    x_in = nc.dram_tensor("x", [S, D_MODEL], F32, kind="ExternalInput").ap()
    w_in = nc.dram_tensor("w", [128, DEPTH * LAYER_W], F32, kind="ExternalInput").ap()
    vec_in = nc.dram_tensor("vecs", [128, DEPTH * NVEC], F32, kind="ExternalInput").ap()
    ident_in = nc.dram_tensor("ident", [128, 128], F32, kind="ExternalInput").ap()
    cb_in = nc.dram_tensor("cb", [128, NCB], F32, kind="ExternalInput").ap()
    cf_in = nc.dram_tensor("cf", [128, DEPTH * NCF], F32, kind="ExternalInput").ap()
    out = nc.dram_tensor("out", [S, D_MODEL], F32, kind="ExternalOutput").ap()
    wb = nc.dram_tensor("wb", [128, DEPTH * LAYER_W], BF16, kind="Internal").ap()
    xs = nc.dram_tensor("xs", [NBLK, 128, NKC, TB], F32, kind="Internal").ap()

    P = Prog()
    es = ExitStack()

    def sb(name, shape, dt):
        return es.enter_context(nc.sbuf_tensor("s_" + name, shape, dt))

    def bufs(name, n):
        return [Buf(f"{name}{i}") for i in range(n)]

    with es:
        ident = sb("ident", [128, 128], F32); b_ident = Buf("ident")
        vecs = sb("vecs", [128, DEPTH * NVEC], F32); b_vecs = Buf("vecs")
        ring = sb("ring", [128, NSLOT, SLOT], BF16); b_ring = bufs("ring", NSLOT)
        xT = sb("xT", [128, NKC, TB], F32); b_xT = bufs("xT", NKC)
        xTb = sb("xTb", [128, NKC, TB], BF16); b_xTb = bufs("xTb", NKC)
        xn = sb("xn", [128, NSUB, D_MODEL], F32); b_xn = bufs("xn", NSUB)
        gT = sb("gT", [128, NFC, TB], BF16); b_gT = bufs("gT", NFC)
        S32 = sb("S32", [128, 8, TB], F32); b_S32 = bufs("S32", 8)
        stt = sb("stt", [128, 4, 12], F32); b_stt = bufs("stt", 4)
        mv = sb("mv", [128, 4, 2], F32); b_mv = bufs("mv", 4)
        rs = sb("rs", [128, 4, 4], F32); b_rs = bufs("rs", 4); b_rs0 = bufs("rs0_", 4); b_rs1 = bufs("rs1_", 4)
        psum = [es.enter_context(nc.psum_tensor(f"ps{i}", [128, 512], F32)) for i in range(8)]
        b_ps = [Buf(f"ps{i}", excl=True) for i in range(8)]
        ps_rr = [0]

        def next_ps(subset=range(8)):
            subset = list(subset)
            i = subset[ps_rr[0] % len(subset)]
            ps_rr[0] += 1
            return psum[i], b_ps[i]

        DS_CONST = "const"
        DS_RING = [f"ring{i}" for i in range(NSLOT)]
        DS_PRE = [f"pre{i}" for i in range(8)]
        DS_XIO = ["xio0", "xio1", "xio2", "xio3"]
        DS_XS = "xs"

        P.op("pool", lambda e: e.dma_start(out=ident[:], in_=ident_in[:]), writes=[b_ident], dsem=DS_CONST)
        P.op("pool", lambda e: e.dma_start(out=vecs[:], in_=vec_in[:]), writes=[b_vecs], dsem=DS_CONST)

        b_wb = {}
        npre = [0]
        pre_list = [(l, name, sz) for l in range(cfg.depth) for (name, sz) in SLABS]

        def emit_prepass(n):
            if getattr(cfg, "no_prepass", False):
                return
            for _ in range(n):
                if not pre_list:
                    return
                l, name, sz = pre_list.pop(0)
                off = l * LAYER_W + SLAB_OFF[name][0]
                b = Buf(f"wb{l}{name}")
                b_wb[(l, name)] = b
                P.op("pool",
                     lambda e, off=off, sz=sz: e.dma_start(out=wb[:, off:off + sz], in_=w_in[:, off:off + sz],
                                                            max_dma_last_dim=8192),
                     writes=[b], dsem=DS_PRE[npre[0] % 8])
                npre[0] += 1


        slab_ctr = [0]

        def load_slab(l, name):
            off, sz = SLAB_OFF[name]
            off += l * LAYER_W
            k = slab_ctr[0] % NSLOT
            slab_ctr[0] += 1
            P.op("sp", lambda e, k=k, off=off, sz=sz: e.dma_start(out=ring[:, k, 0:sz], in_=wb[:, off:off + sz]),
                 reads=[b_wb[(l, name)]], writes=[b_ring[k]], dsem=DS_RING[k])
            return ring[:, k, :], b_ring[k]

        def vcol(l, name, j):
            c = l * NVEC + VEC_COLS[name] + j
            return vecs[:, c:c + 1]

        def mm(ps, bps, lhsT, rhs, rd, start, stop, skip=False):
            P.op("pe", lambda e: e.matmul(ps, lhsT=lhsT, rhs=rhs, start=start, stop=stop, skip_group_check=skip),
                 reads=rd, writes=[bps])

        def tr(ps, bps, in_, rd):
            P.op("pe", lambda e: e.transpose(ps, in_, ident[:]), reads=rd + [b_ident], writes=[bps])

        ln_ctr = [0]

        def layer_norm_T(l, gname, bname):
            P.tag = "ln"
            pzs = {}
            for s in range(NSUB + 1):
                if s < NSUB:
                    q = s % 2
                    pz = [next_ps(), next_ps()]
                    pzs[s] = pz
                    for d in range(NKC):
                        ps, bps = pz[d // 4]
                        tr(ps[:, (d % 4) * 128:(d % 4 + 1) * 128], bps, xT[:, d, s * 128:(s + 1) * 128], [b_xT[d]])
                    for h in range(2):
                        ps, bps = pz[h]
                        P.op("dve", lambda e, ps=ps, q=q, h=h: e.bn_stats(out=stt[:, q, h * 6:(h + 1) * 6], in_=ps[:]),
                             reads=[bps], writes=[b_stt[q]])
                    P.op("dve", lambda e, q=q: e.bn_aggr(out=mv[:, q, :], in_=stt[:, q, :]),
                         reads=[b_stt[q]], writes=[b_mv[q]])
                    P.op("act", lambda e, q=q: e.activation(out=rs[:, q, 0:1], in_=mv[:, q, 1:2], func=AF.Sqrt,
                                                            bias=eps_t[:, 0:1], scale=1.0),
                         reads=[b_mv[q], b_eps], writes=[b_rs0[q]])
                if s >= 1:
                    sp_ = s - 1
                    q = sp_ % 2
                    P.op("dve", lambda e, q=q: e.reciprocal(out=rs[:, q, 1:2], in_=rs[:, q, 0:1]),
                         reads=[b_rs0[q]], writes=[b_rs1[q]])
                    P.op("dve", lambda e, q=q: e.scalar_tensor_tensor(out=rs[:, q, 2:3], in0=mv[:, q, 0:1], scalar=-1.0,
                                                                       in1=rs[:, q, 1:2], op0=ALU.mult, op1=ALU.mult),
                         reads=[b_rs1[q], b_mv[q]], writes=[b_rs[q]])
                    for h in range(2):
                        ps, bps = pzs[sp_][h]
                        P.op("act", lambda e, ps=ps, q=q, h=h, sp_=sp_: e.activation(
                            out=xn[:, sp_, h * 512:(h + 1) * 512], in_=ps[:], func=AF.Identity,
                            bias=rs[:, q, 2:3], scale=rs[:, q, 1:2]),
                            reads=[bps, b_rs[q], b_rs1[q]], writes=[b_xn[sp_]])
            tails = []
            for d in range(NKC):
                ps, bps = next_ps()
                for s in range(NSUB):
                    tr(ps[:, s * 128:(s + 1) * 128], bps, xn[:, s, d * 128:(d + 1) * 128], [b_xn[s]])
                g = vcol(l, gname, d)
                b = vcol(l, bname, d)
                tails.append((ps, bps, g, b))
                if d % 2 == 0:
                    P.op("act", lambda e, ps=ps, d=d, g=g, b=b: e.activation(out=xTb[:, d, :], in_=ps[:], func=AF.Identity,
                                                                             bias=b, scale=g),
                         reads=[bps, b_vecs], writes=[b_xTb[d]])
                else:
                    P.op("dve", lambda e, ps=ps, d=d, g=g, b=b: e.tensor_scalar(out=xTb[:, d, :], in0=ps[:], scalar1=g,
                                                                               scalar2=b, op0=ALU.mult, op1=ALU.add),
                         reads=[bps, b_vecs], writes=[b_xTb[d]])
            for d in range(NKC):
                ps, bps, g, b = tails[d]
                if d % 2 == 0:
                    P.op("act", lambda e, ps=ps, d=d, g=g, b=b: e.activation(out=xT[:, d, :], in_=ps[:], func=AF.Identity,
                                                                             bias=b, scale=g),
                         reads=[bps, b_vecs], writes=[b_xT[d]])
                else:
                    P.op("dve", lambda e, ps=ps, d=d, g=g, b=b: e.tensor_scalar(out=xT[:, d, :], in0=ps[:], scalar1=g,
                                                                               scalar2=b, op0=ALU.mult, op1=ALU.add),
                         reads=[bps, b_vecs], writes=[b_xT[d]])

        def ffn_phase(l, i, ln_idx):
            c_res = 0.5 / DN_ALPHA
            P.tag = "ffn_in"
            b_sil = [Buf("silh0"), Buf("silh1")]
            P.op("dve", lambda e: e.memset(xn[:, 0, 0:2], 0.0), writes=[b_xn[0]] + b_sil)
            for j2 in range(NFC // 2):
                slab, bsl = load_slab(l, f"f{i}_in{j2}")
                for jj in range(2):
                    j = 2 * j2 + jj
                    pa = next_ps()
                    pu = next_ps()
                    for t, (ps, bps) in enumerate((pa, pu)):
                        for k in range(NKC):
                            o = ((jj * 2 + t) * NKC + k) * 128
                            mm(ps[:], bps, slab[:, o:o + 128], xTb[:, k, :], [bsl, b_xTb[k]], k == 0, k == NKC - 1)
                    q = j % 2
                    P.op("act", lambda e, ps=pa[0], q=q: e.activation(out=xn[:, 0, q * 512:(q + 1) * 512], in_=ps[:], func=AF.Silu),
                         reads=[pa[1]], writes=[b_sil[q]])
                    P.op("dve", lambda e, ps=pu[0], q=q, j=j: e.tensor_tensor(out=gT[:, j, :], in0=ps[:],
                                                                             in1=xn[:, 0, q * 512:(q + 1) * 512], op=ALU.mult),
                         reads=[pu[1], b_sil[q]], writes=[b_gT[j]])
            P.op("dve", lambda e: e.memset(xn[:, 0, 0:2], 0.0), writes=[b_xn[0]] + b_sil)
            P.tag = "ffn_out"
            P.op("act", lambda e: e.activation(out=acts[:, 0:1], in_=one1[:, 0:1], func=AF.Sqrt), reads=[b_eps], writes=[b_acts])
            for d in range(NKC):
                slab, bsl = load_slab(l, f"f{i}_out{d}")
                ps, bps = next_ps()
                for j in range(NFC):
                    mm(ps[:], bps, slab[:, j * 128:(j + 1) * 128], gT[:, j, :], [bsl, b_gT[j]], j == 0, j == NFC - 1)
                P.op("dve", lambda e, ps=ps, d=d: e.scalar_tensor_tensor(out=xT[:, d, :], in0=ps[:], scalar=c_res,
                                                                         in1=xT[:, d, :], op0=ALU.mult, op1=ALU.add),
                     reads=[bps, b_xT[d]], writes=[b_xT[d]])
            layer_norm_T(l, f"ln_g{ln_idx}", f"ln_b{ln_idx}")

        eps_t = sb("eps_t", [128, 1], F32); b_eps = Buf("eps")
        P.op("dve", lambda e: e.memset(eps_t[:], LN_EPS / (DN_ALPHA * DN_ALPHA)), writes=[b_eps])
        eps1 = sb("eps1", [128, 1], F32)
        one1 = sb("one1", [128, 1], F32)
        acts = sb("acts", [128, 1], F32); b_acts = Buf("acts")
        mhalf = sb("mhalf", [128, 1], F32)
        P.op("dve", lambda e: e.memset(mhalf[:], -0.5), writes=[b_eps])
        P.op("dve", lambda e: e.memset(eps1[:], LN_EPS), writes=[b_eps])
        P.op("dve", lambda e: e.memset(one1[:], 1.0), writes=[b_eps])

        DS_PF = ["pf0", "pf1", "pf2", "pf3"]

        def prefetch_x(blk):
            for s_ in range(NSUB):
                r0 = blk * TB + s_ * 128
                P.op("pool", lambda e, s_=s_, r0=r0: e.dma_start(
                    out=S32[:, 2 * s_:2 * s_ + 2, :], in_=x_in[r0:r0 + 128, :].rearrange("p (a b) -> p a b", a=2)),
                    writes=[b_S32[2 * s_], b_S32[2 * s_ + 1]], dsem=DS_PF[s_])

        def consume_x(blk):
            P.tag = "io"
            tails = []
            for d in range(NKC):
                ps, bps = next_ps()
                for s_ in range(NSUB):
                    sl = 2 * s_ + d // 4
                    tr(ps[:, s_ * 128:(s_ + 1) * 128], bps, S32[:, sl, (d % 4) * 128:(d % 4 + 1) * 128], [b_S32[sl]])
                tails.append((ps, bps))
                if d % 2 == 0:
                    P.op("act", lambda e, ps=ps, d=d: e.activation(out=xTb[:, d, :], in_=ps[:], func=AF.Identity),
                         reads=[bps], writes=[b_xTb[d]])
                else:
                    P.op("dve", lambda e, ps=ps, d=d: e.tensor_copy(out=xTb[:, d, :], in_=ps[:]),
                         reads=[bps], writes=[b_xTb[d]])
            for d in range(NKC):
                ps, bps = tails[d]
                if d % 2 == 0:
                    P.op("act", lambda e, ps=ps, d=d: e.activation(out=xT[:, d, :], in_=ps[:], func=AF.Identity),
                         reads=[bps], writes=[b_xT[d]])
                else:
                    P.op("dve", lambda e, ps=ps, d=d: e.tensor_copy(out=xT[:, d, :], in_=ps[:]),
                         reads=[bps], writes=[b_xT[d]])

        def prefetch_xs(blk):
            P.op("pool", lambda e, blk=blk: e.dma_start(out=S32[:], in_=xs[blk]),
                 reads=[b_xs[blk]], writes=b_S32, dsem=DS_PF[0])

        def consume_xs(blk):
            P.tag = "io"
            for d in range(NKC):
                if d % 2 == 0:
                    P.op("act", lambda e, d=d: e.activation(out=xT[:, d, :], in_=S32[:, d, :], func=AF.Identity),
                         reads=[b_S32[d]], writes=[b_xT[d]])
                    P.op("dve", lambda e, d=d: e.tensor_copy(out=xTb[:, d, :], in_=S32[:, d, :]),
                         reads=[b_S32[d]], writes=[b_xTb[d]])
                else:
                    P.op("dve", lambda e, d=d: e.tensor_copy(out=xT[:, d, :], in_=S32[:, d, :]),
                         reads=[b_S32[d]], writes=[b_xT[d]])
                    P.op("act", lambda e, d=d: e.activation(out=xTb[:, d, :], in_=S32[:, d, :], func=AF.Identity),
                         reads=[b_S32[d]], writes=[b_xTb[d]])

        def store_block_to_out(blk):
            P.tag = "io"
            for s in range(NSUB):
                q = s
                for h in range(2):
                    ps, bps = next_ps()
                    for dd in range(4):
                        d = h * 4 + dd
                        tr(ps[:, dd * 128:(dd + 1) * 128], bps, xT[:, d, s * 128:(s + 1) * 128], [b_xT[d]])
                    eng = "act" if h == 0 else "dve"
                    if eng == "act":
                        P.op("act", lambda e, ps=ps, q=q, h=h: e.activation(out=xn[:, q, h * 512:(h + 1) * 512],
                                                                            in_=ps[:], func=AF.Identity),
                             reads=[bps], writes=[b_xn[q]])
                    else:
                        P.op("dve", lambda e, ps=ps, q=q, h=h: e.tensor_copy(out=xn[:, q, h * 512:(h + 1) * 512],
                                                                             in_=ps[:]),
                             reads=[bps], writes=[b_xn[q]])
                r0 = blk * TB + s * 128
                P.op("pool", lambda e, q=q, r0=r0: e.dma_start(out=out[r0:r0 + 128, :], in_=xn[:, q, :]),
                     reads=[b_xn[q]], dsem=DS_XIO[q])

        def store_block_xs(blk):
            P.op("pool", lambda e, blk=blk: e.dma_start(out=xs[blk], in_=xT[:]),
                 reads=b_xT, writes=[b_xs[blk]], dsem=DS_XS)

        cbt = sb("cbt", [128, NCB], BF16); b_cb = Buf("cb")
        cft = sb("cft", [128, NCF], F32); b_cf = Buf("cf")
        wsm = sb("wsm", [128, 4, 128], BF16); b_wsm = Buf("wsm")
        kTc = sb("kTc", [128, 2, S], BF16); b_kTc = Buf("kTc")
        Vc = sb("Vc", [128, S // 128, BW], BF16); b_Vc = Buf("Vc")
        B16 = sb("B16", [128, 8, TB], BF16); b_B16 = bufs("B16", 8)
        yT4 = sb("yT4", [128, 8, TB], BF16); b_yT4 = bufs("yT4", 8)
        qT = sb("qT", [128, 2, TB], BF16); b_qT = bufs("qT", 2)
        vln = sb("vln", [128, NSUB, BW], BF16); b_vln = bufs("vln", NSUB)
        ptok = sb("ptok", [128, NSUB + 1, BW], BF16); b_ptok = bufs("ptok", NSUB + 1)
        ybuf = sb("ybuf", [128, 2, 30 + TB], BF16); b_ybuf = bufs("ybuf", 2)
        dg = sb("dg", [128, 4, 128], BF16); b_dg = bufs("dg", 4)
        P.op("pool", lambda e: e.dma_start(out=cbt[:], in_=cb_in[:], max_dma_last_dim=8192), writes=[b_cb], dsem="cb")
        ntri = cbt[:, CB_NTRI:CB_NTRI + 128]
        nustr = cbt[:, CB_NUSTR:CB_NUSTR + 128]
        mergedT = gT
        b_merged = b_gT
        dg_ctr = [0]
        st_ctr = [0]

        def tok_stats_a(src_ap, src_buf, eps_ap):
            q = st_ctr[0] % 4
            st_ctr[0] += 1
            P.op("dve", lambda e: e.bn_stats(out=stt[:, q, 0:6], in_=src_ap), reads=[src_buf], writes=[b_stt[q]])
            P.op("dve", lambda e: e.bn_aggr(out=mv[:, q, :], in_=stt[:, q, 0:6]), reads=[b_stt[q]], writes=[b_mv[q]])
            P.op("act", lambda e: e.activation(out=rs[:, q, 0:1], in_=mv[:, q, 1:2], func=AF.Sqrt, bias=eps_ap, scale=1.0),
                 reads=[b_mv[q], b_eps], writes=[b_rs0[q]])
            return q

        def tok_stats_b(q):
            P.op("dve", lambda e: e.reciprocal(out=rs[:, q, 1:2], in_=rs[:, q, 0:1]), reads=[b_rs0[q]], writes=[b_rs1[q]])
            P.op("dve", lambda e: e.scalar_tensor_tensor(out=rs[:, q, 2:3], in0=mv[:, q, 0:1], scalar=-1.0,
                                                          in1=rs[:, q, 1:2], op0=ALU.mult, op1=ALU.mult),
                 reads=[b_rs1[q], b_mv[q]], writes=[b_rs[q]])
            return rs[:, q, 1:2], rs[:, q, 2:3], [b_rs[q], b_rs1[q]]

        def layer_setup(l):
            o = l * NCF
            P.op("pool", lambda e: e.dma_start(out=cft[:], in_=cf_in[:, o:o + NCF]), writes=[b_cf], dsem="cf")
            for g in range(4):
                P.op("dve", lambda e, g=g: e.tensor_tensor(out=wsm[:, g, :], in0=cft[:, CF_SGW + g * 128:CF_SGW + (g + 1) * 128],
                                                          in1=cbt[:, CB_TRILE:CB_TRILE + 128], op=ALU.mult),
                     reads=[b_cf, b_cb], writes=[b_wsm])
            for c in range(2):
                P.op("dve", lambda e, c=c: e.memset(ybuf[:, c, 0:30], 0.0), writes=[b_ybuf[c]])

        def mixer_phase(l, blk):
            t0 = blk * TB
            nt0 = blk * NSUB
            if blk == 0 and l > 0:
                layer_setup(l)
            P.tag = "mix_qk"
            slab, bsl = load_slab(l, "m_qk")
            for cc in range(4):
                ps, bps = next_ps()
                for k in range(NKC):
                    o = (cc * NKC + k) * 128
                    mm(ps[:], bps, slab[:, o:o + 128], xTb[:, k, :], [bsl, b_xTb[k]], k == 0, k == NKC - 1)
                if cc < 2:
                    P.op("act", lambda e, ps=ps, cc=cc: e.activation(out=qT[:, cc, :], in_=ps[:], func=AF.Identity),
                         reads=[bps], writes=[b_qT[cc]])
                else:
                    P.op("dve", lambda e, ps=ps, cc=cc: e.tensor_copy(out=kTc[:, cc - 2, t0:t0 + TB], in_=ps[:]),
                         reads=[bps], writes=[b_kTc])
            P.tag = "mix_tok"
            slab, bsl = load_slab(l, "m_tok")
            slabp, bslp = load_slab(l, "m_pool")
            slabw, bslw = load_slab(l, "m_poolw")
            for s in range(NSUB):
                ps, bps = next_ps()
                for k in range(NKC):
                    mm(ps[:], bps, xTb[:, k, s * 128:(s + 1) * 128], slab[:, k * 512:(k + 1) * 512],
                       [bsl, b_xTb[k]], k == 0, k == NKC - 1)
                P.op("dve", lambda e, ps=ps, s=s: e.tensor_copy(out=Vc[:, nt0 + s, :], in_=ps[:, 0:256]),
                     reads=[bps], writes=[b_Vc])
                P.op("act", lambda e, ps=ps, s=s: e.activation(out=xn[:, s, 256:512], in_=ps[:, 256:512],
                                                               func=AF.Gelu_apprx_tanh),
                     reads=[bps], writes=[b_xn[s]])
                ps2, bps2 = next_ps()
                for k in range(NKC):
                    mm(ps2[:, 0:256], bps2, xTb[:, k, s * 128:(s + 1) * 128], slabp[:, k * 256:(k + 1) * 256],
                       [bslp, b_xTb[k]], k == 0, k == NKC - 1)
                P.op("act", lambda e, ps2=ps2, s=s: e.activation(out=ptok[:, s + 1, :], in_=ps2[:, 0:256], func=AF.Identity),
                     reads=[bps2], writes=[b_ptok[s + 1]])
            P.tag = "mix_sgln"
            qs = [tok_stats_a(xn[:, s, 256:512], b_xn[s], eps1[:, 0:1]) for s in range(NSUB)]
            for s in range(NSUB):
                rstd, nmr, brs = tok_stats_b(qs[s])
                P.op("act", lambda e, s=s, rstd=rstd, nmr=nmr: e.activation(out=xn[:, s, 512:768], in_=xn[:, s, 256:512],
                                                                            func=AF.Identity, bias=nmr, scale=rstd),
                     reads=[b_xn[s]] + brs, writes=[b_xn[s]])
                P.op("dve", lambda e, s=s: e.tensor_tensor(out=xn[:, s, 512:768], in0=xn[:, s, 512:768],
                                                           in1=cft[:, CF_SGG:CF_SGG + 256], op=ALU.mult),
                     reads=[b_xn[s], b_cf], writes=[b_xn[s]])
                P.op("dve", lambda e, s=s: e.tensor_tensor(out=vln[:, s, :], in0=xn[:, s, 512:768],
                                                           in1=cft[:, CF_SGBE:CF_SGBE + 256], op=ALU.add),
                     reads=[b_xn[s], b_cf], writes=[b_vln[s]])
            P.tag = "mix_u"
            slab, bsl = load_slab(l, "m_ua")
            slabg, bslg = load_slab(l, "m_g")
            for c in range(2):
                ps, bps = next_ps()
                for k in range(NKC):
                    o = (c * NKC + k) * 128
                    mm(ps[:], bps, slab[:, o:o + 128], xTb[:, k, :], [bsl, b_xTb[k]], k == 0, k == NKC - 1)
                P.op("act", lambda e, ps=ps, c=c: e.activation(out=S32[:, 2 + c, :], in_=ps[:], func=AF.Gelu_apprx_tanh),
                     reads=[bps], writes=[b_S32[2 + c]])
            P.tag = "mix_conv"
            for c in range(2):
                psa, bpsa = next_ps()
                for k in range(NKC):
                    o = ((2 + c) * NKC + k) * 128
                    mm(psa[:], bpsa, slab[:, o:o + 128], xTb[:, k, :], [bsl, b_xTb[k]], k == 0, k == NKC - 1)
                psg, bpsg = next_ps()
                for k in range(NKC):
                    o = (c * NKC + k) * 128
                    mm(psg[:], bpsg, slabg[:, o:o + 128], xTb[:, k, :], [bslg, b_xTb[k]], k == 0, k == NKC - 1)
                P.op("act", lambda e, psg=psg: e.activation(out=S32[:, 0, :], in_=psg[:], func=AF.Sigmoid),
                     reads=[bpsg], writes=[b_S32[0]])
                P.op("dve", lambda e, psa=psa, c=c: e.tensor_tensor(out=ybuf[:, c, 30:30 + TB], in0=psa[:], in1=S32[:, 0, :],
                                                                    op=ALU.mult),
                     reads=[bpsa, b_S32[0]], writes=[b_ybuf[c]])
            for c in range(2):
                ps, bps = next_ps()
                for k in range(CONV_W):
                    q = dg_ctr[0] % 4
                    dg_ctr[0] += 1
                    wc = vcol(l, "conv_w", c * 31 + k)
                    P.op("dve", lambda e, q=q, wc=wc: e.tensor_scalar(out=dg[:, q, :], in0=ident[:], scalar1=wc, scalar2=None,
                                                                      op0=ALU.mult),
                         reads=[b_ident, b_vecs], writes=[b_dg[q]])
                    mm(ps[:], bps, dg[:, q, :], ybuf[:, c, k:k + TB], [b_dg[q], b_ybuf[c]], k == 0, k == CONV_W - 1)
                cb_ = vcol(l, "conv_b", c)
                P.op("act", lambda e, ps=ps, c=c, cb_=cb_: e.activation(out=S32[:, 6 + c, :], in_=ps[:], func=AF.Identity,
                                                                         bias=cb_, scale=1.0),
                     reads=[bps, b_vecs], writes=[b_S32[6 + c]])
                P.op("dve", lambda e, c=c: e.tensor_copy(out=ybuf[:, c, 0:30], in_=ybuf[:, c, TB:TB + 30]),
                     reads=[b_ybuf[c]], writes=[b_ybuf[c]])
            cps = []
            for s in range(NSUB):
                ps, bps = next_ps(range(4, 8))
                for c in range(2):
                    tr(ps[:, c * 128:(c + 1) * 128], bps, S32[:, 6 + c, s * 128:(s + 1) * 128], [b_S32[6 + c]])
                cps.append((ps, bps, tok_stats_a(ps[:, 0:256], bps, eps1[:, 0:1])))
            for s in range(NSUB):
                ps, bps, q = cps[s]
                rstd, nmr, brs = tok_stats_b(q)
                P.op("act", lambda e, ps=ps, s=s, rstd=rstd, nmr=nmr: e.activation(out=xn[:, s, 0:256], in_=ps[:, 0:256],
                                                                                   func=AF.Identity, bias=nmr, scale=rstd),
                     reads=[bps] + brs, writes=[b_xn[s]])
            P.tag = "mix_pool"
            for c in range(2):
                for par in range(2):
                    g = 2 * c + par
                    r0 = par * 64
                    ps, bps = next_ps(range(4))
                    for s in range(NSUB):
                        osl = ps[:, s * 128:(s + 1) * 128]
                        cur = ptok[:, s + 1, c * 128:(c + 1) * 128]
                        if blk == 0 and s == 0:
                            mm(osl, bps, cur, cbt[:, CB_BAND0 + g * 128:CB_BAND0 + (g + 1) * 128],
                               [b_ptok[s + 1], b_cb], True, True)
                        else:
                            mm(osl, bps, cur, cbt[:, CB_BAND + g * 128:CB_BAND + (g + 1) * 128],
                               [b_ptok[s + 1], b_cb], True, False)
                            mm(osl, bps, ptok[:, s, c * 128:(c + 1) * 128],
                               cbt[:, CB_BANDP + g * 128:CB_BANDP + (g + 1) * 128], [b_ptok[s], b_cb], False, True)
                    P.op("act", lambda e, ps=ps, c=c, r0=r0: e.activation(out=B16[r0:r0 + 64, 4 + c, :], in_=ps[r0:r0 + 64, :],
                                                                          func=AF.Identity),
                         reads=[bps], writes=[b_B16[4 + c]])
            P.op("dve", lambda e: e.tensor_copy(out=ptok[:, 0, :], in_=ptok[:, NSUB, :]), reads=[b_ptok[NSUB]],
                 writes=[b_ptok[0]])
            for c in range(2):
                ps, bps = next_ps(range(4))
                mm(ps[:], bps, slabw[:, c * 128:(c + 1) * 128], B16[:, 4 + c, :], [bslw, b_B16[4 + c]], True, True)
                sc_ = vcol(l, "pool_scale", c)
                P.op("act", lambda e, ps=ps, c=c, sc_=sc_: e.activation(out=yT4[:, 4 + c, :], in_=ps[:], func=AF.Identity,
                                                                        scale=sc_),
                     reads=[bps, b_vecs], writes=[b_yT4[4 + c]])
            P.tag = "mix_sgmix"
            for c in range(2):
                for par in range(2):
                    g = 2 * c + par
                    r0 = par * 64
                    ps, bps = next_ps(range(4))
                    for s in range(NSUB):
                        mm(ps[:, s * 128:(s + 1) * 128], bps, vln[:, s, c * 128:(c + 1) * 128], wsm[:, g, :],
                           [b_vln[s], b_wsm], True, True)
                    P.op("dve", lambda e, ps=ps, c=c, r0=r0: e.tensor_tensor(
                        out=S32[r0:r0 + 64, 4, :], in0=ps[r0:r0 + 64, :],
                        in1=cft[r0:r0 + 64, CF_SGB + c * 512:CF_SGB + (c + 1) * 512], op=ALU.add),
                        reads=[bps, b_cf], writes=[b_S32[4]])
                    P.op("dve", lambda e, c=c, r0=r0: e.tensor_tensor(
                        out=yT4[r0:r0 + 64, 2 + c, :], in0=S32[r0:r0 + 64, 4, :], in1=S32[r0:r0 + 64, 2 + c, :], op=ALU.mult),
                        reads=[b_S32[4], b_S32[2 + c]], writes=[b_yT4[2 + c]])
            P.tag = "mix_conv"
            for c in range(2):
                ps, bps = next_ps(range(4))
                for s in range(NSUB):
                    tr(ps[:, s * 128:(s + 1) * 128], bps, xn[:, s, c * 128:(c + 1) * 128], [b_xn[s]])
                g_ = vcol(l, "conv_ln_g", c)
                b_ = vcol(l, "conv_ln_b", c)
                P.op("act", lambda e, ps=ps, c=c, g_=g_, b_=b_: e.activation(out=yT4[:, 6 + c, :], in_=ps[:], func=AF.Silu,
                                                                             bias=b_, scale=g_),
                     reads=[bps, b_vecs], writes=[b_yT4[6 + c]])
            P.tag = "mix_attn"
            npair = nt0 + NSUB
            a_hi = npair - 1
            b_xnh = [[Buf(f"xnh{s_}{hh}") for hh in range(2)] for s_ in range(NSUB)]
            for s_ in range(NSUB):
                P.op("dve", lambda e, s_=s_: e.memset(xn[:, s_, 0:2], 0.0), writes=[b_xn[s_]] + b_xnh[s_])
            ebuf = []
            for h in range(4):
                ebuf.append([(xn[:, h, 0:512], b_xnh[h][0]), (xn[:, h, 512:1024], b_xnh[h][1]), (S32[:, h, :], b_S32[h])])
            Ebuf = [(S32[:, 4 + h, :], b_S32[4 + h]) for h in range(4)]
            spbuf = [[(gT[:, 8 + h * 3 + i, :], b_gT[8 + h * 3 + i]) for i in range(3)] for h in range(4)]
            wbuf = [[(B16[:, h * 2 + i, :], b_B16[h * 2 + i]) for i in range(2)] for h in range(4)]
            actr = [0]

            def pinfo(p):
                a = a_hi - p
                dd = a - nt0
                c0 = 128 * max(dd, 0)
                return a, dd, c0

            for t in range(npair + 2):
                if t < npair:
                    a, dd, c0 = pinfo(t)
                    pend_ln = None
                    for h in range(4):
                        c = h // 2
                        r0 = (h % 2) * 64
                        psA, bA = psum[actr[0] % 2], b_ps[actr[0] % 2]
                        actr[0] += 1
                        mm(psA[:, c0:TB], bA, kTc[r0:r0 + 64, c, a * 128:(a + 1) * 128], qT[r0:r0 + 64, c, c0:TB],
                           [b_kTc, b_qT[c]], True, True)
                        eap, eb = ebuf[h][t % 3]
                        P.op("act", lambda e, psA=psA, eap=eap, c0=c0: e.activation(out=eap[:, c0:TB], in_=psA[:, c0:TB],
                                                                                  func=AF.Exp, scale=0.125),
                             reads=[bA], writes=[eb])
                        if dd >= 0:
                            P.op("dve", lambda e, eap=eap, dd=dd, c0=c0: e.tensor_tensor(
                                out=eap[:, c0:TB], in0=eap[:, c0:TB], in1=cbt[:, CB_MASK + dd * 512 + c0:CB_MASK + (dd + 1) * 512],
                                op=ALU.mult), reads=[eb, b_cb], writes=[eb])
                        if pend_ln is not None:
                            pend_ln()
                        sap, sbf = spbuf[h][t % 3]

                        def _ln(eap=eap, eb=eb, sap=sap, sbf=sbf, c0=c0):
                            P.op("act", lambda e: e.activation(out=sap[:, c0:TB], in_=eap[:, c0:TB], func=AF.Ln, bias=1.0, scale=1.0),
                                 reads=[eb], writes=[sbf])
                        pend_ln = _ln
                    pend_ln()
                if t >= 2:
                    p = t - 2
                    a, dd, c0 = pinfo(p)
                    for h in range(4):
                        c = h // 2
                        r0 = (h % 2) * 64
                        sap, sbf = spbuf[h][p % 3]
                        wap, wbf = wbuf[h][p % 2]
                        if p < npair - 1:
                            mm(psum[2 + h][:, c0:TB], b_ps[2 + h], nustr, sap[:, c0:TB], [b_cb, sbf], False, True, skip=True)
                        mm(psum[6 + c][r0:r0 + 64, c0:TB], b_ps[6 + c], Vc[:, a, h * 64:(h + 1) * 64], wap[:, c0:TB],
                           [b_Vc, wbf], p == 0, True, skip=(p > 0))
                if 1 <= t <= npair:
                    p = t - 1
                    a, dd, c0 = pinfo(p)
                    for h in range(4):
                        sap, sbf = spbuf[h][p % 3]
                        eap, eb = ebuf[h][p % 3]
                        Eap, Ebf = Ebuf[h]
                        wap, wbf = wbuf[h][p % 2]
                        psB, bB = psum[2 + h], b_ps[2 + h]
                        mm(psB[:, c0:TB], bB, ntri, sap[:, c0:TB], [b_cb, sbf], p == 0, True, skip=(p > 0))
                        P.op("act", lambda e, psB=psB, Eap=Eap, c0=c0: e.activation(out=Eap[:, c0:TB], in_=psB[:, c0:TB], func=AF.Exp),
                             reads=[bB], writes=[Ebf])
                        P.op("dve", lambda e, wap=wap, eap=eap, Eap=Eap, c0=c0: e.tensor_tensor(
                            out=wap[:, c0:TB], in0=eap[:, c0:TB], in1=Eap[:, c0:TB], op=ALU.mult),
                            reads=[eb, Ebf], writes=[wbf])
            for c in range(2):
                P.op("act", lambda e, c=c: e.activation(out=yT4[:, c, :], in_=psum[6 + c][:], func=AF.Identity),
                     reads=[b_ps[6 + c]], writes=[b_yT4[c]])
            for s_ in range(NSUB):
                P.op("dve", lambda e, s_=s_: e.memset(xn[:, s_, 0:2], 0.0), writes=[b_xn[s_]] + b_xnh[s_])
            P.tag = "mix_gates"
            for d in range(NKC):
                slabg_, bslg_ = load_slab(l, f"m_gate{d}")
                if d % 4 == 0:
                    slabb, bslb = load_slab(l, f"m_br{d // 4}")
                for ni, n in enumerate((1, 2, 3, 0)):
                    psg, bpsg = next_ps()
                    for k in range(NKC):
                        o = (n * NKC + k) * 128
                        mm(psg[:], bpsg, slabg_[:, o:o + 128], xTb[:, k, :], [bslg_, b_xTb[k]], k == 0, k == NKC - 1)
                    gb = vcol(l, f"gate_b{n}", d)
                    gi = n % 2
                    P.op("act", lambda e, psg=psg, gi=gi, gb=gb: e.activation(out=S32[:, gi, :], in_=psg[:], func=AF.Sigmoid,
                                                                             bias=gb, scale=1.0),
                         reads=[bpsg, b_vecs], writes=[b_S32[gi]])
                    psb, bpsb = next_ps()
                    for kk in range(2):
                        o = (((d % 4) * 4 + n) * 2 + kk) * 128
                        mm(psb[:], bpsb, slabb[:, o:o + 128], yT4[:, 2 * n + kk, :], [bslb, b_yT4[2 * n + kk]], kk == 0, kk == 1)
                    ai = 4 + d % 2
                    if ni == 0:
                        P.op("dve", lambda e, psb=psb, gi=gi, ai=ai: e.tensor_tensor(out=S32[:, ai, :], in0=psb[:], in1=S32[:, gi, :],
                                                                                    op=ALU.mult),
                             reads=[bpsb, b_S32[gi]], writes=[b_S32[ai]])
                    else:
                        ti = 2 + n % 2
                        P.op("dve", lambda e, psb=psb, gi=gi, ti=ti: e.tensor_tensor(out=S32[:, ti, :], in0=psb[:], in1=S32[:, gi, :],
                                                                                    op=ALU.mult),
                             reads=[bpsb, b_S32[gi]], writes=[b_S32[ti]])
                        if ni < 3:
                            P.op("dve", lambda e, ti=ti, ai=ai: e.tensor_tensor(out=S32[:, ai, :], in0=S32[:, ai, :], in1=S32[:, ti, :],
                                                                               op=ALU.add),
                                 reads=[b_S32[ai], b_S32[ti]], writes=[b_S32[ai]])
                        else:
                            P.op("dve", lambda e, ti=ti, ai=ai, d=d: e.tensor_tensor(out=mergedT[:, d, :], in0=S32[:, ai, :],
                                                                                    in1=S32[:, ti, :], op=ALU.add),
                                 reads=[b_S32[ai], b_S32[ti]], writes=[b_merged[d]])
            P.tag = "mix_out"
            P.op("act", lambda e: e.activation(out=acts[:, 0:1], in_=one1[:, 0:1], func=AF.Sqrt), reads=[b_eps], writes=[b_acts])
            c_res = 1.0 / DN_ALPHA
            for d in range(NKC):
                if d % 4 == 0:
                    slabo, bslo = load_slab(l, f"m_out{d // 4}")
                ps, bps = next_ps()
                for k in range(NKC):
                    o = ((d % 4) * NKC + k) * 128
                    mm(ps[:], bps, slabo[:, o:o + 128], mergedT[:, k, :], [bslo, b_merged[k]], k == 0, k == NKC - 1)
                P.op("dve", lambda e, ps=ps, d=d: e.scalar_tensor_tensor(out=xT[:, d, :], in0=ps[:], scalar=c_res,
                                                                         in1=xT[:, d, :], op0=ALU.mult, op1=ALU.add),
                     reads=[bps, b_xT[d]], writes=[b_xT[d]])
            layer_norm_T(l, "ln_g1", "ln_b1")

        def names_of(ph):
            if ph == "f0":
                return [n for n, _ in SLABS if n.startswith("f0_")]
            if ph == "f1":
                return [n for n, _ in SLABS if n.startswith("f1_")]
            return [n for n, _ in SLABS if n.startswith("m_")]

        def prepass_names(l, names):
            for nm in names:
                for i, (ll, name, sz) in enumerate(pre_list):
                    if ll == l and name == nm:
                        pre_list.insert(0, pre_list.pop(i))
                        emit_prepass(1)
                        break

        phase_fn = {"f0": lambda l, blk: ffn_phase(l, 0, 0), "mix": mixer_phase, "f1": lambda l, blk: ffn_phase(l, 1, 2)}
        order = [ph for ph in ("f0", "mix", "f1") if ph in cfg.phases]
        seq = [(l, blk) for l in range(cfg.depth) for blk in range(NBLK)]

        def prefetch(l, blk):
            if l == 0:
                prefetch_x(blk)
            else:
                prefetch_xs(blk)

        b_xs = bufs("xs", NBLK)
        for idx, (l, blk) in enumerate(seq):
            P.tag = "io"
            if idx == 0:
                if order:
                    prepass_names(0, names_of(order[0])[:4])
                prefetch(l, blk)
                if "mix" in cfg.phases:
                    layer_setup(0)
                emit_prepass(sum(1 for (ll, _n, _s) in pre_list if ll == 0))
            if l > 0 and blk == 0:
                emit_prepass(10 ** 6)
            if l == 0:
                consume_x(blk)
            else:
                consume_xs(blk)
            for pi, ph in enumerate(order):
                P.tag = "io"
                if l == 0 and blk == 0:
                    prepass_names(0, names_of(ph))
                elif l == 0 and cfg.depth > 1:
                    emit_prepass(3)
                if pi == len(order) - 1 and ph != "mix" and idx + 1 < len(seq):
                    prefetch(*seq[idx + 1])
                phase_fn[ph](l, blk)
            if not (order and order[-1] != "mix") and idx + 1 < len(seq):
                prefetch(*seq[idx + 1])
            if l == cfg.depth - 1:
                store_block_to_out(blk)
            else:
                store_block_xs(blk)

        fin = sb("fin", [128, 1], F32)
        b_fin = Buf("fin")
        last_out_ops = [o for o in P.ops["pool"] if o.dsem in DS_XIO]
        tail = P.op("pool", lambda e: e.memset(fin[:], 0.0), writes=[b_fin])
        for q in range(4):
            lo = [o for o in last_out_ops if o.dsem == DS_XIO[q]]
            if lo:
                tail.deps.append(lo[-1])

        P.finalize()
        if getattr(cfg, "dbg", None) is not None:
            P.dbg = cfg.dbg
        if getattr(cfg, "pe_tags", None) is not None:
            cfg.pe_tags.extend(o.tag for o in P.ops["pe"])
        dsem_names = [DS_CONST, "cb", "cf"] + DS_RING + DS_PRE + DS_XIO + [DS_XS] + DS_PF
        esems = {e: es.enter_context(nc.semaphore(f"e_{e}")) for e in Prog.ENGS}
        dsems = {n: es.enter_context(nc.semaphore(f"d_{n}")) for n in dsem_names}
        block = es.enter_context(nc.Block())

        @block.tensor
        def _(e):
            P.emit("pe", e, esems, dsems)

        @block.scalar
        def _(e):
            P.emit("act", e, esems, dsems)

        @block.vector
        def _(e):
            P.emit("dve", e, esems, dsems)

        @block.gpsimd
        def _(e):
            P.emit("pool", e, esems, dsems)

        @block.sync
        def _(e):
            P.emit("sp", e, esems, dsems)

    return nc


_PROGRAM_CACHE = {}


def kernel(**inputs):
    inp = {k: np.asarray(v) for k, v in inputs.items()}
    x = inp["x"].astype(np.float32, copy=False)
    B = x.shape[0]
    W = pack_weights(inp)
    V = pack_vecs(inp)
    C = pack_consts(inp)
    cfg = Cfg()
    nc = build_program(cfg)
    in_maps = [dict(x=np.ascontiguousarray(x[b]), w=W, vecs=V, **C) for b in range(B)]
    res = run_bass_kernel_spmd(nc, in_maps, core_ids=list(range(B)))
    return np.stack([np.asarray(r["out"]) for r in res.results], axis=0).astype(np.float32)
```

```python
import math
from contextlib import ExitStack

import numpy as np

import concourse.bass as bass
import concourse.mybir as mybir
from concourse.bass_utils import run_bass_kernel_spmd

F32 = mybir.dt.float32
BF16 = mybir.dt.bfloat16
AF = mybir.ActivationFunctionType
ALU = mybir.AluOpType

D_MODEL = 1024
SEQ = 4096
DEPTH = 2
BW = 256
HEAD_DIM = 64
D_FF = 2816
CONV_W = 31
LN_EPS = 1e-5
DN_ALPHA = (2.0 * DEPTH) ** 0.25
POOL_WINDOWS = (2, 4, 8, 16)

TB = 512
NSUB = TB // 128
NKC = D_MODEL // 128
NFC = D_FF // 128
SLOT = 4096
NSLOT = 5


class Buf:
    __slots__ = ("name", "w", "r", "excl")

    def __init__(self, name, excl=False):
        self.name = name
        self.w = None
        self.r = {}
        self.excl = excl


class Op:
    __slots__ = ("eng", "fn", "deps", "signal", "sigval", "dsem", "key", "tag")


class Prog:
    ENGS = ("pe", "act", "dve", "pool", "sp")

    def __init__(self):
        self.ops = {e: [] for e in self.ENGS}
        self.dsem_count = {}
        self.last_dma = {}
        self.tag = ""

    def op(self, eng, fn, reads=(), writes=(), dsem=None):
        o = Op()
        o.eng = eng
        o.fn = fn
        o.signal = False
        o.sigval = None
        o.dsem = dsem
        o.tag = self.tag
        o.key = ("d", dsem) if dsem is not None else ("e", eng)
        deps = {}
        for b in reads:
            if b.w is not None:
                deps[id(b.w)] = b.w
            if b.excl:
                for r in b.r.values():
                    if r.key != o.key:
                        deps[id(r)] = r
        for b in writes:
            if b.w is not None:
                deps[id(b.w)] = b.w
            for r in b.r.values():
                deps[id(r)] = r
        if dsem is not None:
            pd = self.last_dma.get(dsem)
            if pd is not None:
                deps[id(pd)] = pd
            self.last_dma[dsem] = o
        dl = []
        for d in deps.values():
            if d is o:
                continue
            if d.dsem is None and d.eng == "pe" and eng == "pe" and dsem is None:
                continue
            d.signal = True
            dl.append(d)
        o.deps = dl
        for b in reads:
            b.r[o.key] = o
        for b in writes:
            b.w = o
            b.r = {}
        self.ops[eng].append(o)
        return o

    def finalize(self):
        cnt = {e: 0 for e in self.ENGS}
        for e in self.ENGS:
            for o in self.ops[e]:
                if o.dsem is not None:
                    c = self.dsem_count.get(o.dsem, 0) + 16
                    self.dsem_count[o.dsem] = c
                    o.sigval = c
                elif o.signal:
                    cnt[e] += 1
                    o.sigval = cnt[e]
        return cnt

    def emit(self, eng, handle, esems, dsems):
        seen = {}
        dbg = getattr(self, "dbg", None)
        for o in self.ops[eng]:
            if dbg is not None:
                dbg.append((eng, o.fn.__code__.co_firstlineno, [(d.key, d.sigval) for d in o.deps], o.key, o.sigval))
            need = {}
            for d in o.deps:
                k = d.key
                if d.sigval > need.get(k, 0):
                    need[k] = d.sigval
            for k, v in need.items():
                if v > seen.get(k, 0):
                    seen[k] = v
                    sem = dsems[k[1]] if k[0] == "d" else esems[k[1]]
                    handle.wait_ge(sem, v)
            ins = o.fn(handle)
            if o.dsem is not None:
                ins.then_inc(dsems[o.dsem], 16)
            elif o.signal:
                ins.then_inc(esems[eng], 1)


def _slab_list():
    sl = []
    for i in (0,):
        pass
    def ffn(i):
        for j2 in range(NFC // 2):
            sl.append((f"f{i}_in{j2}", 4096))
        for d in range(NKC):
            sl.append((f"f{i}_out{d}", NFC * 128))
    ffn(0)
    sl.append(("m_qk", 4096))
    sl.append(("m_tok", 4096))
    sl.append(("m_pool", 2048))
    sl.append(("m_poolw", 256))
    sl.append(("m_ua", 4096))
    sl.append(("m_g", 2048))
    for d in range(NKC):
        sl.append((f"m_gate{d}", 4096))
        if d % 4 == 0:
            sl.append((f"m_br{d // 4}", 4096))
    sl.append(("m_out0", 4096))
    sl.append(("m_out1", 4096))
    ffn(1)
    return sl


SLABS = _slab_list()
SLAB_OFF = {}
_o = 0
for _n, _sz in SLABS:
    SLAB_OFF[_n] = (_o, _sz)
    _o += _sz
LAYER_W = _o


def _kc(w, cols):
    k = w.shape[0] // 128
    nch = len(cols) // 128
    a = w[:, cols].reshape(k, 128, nch, 128)
    return a.transpose(1, 2, 0, 3)


def pack_weights(inp):
    W = np.empty((128, DEPTH * LAYER_W), np.float32)
    for l in range(DEPTH):
        def put(name, arr):
            off, sz = SLAB_OFF[name]
            a = np.ascontiguousarray(arr).reshape(128, -1)
            assert a.shape[1] == sz, (name, a.shape, sz)
            W[:, l * LAYER_W + off: l * LAYER_W + off + sz] = a
        for i in range(2):
            w_in = inp["ffn_w_in"][l, i]
            w4 = w_in.reshape(NKC, 128, 2, NFC, 128)
            for j2 in range(NFC // 2):
                a = w4[:, :, :, 2 * j2:2 * j2 + 2, :]
                put(f"f{i}_in{j2}", a.transpose(1, 3, 2, 0, 4))
            w_out = inp["ffn_w_out"][l, i].reshape(NFC, 128, NKC, 128)
            for d in range(NKC):
                put(f"f{i}_out{d}", w_out[:, :, d, :].transpose(1, 0, 2))
        mw = inp["mix_w_in"][l]
        put("m_qk", _kc(mw, np.arange(0, 512)))
        tokc = np.concatenate([np.arange(512, 768), np.arange(1024, 1280)])
        put("m_tok", mw[:, tokc].reshape(NKC, 128, 512).transpose(1, 0, 2))
        put("m_pool", mw[:, 1280:1536].reshape(NKC, 128, 256).transpose(1, 0, 2))
        pw = np.zeros((128, 2, 128), np.float32)
        for g in range(4):
            r = (g % 2) * 64
            pw[r:r + 64, g // 2, r:r + 64] = inp["pool_w"][l, g]
        put("m_poolw", pw)
        put("m_ua", _kc(mw, np.concatenate([np.arange(768, 1024), np.arange(1536, 1792)])))
        put("m_g", _kc(mw, np.arange(1792, 2048)))
        gw = inp["gate_w"][l].reshape(4, NKC, 128, NKC, 128)
        for d in range(NKC):
            put(f"m_gate{d}", gw[:, :, :, d, :].transpose(2, 0, 1, 3))
        bw = inp["branch_w"][l].reshape(4, 2, 128, NKC, 128)
        for h in range(2):
            put(f"m_br{h}", bw[:, :, :, 4 * h:4 * h + 4, :].transpose(2, 3, 0, 1, 4))
        ow = inp["out_w"][l].reshape(NKC, 128, NKC, 128)
        for h in range(2):
            put(f"m_out{h}", ow[:, :, 4 * h:4 * h + 4, :].transpose(1, 2, 0, 3))
    return W


VEC_COLS = {}
_c = 0
def _vc(name, n):
    global _c
    VEC_COLS[name] = _c
    _c += n
for _i in range(3):
    _vc(f"ln_g{_i}", 8); _vc(f"ln_b{_i}", 8)
for _n in range(4):
    _vc(f"gate_b{_n}", 8)
_vc("pool_scale", 2); _vc("conv_b", 2); _vc("conv_ln_g", 2); _vc("conv_ln_b", 2)
_vc("conv_w", 62)
NVEC = _c


def pack_vecs(inp):
    V = np.zeros((128, DEPTH * NVEC), np.float32)
    for l in range(DEPTH):
        def put(name, v):
            c0 = l * NVEC + VEC_COLS[name]
            a = np.asarray(v).reshape(-1, 128).T
            V[:, c0:c0 + a.shape[1]] = a
        for i in range(3):
            put(f"ln_g{i}", inp["ln_g"][l, i]); put(f"ln_b{i}", inp["ln_b"][l, i])
        for n in range(4):
            put(f"gate_b{n}", inp["gate_b"][l, n])
        put("pool_scale", inp["pool_scale"][l]); put("conv_b", inp["conv_b"][l])
        put("conv_ln_g", inp["conv_ln_g"][l]); put("conv_ln_b", inp["conv_ln_b"][l])
        cw = inp["conv_w"][l]
        c0 = l * NVEC + VEC_COLS["conv_w"]
        for c in range(2):
            V[:, c0 + c * 31: c0 + (c + 1) * 31] = cw[:, c * 128:(c + 1) * 128].T
    return V


CB_NTRI, CB_NUSTR, CB_TRILE = 0, 128, 256
CB_BAND, CB_BANDP, CB_BAND0 = 384, 896, 1408
CB_MASK = 1920
NCB = CB_MASK + 2048
CF_SGW, CF_SGB, CF_SGG, CF_SGBE = 0, 512, 1536, 1792
NCF = 2048


def pack_consts(inp=None):
    c = {}
    c["ident"] = np.eye(128, dtype=np.float32)
    j = np.arange(128)[:, None]
    t = np.arange(128)[None, :]
    cb = np.zeros((128, NCB), np.float32)
    cb[:, CB_NTRI:CB_NTRI + 128] = -1.0 * (j >= t)
    cb[:, CB_NUSTR:CB_NUSTR + 128] = -1.0 * (j < t)
    cb[:, CB_TRILE:CB_TRILE + 128] = (j <= t)
    for g, win in enumerate(POOL_WINDOWS):
        band = ((j <= t) & (j > t - win)).astype(np.float32) / win - (j == t)
        cnt = np.minimum(t + 1, win).astype(np.float32)
        band0 = ((j <= t) & (j > t - win)).astype(np.float32) / cnt - (j == t)
        bandp = ((j + 0 - 128 > t - win)).astype(np.float32) / win
        cb[:, CB_BAND + g * 128:CB_BAND + (g + 1) * 128] = band
        cb[:, CB_BANDP + g * 128:CB_BANDP + (g + 1) * 128] = bandp
        cb[:, CB_BAND0 + g * 128:CB_BAND0 + (g + 1) * 128] = band0
    col = np.arange(512)[None, :]
    for d in range(4):
        cb[:, CB_MASK + d * 512:CB_MASK + (d + 1) * 512] = (col > 128 * d + j)
    c["cb"] = cb
    if inp is not None:
        cf = np.zeros((128, DEPTH * NCF), np.float32)
        for l in range(DEPTH):
            o = l * NCF
            sgw = inp["sg_w"][l]
            cf[:, o + CF_SGW:o + CF_SGW + 512] = sgw.transpose(2, 0, 1).reshape(128, 512)
            sgb = inp["sg_b"][l]
            rep = np.empty((128, 2, 4, 128), np.float32)
            for cc in range(2):
                rep[0:64, cc] = sgb[2 * cc][None, None, :]
                rep[64:128, cc] = sgb[2 * cc + 1][None, None, :]
            cf[:, o + CF_SGB:o + CF_SGB + 1024] = rep.reshape(128, 1024)
            cf[:, o + CF_SGG:o + CF_SGG + 256] = inp["sg_ln_g"][l][None, :]
            cf[:, o + CF_SGBE:o + CF_SGBE + 256] = inp["sg_ln_b"][l][None, :]
        c["cf"] = cf
    return c


class Cfg:
    def __init__(self, seq=SEQ, depth=DEPTH, phases=("f0", "mix", "f1"), dbg=None):
        self.seq = seq
        self.depth = depth
        self.phases = phases
        self.nblk = seq // TB
        self.dbg = dbg


def build_program(cfg):
    nc = bass.Bass("TRN2", target_bir_lowering=False)
    S = cfg.seq
    NBLK = cfg.nblk
    x_in = nc.dram_tensor("x", [S, D_MODEL], F32, kind="ExternalInput").ap()
    w_in = nc.dram_tensor("w", [128, DEPTH * LAYER_W], F32, kind="ExternalInput").ap()
    vec_in = nc.dram_tensor("vecs", [128, DEPTH * NVEC], F32, kind="ExternalInput").ap()
    ident_in = nc.dram_tensor("ident", [128, 128], F32, kind="ExternalInput").ap()
    cb_in = nc.dram_tensor("cb", [128, NCB], F32, kind="ExternalInput").ap()
    cf_in = nc.dram_tensor("cf", [128, DEPTH * NCF], F32, kind="ExternalInput").ap()
    out = nc.dram_tensor("out", [S, D_MODEL], F32, kind="ExternalOutput").ap()
    wb = nc.dram_tensor("wb", [128, DEPTH * LAYER_W], BF16, kind="Internal").ap()
    xs = nc.dram_tensor("xs", [NBLK, 128, NKC, TB], F32, kind="Internal").ap()

    P = Prog()
    es = ExitStack()

    def sb(name, shape, dt):
        return es.enter_context(nc.sbuf_tensor("s_" + name, shape, dt))

    def bufs(name, n):
        return [Buf(f"{name}{i}") for i in range(n)]

    with es:
        ident = sb("ident", [128, 128], F32); b_ident = Buf("ident")
        vecs = sb("vecs", [128, DEPTH * NVEC], F32); b_vecs = Buf("vecs")
        ring = sb("ring", [128, NSLOT, SLOT], BF16); b_ring = bufs("ring", NSLOT)
        xT = sb("xT", [128, NKC, TB], F32); b_xT = bufs("xT", NKC)
        xTb = sb("xTb", [128, NKC, TB], BF16); b_xTb = bufs("xTb", NKC)
        xn = sb("xn", [128, NSUB, D_MODEL], F32); b_xn = bufs("xn", NSUB)
        gT = sb("gT", [128, NFC, TB], BF16); b_gT = bufs("gT", NFC)
        S32 = sb("S32", [128, 8, TB], F32); b_S32 = bufs("S32", 8)
        stt = sb("stt", [128, 4, 12], F32); b_stt = bufs("stt", 4)
        mv = sb("mv", [128, 4, 2], F32); b_mv = bufs("mv", 4)
        rs = sb("rs", [128, 4, 4], F32); b_rs = bufs("rs", 4); b_rs0 = bufs("rs0_", 4); b_rs1 = bufs("rs1_", 4)
        psum = [es.enter_context(nc.psum_tensor(f"ps{i}", [128, 512], F32)) for i in range(8)]
        b_ps = [Buf(f"ps{i}", excl=True) for i in range(8)]
        ps_rr = [0]

        def next_ps(subset=range(8)):
            subset = list(subset)
            i = subset[ps_rr[0] % len(subset)]
            ps_rr[0] += 1
            return psum[i], b_ps[i]

        DS_CONST = "const"
        DS_RING = [f"ring{i}" for i in range(NSLOT)]
        DS_PRE = [f"pre{i}" for i in range(8)]
        DS_XIO = ["xio0", "xio1", "xio2", "xio3"]
        DS_XS = "xs"

        P.op("pool", lambda e: e.dma_start(out=ident[:], in_=ident_in[:]), writes=[b_ident], dsem=DS_CONST)
        P.op("pool", lambda e: e.dma_start(out=vecs[:], in_=vec_in[:]), writes=[b_vecs], dsem=DS_CONST)

        b_wb = {}
        npre = [0]
        pre_list = [(l, name, sz) for l in range(cfg.depth) for (name, sz) in SLABS]

        def emit_prepass(n):
            if getattr(cfg, "no_prepass", False):
                return
            for _ in range(n):
                if not pre_list:
                    return
                l, name, sz = pre_list.pop(0)
                off = l * LAYER_W + SLAB_OFF[name][0]
                b = Buf(f"wb{l}{name}")
                b_wb[(l, name)] = b
                P.op("pool",
                     lambda e, off=off, sz=sz: e.dma_start(out=wb[:, off:off + sz], in_=w_in[:, off:off + sz],
                                                            max_dma_last_dim=8192),
                     writes=[b], dsem=DS_PRE[npre[0] % 8])
                npre[0] += 1


        slab_ctr = [0]

        def load_slab(l, name):
            off, sz = SLAB_OFF[name]
            off += l * LAYER_W
            k = slab_ctr[0] % NSLOT
            slab_ctr[0] += 1
            P.op("sp", lambda e, k=k, off=off, sz=sz: e.dma_start(out=ring[:, k, 0:sz], in_=wb[:, off:off + sz]),
                 reads=[b_wb[(l, name)]], writes=[b_ring[k]], dsem=DS_RING[k])
            return ring[:, k, :], b_ring[k]

        def vcol(l, name, j):
            c = l * NVEC + VEC_COLS[name] + j
            return vecs[:, c:c + 1]

        def mm(ps, bps, lhsT, rhs, rd, start, stop, skip=False):
            P.op("pe", lambda e: e.matmul(ps, lhsT=lhsT, rhs=rhs, start=start, stop=stop, skip_group_check=skip),
                 reads=rd, writes=[bps])

        def tr(ps, bps, in_, rd):
            P.op("pe", lambda e: e.transpose(ps, in_, ident[:]), reads=rd + [b_ident], writes=[bps])

        ln_ctr = [0]

        def layer_norm_T(l, gname, bname):
            P.tag = "ln"
            pzs = {}
            for s in range(NSUB + 1):
                if s < NSUB:
                    q = s % 2
                    pz = [next_ps(), next_ps()]
                    pzs[s] = pz
                    for d in range(NKC):
                        ps, bps = pz[d // 4]
                        tr(ps[:, (d % 4) * 128:(d % 4 + 1) * 128], bps, xT[:, d, s * 128:(s + 1) * 128], [b_xT[d]])
                    for h in range(2):
                        ps, bps = pz[h]
                        P.op("dve", lambda e, ps=ps, q=q, h=h: e.bn_stats(out=stt[:, q, h * 6:(h + 1) * 6], in_=ps[:]),
                             reads=[bps], writes=[b_stt[q]])
                    P.op("dve", lambda e, q=q: e.bn_aggr(out=mv[:, q, :], in_=stt[:, q, :]),
                         reads=[b_stt[q]], writes=[b_mv[q]])
                    P.op("act", lambda e, q=q: e.activation(out=rs[:, q, 0:1], in_=mv[:, q, 1:2], func=AF.Sqrt,
                                                            bias=eps_t[:, 0:1], scale=1.0),
                         reads=[b_mv[q], b_eps], writes=[b_rs0[q]])
                if s >= 1:
                    sp_ = s - 1
                    q = sp_ % 2
                    P.op("dve", lambda e, q=q: e.reciprocal(out=rs[:, q, 1:2], in_=rs[:, q, 0:1]),
                         reads=[b_rs0[q]], writes=[b_rs1[q]])
                    P.op("dve", lambda e, q=q: e.scalar_tensor_tensor(out=rs[:, q, 2:3], in0=mv[:, q, 0:1], scalar=-1.0,
                                                                       in1=rs[:, q, 1:2], op0=ALU.mult, op1=ALU.mult),
                         reads=[b_rs1[q], b_mv[q]], writes=[b_rs[q]])
                    for h in range(2):
                        ps, bps = pzs[sp_][h]
                        P.op("act", lambda e, ps=ps, q=q, h=h, sp_=sp_: e.activation(
                            out=xn[:, sp_, h * 512:(h + 1) * 512], in_=ps[:], func=AF.Identity,
                            bias=rs[:, q, 2:3], scale=rs[:, q, 1:2]),
                            reads=[bps, b_rs[q], b_rs1[q]], writes=[b_xn[sp_]])
            tails = []
            for d in range(NKC):
                ps, bps = next_ps()
                for s in range(NSUB):
                    tr(ps[:, s * 128:(s + 1) * 128], bps, xn[:, s, d * 128:(d + 1) * 128], [b_xn[s]])
                g = vcol(l, gname, d)
                b = vcol(l, bname, d)
                tails.append((ps, bps, g, b))
                if d % 2 == 0:
                    P.op("act", lambda e, ps=ps, d=d, g=g, b=b: e.activation(out=xTb[:, d, :], in_=ps[:], func=AF.Identity,
                                                                             bias=b, scale=g),
                         reads=[bps, b_vecs], writes=[b_xTb[d]])
                else:
                    P.op("dve", lambda e, ps=ps, d=d, g=g, b=b: e.tensor_scalar(out=xTb[:, d, :], in0=ps[:], scalar1=g,
                                                                               scalar2=b, op0=ALU.mult, op1=ALU.add),
                         reads=[bps, b_vecs], writes=[b_xTb[d]])
            for d in range(NKC):
                ps, bps, g, b = tails[d]
                if d % 2 == 0:
                    P.op("act", lambda e, ps=ps, d=d, g=g, b=b: e.activation(out=xT[:, d, :], in_=ps[:], func=AF.Identity,
                                                                             bias=b, scale=g),
                         reads=[bps, b_vecs], writes=[b_xT[d]])
                else:
                    P.op("dve", lambda e, ps=ps, d=d, g=g, b=b: e.tensor_scalar(out=xT[:, d, :], in0=ps[:], scalar1=g,
                                                                               scalar2=b, op0=ALU.mult, op1=ALU.add),
                         reads=[bps, b_vecs], writes=[b_xT[d]])

        def ffn_phase(l, i, ln_idx):
            c_res = 0.5 / DN_ALPHA
            P.tag = "ffn_in"
            b_sil = [Buf("silh0"), Buf("silh1")]
            P.op("dve", lambda e: e.memset(xn[:, 0, 0:2], 0.0), writes=[b_xn[0]] + b_sil)
            for j2 in range(NFC // 2):
                slab, bsl = load_slab(l, f"f{i}_in{j2}")
                for jj in range(2):
                    j = 2 * j2 + jj
                    pa = next_ps()
                    pu = next_ps()
                    for t, (ps, bps) in enumerate((pa, pu)):
                        for k in range(NKC):
                            o = ((jj * 2 + t) * NKC + k) * 128
                            mm(ps[:], bps, slab[:, o:o + 128], xTb[:, k, :], [bsl, b_xTb[k]], k == 0, k == NKC - 1)
                    q = j % 2
                    P.op("act", lambda e, ps=pa[0], q=q: e.activation(out=xn[:, 0, q * 512:(q + 1) * 512], in_=ps[:], func=AF.Silu),
                         reads=[pa[1]], writes=[b_sil[q]])
                    P.op("dve", lambda e, ps=pu[0], q=q, j=j: e.tensor_tensor(out=gT[:, j, :], in0=ps[:],
                                                                             in1=xn[:, 0, q * 512:(q + 1) * 512], op=ALU.mult),
                         reads=[pu[1], b_sil[q]], writes=[b_gT[j]])
            P.op("dve", lambda e: e.memset(xn[:, 0, 0:2], 0.0), writes=[b_xn[0]] + b_sil)
            P.tag = "ffn_out"
            P.op("act", lambda e: e.activation(out=acts[:, 0:1], in_=one1[:, 0:1], func=AF.Sqrt), reads=[b_eps], writes=[b_acts])
            for d in range(NKC):
                slab, bsl = load_slab(l, f"f{i}_out{d}")
                ps, bps = next_ps()
                for j in range(NFC):
                    mm(ps[:], bps, slab[:, j * 128:(j + 1) * 128], gT[:, j, :], [bsl, b_gT[j]], j == 0, j == NFC - 1)
                P.op("dve", lambda e, ps=ps, d=d: e.scalar_tensor_tensor(out=xT[:, d, :], in0=ps[:], scalar=c_res,
                                                                         in1=xT[:, d, :], op0=ALU.mult, op1=ALU.add),
                     reads=[bps, b_xT[d]], writes=[b_xT[d]])
            layer_norm_T(l, f"ln_g{ln_idx}", f"ln_b{ln_idx}")

        eps_t = sb("eps_t", [128, 1], F32); b_eps = Buf("eps")
        P.op("dve", lambda e: e.memset(eps_t[:], LN_EPS / (DN_ALPHA * DN_ALPHA)), writes=[b_eps])
        eps1 = sb("eps1", [128, 1], F32)
        one1 = sb("one1", [128, 1], F32)
        acts = sb("acts", [128, 1], F32); b_acts = Buf("acts")
        mhalf = sb("mhalf", [128, 1], F32)
        P.op("dve", lambda e: e.memset(mhalf[:], -0.5), writes=[b_eps])
        P.op("dve", lambda e: e.memset(eps1[:], LN_EPS), writes=[b_eps])
        P.op("dve", lambda e: e.memset(one1[:], 1.0), writes=[b_eps])

        DS_PF = ["pf0", "pf1", "pf2", "pf3"]

        def prefetch_x(blk):
            for s_ in range(NSUB):
                r0 = blk * TB + s_ * 128
                P.op("pool", lambda e, s_=s_, r0=r0: e.dma_start(
                    out=S32[:, 2 * s_:2 * s_ + 2, :], in_=x_in[r0:r0 + 128, :].rearrange("p (a b) -> p a b", a=2)),
                    writes=[b_S32[2 * s_], b_S32[2 * s_ + 1]], dsem=DS_PF[s_])

        def consume_x(blk):
            P.tag = "io"
            tails = []
            for d in range(NKC):
                ps, bps = next_ps()
                for s_ in range(NSUB):
                    sl = 2 * s_ + d // 4
                    tr(ps[:, s_ * 128:(s_ + 1) * 128], bps, S32[:, sl, (d % 4) * 128:(d % 4 + 1) * 128], [b_S32[sl]])
                tails.append((ps, bps))
                if d % 2 == 0:
                    P.op("act", lambda e, ps=ps, d=d: e.activation(out=xTb[:, d, :], in_=ps[:], func=AF.Identity),
                         reads=[bps], writes=[b_xTb[d]])
                else:
                    P.op("dve", lambda e, ps=ps, d=d: e.tensor_copy(out=xTb[:, d, :], in_=ps[:]),
                         reads=[bps], writes=[b_xTb[d]])
            for d in range(NKC):
                ps, bps = tails[d]
                if d % 2 == 0:
                    P.op("act", lambda e, ps=ps, d=d: e.activation(out=xT[:, d, :], in_=ps[:], func=AF.Identity),
                         reads=[bps], writes=[b_xT[d]])
                else:
                    P.op("dve", lambda e, ps=ps, d=d: e.tensor_copy(out=xT[:, d, :], in_=ps[:]),
                         reads=[bps], writes=[b_xT[d]])

        def prefetch_xs(blk):
            P.op("pool", lambda e, blk=blk: e.dma_start(out=S32[:], in_=xs[blk]),
                 reads=[b_xs[blk]], writes=b_S32, dsem=DS_PF[0])

        def consume_xs(blk):
            P.tag = "io"
            for d in range(NKC):
                if d % 2 == 0:
                    P.op("act", lambda e, d=d: e.activation(out=xT[:, d, :], in_=S32[:, d, :], func=AF.Identity),
                         reads=[b_S32[d]], writes=[b_xT[d]])
                    P.op("dve", lambda e, d=d: e.tensor_copy(out=xTb[:, d, :], in_=S32[:, d, :]),
                         reads=[b_S32[d]], writes=[b_xTb[d]])
                else:
                    P.op("dve", lambda e, d=d: e.tensor_copy(out=xT[:, d, :], in_=S32[:, d, :]),
                         reads=[b_S32[d]], writes=[b_xT[d]])
                    P.op("act", lambda e, d=d: e.activation(out=xTb[:, d, :], in_=S32[:, d, :], func=AF.Identity),
                         reads=[b_S32[d]], writes=[b_xTb[d]])

        def store_block_to_out(blk):
            P.tag = "io"
            for s in range(NSUB):
                q = s
                for h in range(2):
                    ps, bps = next_ps()
                    for dd in range(4):
                        d = h * 4 + dd
                        tr(ps[:, dd * 128:(dd + 1) * 128], bps, xT[:, d, s * 128:(s + 1) * 128], [b_xT[d]])
                    eng = "act" if h == 0 else "dve"
                    if eng == "act":
                        P.op("act", lambda e, ps=ps, q=q, h=h: e.activation(out=xn[:, q, h * 512:(h + 1) * 512],
                                                                            in_=ps[:], func=AF.Identity),
                             reads=[bps], writes=[b_xn[q]])
                    else:
                        P.op("dve", lambda e, ps=ps, q=q, h=h: e.tensor_copy(out=xn[:, q, h * 512:(h + 1) * 512],
                                                                             in_=ps[:]),
                             reads=[bps], writes=[b_xn[q]])
                r0 = blk * TB + s * 128
                P.op("pool", lambda e, q=q, r0=r0: e.dma_start(out=out[r0:r0 + 128, :], in_=xn[:, q, :]),
                     reads=[b_xn[q]], dsem=DS_XIO[q])

        def store_block_xs(blk):
            P.op("pool", lambda e, blk=blk: e.dma_start(out=xs[blk], in_=xT[:]),
                 reads=b_xT, writes=[b_xs[blk]], dsem=DS_XS)

        cbt = sb("cbt", [128, NCB], BF16); b_cb = Buf("cb")
        cft = sb("cft", [128, NCF], F32); b_cf = Buf("cf")
        wsm = sb("wsm", [128, 4, 128], BF16); b_wsm = Buf("wsm")
        kTc = sb("kTc", [128, 2, S], BF16); b_kTc = Buf("kTc")
        Vc = sb("Vc", [128, S // 128, BW], BF16); b_Vc = Buf("Vc")
        B16 = sb("B16", [128, 8, TB], BF16); b_B16 = bufs("B16", 8)
        yT4 = sb("yT4", [128, 8, TB], BF16); b_yT4 = bufs("yT4", 8)
        qT = sb("qT", [128, 2, TB], BF16); b_qT = bufs("qT", 2)
        vln = sb("vln", [128, NSUB, BW], BF16); b_vln = bufs("vln", NSUB)
        ptok = sb("ptok", [128, NSUB + 1, BW], BF16); b_ptok = bufs("ptok", NSUB + 1)
        ybuf = sb("ybuf", [128, 2, 30 + TB], BF16); b_ybuf = bufs("ybuf", 2)
        dg = sb("dg", [128, 8, 128], BF16); b_dg = bufs("dg", 8)
        P.op("pool", lambda e: e.dma_start(out=cbt[:], in_=cb_in[:], max_dma_last_dim=8192), writes=[b_cb], dsem="cb")
        ntri = cbt[:, CB_NTRI:CB_NTRI + 128]
        nustr = cbt[:, CB_NUSTR:CB_NUSTR + 128]
        mergedT = gT
        b_merged = b_gT
        dg_ctr = [0]
        st_ctr = [0]

        def tok_stats_a(src_ap, src_buf, eps_ap):
            q = st_ctr[0] % 4
            st_ctr[0] += 1
            P.op("dve", lambda e: e.bn_stats(out=stt[:, q, 0:6], in_=src_ap), reads=[src_buf], writes=[b_stt[q]])
            P.op("dve", lambda e: e.bn_aggr(out=mv[:, q, :], in_=stt[:, q, 0:6]), reads=[b_stt[q]], writes=[b_mv[q]])
            P.op("act", lambda e: e.activation(out=rs[:, q, 0:1], in_=mv[:, q, 1:2], func=AF.Sqrt, bias=eps_ap, scale=1.0),
                 reads=[b_mv[q], b_eps], writes=[b_rs0[q]])
            return q

        def tok_stats_b(q):
            P.op("dve", lambda e: e.reciprocal(out=rs[:, q, 1:2], in_=rs[:, q, 0:1]), reads=[b_rs0[q]], writes=[b_rs1[q]])
            P.op("dve", lambda e: e.scalar_tensor_tensor(out=rs[:, q, 2:3], in0=mv[:, q, 0:1], scalar=-1.0,
                                                          in1=rs[:, q, 1:2], op0=ALU.mult, op1=ALU.mult),
                 reads=[b_rs1[q], b_mv[q]], writes=[b_rs[q]])
            return rs[:, q, 1:2], rs[:, q, 2:3], [b_rs[q], b_rs1[q]]

        def layer_setup(l):
            o = l * NCF
            P.op("pool", lambda e: e.dma_start(out=cft[:], in_=cf_in[:, o:o + NCF]), writes=[b_cf], dsem="cf")
            for g in range(4):
                P.op("dve", lambda e, g=g: e.tensor_tensor(out=wsm[:, g, :], in0=cft[:, CF_SGW + g * 128:CF_SGW + (g + 1) * 128],
                                                          in1=cbt[:, CB_TRILE:CB_TRILE + 128], op=ALU.mult),
                     reads=[b_cf, b_cb], writes=[b_wsm])
            for c in range(2):
                P.op("dve", lambda e, c=c: e.memset(ybuf[:, c, 0:30], 0.0), writes=[b_ybuf[c]])

        def mixer_phase(l, blk):
            t0 = blk * TB
            nt0 = blk * NSUB
            if blk == 0 and l > 0:
                layer_setup(l)
            P.tag = "mix_qk"
            slab, bsl = load_slab(l, "m_qk")
            for cc in range(4):
                ps, bps = next_ps()
                for k in range(NKC):
                    o = (cc * NKC + k) * 128
                    mm(ps[:], bps, slab[:, o:o + 128], xTb[:, k, :], [bsl, b_xTb[k]], k == 0, k == NKC - 1)
                if cc < 2:
                    P.op("act", lambda e, ps=ps, cc=cc: e.activation(out=qT[:, cc, :], in_=ps[:], func=AF.Identity),
                         reads=[bps], writes=[b_qT[cc]])
                else:
                    P.op("dve", lambda e, ps=ps, cc=cc: e.tensor_copy(out=kTc[:, cc - 2, t0:t0 + TB], in_=ps[:]),
                         reads=[bps], writes=[b_kTc])
            P.tag = "mix_tok"
            slab, bsl = load_slab(l, "m_tok")
            slabp, bslp = load_slab(l, "m_pool")
            slabw, bslw = load_slab(l, "m_poolw")
            for s in range(NSUB):
                ps, bps = next_ps()
                for k in range(NKC):
                    mm(ps[:], bps, xTb[:, k, s * 128:(s + 1) * 128], slab[:, k * 512:(k + 1) * 512],
                       [bsl, b_xTb[k]], k == 0, k == NKC - 1)
                P.op("dve", lambda e, ps=ps, s=s: e.tensor_copy(out=Vc[:, nt0 + s, :], in_=ps[:, 0:256]),
                     reads=[bps], writes=[b_Vc])
                P.op("act", lambda e, ps=ps, s=s: e.activation(out=xn[:, s, 256:512], in_=ps[:, 256:512],
                                                               func=AF.Gelu_apprx_tanh),
                     reads=[bps], writes=[b_xn[s]])
                ps2, bps2 = next_ps()
                for k in range(NKC):
                    mm(ps2[:, 0:256], bps2, xTb[:, k, s * 128:(s + 1) * 128], slabp[:, k * 256:(k + 1) * 256],
                       [bslp, b_xTb[k]], k == 0, k == NKC - 1)
                P.op("act", lambda e, ps2=ps2, s=s: e.activation(out=ptok[:, s + 1, :], in_=ps2[:, 0:256], func=AF.Identity),
                     reads=[bps2], writes=[b_ptok[s + 1]])
            P.tag = "mix_sgln"
            qs = [tok_stats_a(xn[:, s, 256:512], b_xn[s], eps1[:, 0:1]) for s in range(NSUB)]
            for s in range(NSUB):
                rstd, nmr, brs = tok_stats_b(qs[s])
                P.op("act", lambda e, s=s, rstd=rstd, nmr=nmr: e.activation(out=xn[:, s, 512:768], in_=xn[:, s, 256:512],
                                                                            func=AF.Identity, bias=nmr, scale=rstd),
                     reads=[b_xn[s]] + brs, writes=[b_xn[s]])
                P.op("dve", lambda e, s=s: e.tensor_tensor(out=xn[:, s, 512:768], in0=xn[:, s, 512:768],
                                                           in1=cft[:, CF_SGG:CF_SGG + 256], op=ALU.mult),
                     reads=[b_xn[s], b_cf], writes=[b_xn[s]])
                P.op("dve", lambda e, s=s: e.tensor_tensor(out=vln[:, s, :], in0=xn[:, s, 512:768],
                                                           in1=cft[:, CF_SGBE:CF_SGBE + 256], op=ALU.add),
                     reads=[b_xn[s], b_cf], writes=[b_vln[s]])
            P.tag = "mix_u"
            slab, bsl = load_slab(l, "m_ua")
            slabg, bslg = load_slab(l, "m_g")
            for c in range(2):
                ps, bps = next_ps()
                for k in range(NKC):
                    o = (c * NKC + k) * 128
                    mm(ps[:], bps, slab[:, o:o + 128], xTb[:, k, :], [bsl, b_xTb[k]], k == 0, k == NKC - 1)
                P.op("act", lambda e, ps=ps, c=c: e.activation(out=S32[:, 2 + c, :], in_=ps[:], func=AF.Gelu_apprx_tanh),
                     reads=[bps], writes=[b_S32[2 + c]])
            P.tag = "mix_conv"
            for c in range(2):
                psa, bpsa = next_ps()
                for k in range(NKC):
                    o = ((2 + c) * NKC + k) * 128
                    mm(psa[:], bpsa, slab[:, o:o + 128], xTb[:, k, :], [bsl, b_xTb[k]], k == 0, k == NKC - 1)
                psg, bpsg = next_ps()
                for k in range(NKC):
                    o = (c * NKC + k) * 128
                    mm(psg[:], bpsg, slabg[:, o:o + 128], xTb[:, k, :], [bslg, b_xTb[k]], k == 0, k == NKC - 1)
                P.op("act", lambda e, psg=psg: e.activation(out=S32[:, 0, :], in_=psg[:], func=AF.Sigmoid),
                     reads=[bpsg], writes=[b_S32[0]])
                P.op("dve", lambda e, psa=psa, c=c: e.tensor_tensor(out=ybuf[:, c, 30:30 + TB], in0=psa[:], in1=S32[:, 0, :],
                                                                    op=ALU.mult),
                     reads=[bpsa, b_S32[0]], writes=[b_ybuf[c]])
            for c in range(2):
                ps, bps = next_ps()
                for k in range(CONV_W):
                    q = dg_ctr[0] % 8
                    dg_ctr[0] += 1
                    wc = vcol(l, "conv_w", c * 31 + k)
                    if k % 2 == 0:
                        P.op("dve", lambda e, q=q, wc=wc: e.tensor_scalar(out=dg[:, q, :], in0=ident[:], scalar1=wc, scalar2=None,
                                                                          op0=ALU.mult),
                             reads=[b_ident, b_vecs], writes=[b_dg[q]])
                    else:
                        P.op("act", lambda e, q=q, wc=wc: e.activation(out=dg[:, q, :], in_=ident[:], func=AF.Identity, scale=wc),
                             reads=[b_ident, b_vecs], writes=[b_dg[q]])
                    mm(ps[:], bps, dg[:, q, :], ybuf[:, c, k:k + TB], [b_dg[q], b_ybuf[c]], k == 0, k == CONV_W - 1)
                cb_ = vcol(l, "conv_b", c)
                P.op("act", lambda e, ps=ps, c=c, cb_=cb_: e.activation(out=S32[:, 6 + c, :], in_=ps[:], func=AF.Identity,
                                                                         bias=cb_, scale=1.0),
                     reads=[bps, b_vecs], writes=[b_S32[6 + c]])
                P.op("dve", lambda e, c=c: e.tensor_copy(out=ybuf[:, c, 0:30], in_=ybuf[:, c, TB:TB + 30]),
                     reads=[b_ybuf[c]], writes=[b_ybuf[c]])
            cps = []
            for s in range(NSUB):
                ps, bps = next_ps(range(4, 8))
                for c in range(2):
                    tr(ps[:, c * 128:(c + 1) * 128], bps, S32[:, 6 + c, s * 128:(s + 1) * 128], [b_S32[6 + c]])
                cps.append((ps, bps, tok_stats_a(ps[:, 0:256], bps, eps1[:, 0:1])))
            for s in range(NSUB):
                ps, bps, q = cps[s]
                rstd, nmr, brs = tok_stats_b(q)
                P.op("act", lambda e, ps=ps, s=s, rstd=rstd, nmr=nmr: e.activation(out=xn[:, s, 0:256], in_=ps[:, 0:256],
                                                                                   func=AF.Identity, bias=nmr, scale=rstd),
                     reads=[bps] + brs, writes=[b_xn[s]])
            P.tag = "mix_pool"
            for c in range(2):
                for par in range(2):
                    g = 2 * c + par
                    r0 = par * 64
                    ps, bps = next_ps(range(4))
                    for s in range(NSUB):
                        osl = ps[:, s * 128:(s + 1) * 128]
                        cur = ptok[:, s + 1, c * 128:(c + 1) * 128]
                        if blk == 0 and s == 0:
                            mm(osl, bps, cur, cbt[:, CB_BAND0 + g * 128:CB_BAND0 + (g + 1) * 128],
                               [b_ptok[s + 1], b_cb], True, True)
                        else:
                            mm(osl, bps, cur, cbt[:, CB_BAND + g * 128:CB_BAND + (g + 1) * 128],
                               [b_ptok[s + 1], b_cb], True, False)
                            mm(osl, bps, ptok[:, s, c * 128:(c + 1) * 128],
                               cbt[:, CB_BANDP + g * 128:CB_BANDP + (g + 1) * 128], [b_ptok[s], b_cb], False, True)
                    P.op("act", lambda e, ps=ps, c=c, r0=r0: e.activation(out=B16[r0:r0 + 64, 4 + c, :], in_=ps[r0:r0 + 64, :],
                                                                          func=AF.Identity),
                         reads=[bps], writes=[b_B16[4 + c]])
            P.op("dve", lambda e: e.tensor_copy(out=ptok[:, 0, :], in_=ptok[:, NSUB, :]), reads=[b_ptok[NSUB]],
                 writes=[b_ptok[0]])
            for c in range(2):
                ps, bps = next_ps(range(4))
                mm(ps[:], bps, slabw[:, c * 128:(c + 1) * 128], B16[:, 4 + c, :], [bslw, b_B16[4 + c]], True, True)
                sc_ = vcol(l, "pool_scale", c)
                P.op("act", lambda e, ps=ps, c=c, sc_=sc_: e.activation(out=yT4[:, 4 + c, :], in_=ps[:], func=AF.Identity,
                                                                        scale=sc_),
                     reads=[bps, b_vecs], writes=[b_yT4[4 + c]])
            P.tag = "mix_sgmix"
            for c in range(2):
                for par in range(2):
                    g = 2 * c + par
                    r0 = par * 64
                    ps, bps = next_ps(range(4))
                    for s in range(NSUB):
                        mm(ps[:, s * 128:(s + 1) * 128], bps, vln[:, s, c * 128:(c + 1) * 128], wsm[:, g, :],
                           [b_vln[s], b_wsm], True, True)
                    P.op("dve", lambda e, ps=ps, c=c, r0=r0: e.tensor_tensor(
                        out=S32[r0:r0 + 64, 4, :], in0=ps[r0:r0 + 64, :],
                        in1=cft[r0:r0 + 64, CF_SGB + c * 512:CF_SGB + (c + 1) * 512], op=ALU.add),
                        reads=[bps, b_cf], writes=[b_S32[4]])
                    P.op("dve", lambda e, c=c, r0=r0: e.tensor_tensor(
                        out=yT4[r0:r0 + 64, 2 + c, :], in0=S32[r0:r0 + 64, 4, :], in1=S32[r0:r0 + 64, 2 + c, :], op=ALU.mult),
                        reads=[b_S32[4], b_S32[2 + c]], writes=[b_yT4[2 + c]])
            P.tag = "mix_conv"
            for c in range(2):
                ps, bps = next_ps(range(4))
                for s in range(NSUB):
                    tr(ps[:, s * 128:(s + 1) * 128], bps, xn[:, s, c * 128:(c + 1) * 128], [b_xn[s]])
                g_ = vcol(l, "conv_ln_g", c)
                b_ = vcol(l, "conv_ln_b", c)
                P.op("act", lambda e, ps=ps, c=c, g_=g_, b_=b_: e.activation(out=yT4[:, 6 + c, :], in_=ps[:], func=AF.Silu,
                                                                             bias=b_, scale=g_),
                     reads=[bps, b_vecs], writes=[b_yT4[6 + c]])
            P.tag = "mix_attn"
            npair = nt0 + NSUB
            a_hi = npair - 1
            b_xnh = [[Buf(f"xnh{s_}{hh}") for hh in range(2)] for s_ in range(NSUB)]
            for s_ in range(NSUB):
                P.op("dve", lambda e, s_=s_: e.memset(xn[:, s_, 0:2], 0.0), writes=[b_xn[s_]] + b_xnh[s_])
            ebuf = []
            for h in range(4):
                ebuf.append([(xn[:, h, 0:512], b_xnh[h][0]), (xn[:, h, 512:1024], b_xnh[h][1]), (S32[:, h, :], b_S32[h])])
            Ebuf = [(S32[:, 4 + h, :], b_S32[4 + h]) for h in range(4)]
            spbuf = [[(gT[:, 8 + h * 3 + i, :], b_gT[8 + h * 3 + i]) for i in range(3)] for h in range(4)]
            wbuf = [[(B16[:, h * 2 + i, :], b_B16[h * 2 + i]) for i in range(2)] for h in range(4)]
            actr = [0]

            def pinfo(p):
                a = a_hi - p
                dd = a - nt0
                c0 = 128 * max(dd, 0)
                return a, dd, c0

            for t in range(npair + 2):
                if t < npair:
                    a, dd, c0 = pinfo(t)
                    pend_ln = None
                    for h in range(4):
                        c = h // 2
                        r0 = (h % 2) * 64
                        psA, bA = psum[actr[0] % 2], b_ps[actr[0] % 2]
                        actr[0] += 1
                        mm(psA[:, c0:TB], bA, kTc[r0:r0 + 64, c, a * 128:(a + 1) * 128], qT[r0:r0 + 64, c, c0:TB],
                           [b_kTc, b_qT[c]], True, True)
                        eap, eb = ebuf[h][t % 3]
                        P.op("act", lambda e, psA=psA, eap=eap, c0=c0: e.activation(out=eap[:, c0:TB], in_=psA[:, c0:TB],
                                                                                  func=AF.Exp, scale=0.125),
                             reads=[bA], writes=[eb])
                        if dd >= 0:
                            P.op("dve", lambda e, eap=eap, dd=dd, c0=c0: e.tensor_tensor(
                                out=eap[:, c0:TB], in0=eap[:, c0:TB], in1=cbt[:, CB_MASK + dd * 512 + c0:CB_MASK + (dd + 1) * 512],
                                op=ALU.mult), reads=[eb, b_cb], writes=[eb])
                        if pend_ln is not None:
                            pend_ln()
                        sap, sbf = spbuf[h][t % 3]

                        def _ln(eap=eap, eb=eb, sap=sap, sbf=sbf, c0=c0):
                            P.op("act", lambda e: e.activation(out=sap[:, c0:TB], in_=eap[:, c0:TB], func=AF.Ln, bias=1.0, scale=1.0),
                                 reads=[eb], writes=[sbf])
                        pend_ln = _ln
                    pend_ln()
                if t >= 2:
                    p = t - 2
                    a, dd, c0 = pinfo(p)
                    for h in range(4):
                        c = h // 2
                        r0 = (h % 2) * 64
                        sap, sbf = spbuf[h][p % 3]
                        wap, wbf = wbuf[h][p % 2]
                        if p < npair - 1:
                            mm(psum[2 + h][:, c0:TB], b_ps[2 + h], nustr, sap[:, c0:TB], [b_cb, sbf], False, True, skip=True)
                        mm(psum[6 + c][r0:r0 + 64, c0:TB], b_ps[6 + c], Vc[:, a, h * 64:(h + 1) * 64], wap[:, c0:TB],
                           [b_Vc, wbf], p == 0, True, skip=(p > 0))
                if 1 <= t <= npair:
                    p = t - 1
                    a, dd, c0 = pinfo(p)
                    for h in range(4):
                        sap, sbf = spbuf[h][p % 3]
                        eap, eb = ebuf[h][p % 3]
                        Eap, Ebf = Ebuf[h]
                        wap, wbf = wbuf[h][p % 2]
                        psB, bB = psum[2 + h], b_ps[2 + h]
                        mm(psB[:, c0:TB], bB, ntri, sap[:, c0:TB], [b_cb, sbf], p == 0, True, skip=(p > 0))
                        P.op("act", lambda e, psB=psB, Eap=Eap, c0=c0: e.activation(out=Eap[:, c0:TB], in_=psB[:, c0:TB], func=AF.Exp),
                             reads=[bB], writes=[Ebf])
                        P.op("dve", lambda e, wap=wap, eap=eap, Eap=Eap, c0=c0: e.tensor_tensor(
                            out=wap[:, c0:TB], in0=eap[:, c0:TB], in1=Eap[:, c0:TB], op=ALU.mult),
                            reads=[eb, Ebf], writes=[wbf])
            for c in range(2):
                P.op("act", lambda e, c=c: e.activation(out=yT4[:, c, :], in_=psum[6 + c][:], func=AF.Identity),
                     reads=[b_ps[6 + c]], writes=[b_yT4[c]])
            for s_ in range(NSUB):
                P.op("dve", lambda e, s_=s_: e.memset(xn[:, s_, 0:2], 0.0), writes=[b_xn[s_]] + b_xnh[s_])
            P.tag = "mix_gates"
            for d in range(NKC):
                slabg_, bslg_ = load_slab(l, f"m_gate{d}")
                if d % 4 == 0:
                    slabb, bslb = load_slab(l, f"m_br{d // 4}")
                for ni, n in enumerate((1, 2, 3, 0)):
                    psg, bpsg = next_ps()
                    for k in range(NKC):
                        o = (n * NKC + k) * 128
                        mm(psg[:], bpsg, slabg_[:, o:o + 128], xTb[:, k, :], [bslg_, b_xTb[k]], k == 0, k == NKC - 1)
                    gb = vcol(l, f"gate_b{n}", d)
                    gi = n % 2
                    P.op("act", lambda e, psg=psg, gi=gi, gb=gb: e.activation(out=S32[:, gi, :], in_=psg[:], func=AF.Sigmoid,
                                                                             bias=gb, scale=1.0),
                         reads=[bpsg, b_vecs], writes=[b_S32[gi]])
                    psb, bpsb = next_ps()
                    for kk in range(2):
                        o = (((d % 4) * 4 + n) * 2 + kk) * 128
                        mm(psb[:], bpsb, slabb[:, o:o + 128], yT4[:, 2 * n + kk, :], [bslb, b_yT4[2 * n + kk]], kk == 0, kk == 1)
                    ai = 4 + d % 2
                    if ni == 0:
                        P.op("dve", lambda e, psb=psb, gi=gi, ai=ai: e.tensor_tensor(out=S32[:, ai, :], in0=psb[:], in1=S32[:, gi, :],
                                                                                    op=ALU.mult),
                             reads=[bpsb, b_S32[gi]], writes=[b_S32[ai]])
                    else:
                        ti = 2 + n % 2
                        P.op("dve", lambda e, psb=psb, gi=gi, ti=ti: e.tensor_tensor(out=S32[:, ti, :], in0=psb[:], in1=S32[:, gi, :],
                                                                                    op=ALU.mult),
                             reads=[bpsb, b_S32[gi]], writes=[b_S32[ti]])
                        if ni < 3:
                            P.op("dve", lambda e, ti=ti, ai=ai: e.tensor_tensor(out=S32[:, ai, :], in0=S32[:, ai, :], in1=S32[:, ti, :],
                                                                               op=ALU.add),
                                 reads=[b_S32[ai], b_S32[ti]], writes=[b_S32[ai]])
                        else:
                            P.op("dve", lambda e, ti=ti, ai=ai, d=d: e.tensor_tensor(out=mergedT[:, d, :], in0=S32[:, ai, :],
                                                                                    in1=S32[:, ti, :], op=ALU.add),
                                 reads=[b_S32[ai], b_S32[ti]], writes=[b_merged[d]])
            P.tag = "mix_out"
            P.op("act", lambda e: e.activation(out=acts[:, 0:1], in_=one1[:, 0:1], func=AF.Sqrt), reads=[b_eps], writes=[b_acts])
            c_res = 1.0 / DN_ALPHA
            for d in range(NKC):
                if d % 4 == 0:
                    slabo, bslo = load_slab(l, f"m_out{d // 4}")
                ps, bps = next_ps()
                for k in range(NKC):
                    o = ((d % 4) * NKC + k) * 128
                    mm(ps[:], bps, slabo[:, o:o + 128], mergedT[:, k, :], [bslo, b_merged[k]], k == 0, k == NKC - 1)
                P.op("dve", lambda e, ps=ps, d=d: e.scalar_tensor_tensor(out=xT[:, d, :], in0=ps[:], scalar=c_res,
                                                                         in1=xT[:, d, :], op0=ALU.mult, op1=ALU.add),
                     reads=[bps, b_xT[d]], writes=[b_xT[d]])
            layer_norm_T(l, "ln_g1", "ln_b1")

        def names_of(ph):
            if ph == "f0":
                return [n for n, _ in SLABS if n.startswith("f0_")]
            if ph == "f1":
                return [n for n, _ in SLABS if n.startswith("f1_")]
            return [n for n, _ in SLABS if n.startswith("m_")]

        def prepass_names(l, names):
            for nm in names:
                for i, (ll, name, sz) in enumerate(pre_list):
                    if ll == l and name == nm:
                        pre_list.insert(0, pre_list.pop(i))
                        emit_prepass(1)
                        break

        phase_fn = {"f0": lambda l, blk: ffn_phase(l, 0, 0), "mix": mixer_phase, "f1": lambda l, blk: ffn_phase(l, 1, 2)}
        order = [ph for ph in ("f0", "mix", "f1") if ph in cfg.phases]
        seq = [(l, blk) for l in range(cfg.depth) for blk in range(NBLK)]

        def prefetch(l, blk):
            if l == 0:
                prefetch_x(blk)
            else:
                prefetch_xs(blk)

        b_xs = bufs("xs", NBLK)
        for idx, (l, blk) in enumerate(seq):
            P.tag = "io"
            if idx == 0:
                if order:
                    prepass_names(0, names_of(order[0])[:4])
                prefetch(l, blk)
                if "mix" in cfg.phases:
                    layer_setup(0)
                emit_prepass(sum(1 for (ll, _n, _s) in pre_list if ll == 0))
            if l > 0 and blk == 0:
                emit_prepass(10 ** 6)
            if l == 0:
                consume_x(blk)
            else:
                consume_xs(blk)
            for pi, ph in enumerate(order):
                P.tag = "io"
                if l == 0 and blk == 0:
                    prepass_names(0, names_of(ph))
                elif l == 0 and cfg.depth > 1:
                    emit_prepass(3)
                if pi == len(order) - 1 and ph != "mix" and idx + 1 < len(seq):
                    prefetch(*seq[idx + 1])
                phase_fn[ph](l, blk)
            if not (order and order[-1] != "mix") and idx + 1 < len(seq):
                prefetch(*seq[idx + 1])
            if l == cfg.depth - 1:
                store_block_to_out(blk)
            else:
                store_block_xs(blk)

        fin = sb("fin", [128, 1], F32)
        b_fin = Buf("fin")
        last_out_ops = [o for o in P.ops["pool"] if o.dsem in DS_XIO]
        tail = P.op("pool", lambda e: e.memset(fin[:], 0.0), writes=[b_fin])
        for q in range(4):
            lo = [o for o in last_out_ops if o.dsem == DS_XIO[q]]
            if lo:
                tail.deps.append(lo[-1])

        P.finalize()
        if getattr(cfg, "dbg", None) is not None:
            P.dbg = cfg.dbg
        if getattr(cfg, "pe_tags", None) is not None:
            cfg.pe_tags.extend(o.tag for o in P.ops["pe"])
        dsem_names = [DS_CONST, "cb", "cf"] + DS_RING + DS_PRE + DS_XIO + [DS_XS] + DS_PF
        esems = {e: es.enter_context(nc.semaphore(f"e_{e}")) for e in Prog.ENGS}
        dsems = {n: es.enter_context(nc.semaphore(f"d_{n}")) for n in dsem_names}
        block = es.enter_context(nc.Block())

        @block.tensor
        def _(e):
            P.emit("pe", e, esems, dsems)

        @block.scalar
        def _(e):
            P.emit("act", e, esems, dsems)

        @block.vector
        def _(e):
            P.emit("dve", e, esems, dsems)

        @block.gpsimd
        def _(e):
            P.emit("pool", e, esems, dsems)

        @block.sync
        def _(e):
            P.emit("sp", e, esems, dsems)

    return nc


_PROGRAM_CACHE = {}


def kernel(**inputs):
    inp = {k: np.asarray(v) for k, v in inputs.items()}
    x = inp["x"].astype(np.float32, copy=False)
    B = x.shape[0]
    W = pack_weights(inp)
    V = pack_vecs(inp)
    C = pack_consts(inp)
    cfg = Cfg()
    nc = build_program(cfg)
    in_maps = [dict(x=np.ascontiguousarray(x[b]), w=W, vecs=V, **C) for b in range(B)]
    res = run_bass_kernel_spmd(nc, in_maps, core_ids=list(range(B)))
    return np.stack([np.asarray(r["out"]) for r in res.results], axis=0).astype(np.float32)
```

```python
import math
from contextlib import ExitStack

import numpy as np

import concourse.bass as bass
import concourse.mybir as mybir
from concourse.bass_utils import run_bass_kernel_spmd

F32 = mybir.dt.float32
BF16 = mybir.dt.bfloat16
AF = mybir.ActivationFunctionType
ALU = mybir.AluOpType

D_MODEL = 1024
SEQ = 4096
DEPTH = 2
BW = 256
HEAD_DIM = 64
D_FF = 2816
CONV_W = 31
LN_EPS = 1e-5
DN_ALPHA = (2.0 * DEPTH) ** 0.25
POOL_WINDOWS = (2, 4, 8, 16)

TB = 512
NSUB = TB // 128
NKC = D_MODEL // 128
NFC = D_FF // 128
SLOT = 4096
NSLOT = 6


class Buf:
    __slots__ = ("name", "w", "r", "excl")

    def __init__(self, name, excl=False):
        self.name = name
        self.w = None
        self.r = {}
        self.excl = excl


class Op:
    __slots__ = ("eng", "fn", "deps", "signal", "sigval", "dsem", "key", "tag")


class Prog:
    ENGS = ("pe", "act", "dve", "pool", "sp")

    def __init__(self):
        self.ops = {e: [] for e in self.ENGS}
        self.dsem_count = {}
        self.last_dma = {}
        self.tag = ""

    def op(self, eng, fn, reads=(), writes=(), dsem=None):
        o = Op()
        o.eng = eng
        o.fn = fn
        o.signal = False
        o.sigval = None
        o.dsem = dsem
        o.tag = self.tag
        o.key = ("d", dsem) if dsem is not None else ("e", eng)
        deps = {}
        for b in reads:
            if b.w is not None:
                deps[id(b.w)] = b.w
            if b.excl:
                for r in b.r.values():
                    if r.key != o.key:
                        deps[id(r)] = r
        for b in writes:
            if b.w is not None:
                deps[id(b.w)] = b.w
            for r in b.r.values():
                deps[id(r)] = r
        if dsem is not None:
            pd = self.last_dma.get(dsem)
            if pd is not None:
                deps[id(pd)] = pd
            self.last_dma[dsem] = o
        dl = []
        for d in deps.values():
            if d is o:
                continue
            if d.dsem is None and d.eng == "pe" and eng == "pe" and dsem is None:
                continue
            d.signal = True
            dl.append(d)
        o.deps = dl
        for b in reads:
            b.r[o.key] = o
        for b in writes:
            b.w = o
            b.r = {}
        self.ops[eng].append(o)
        return o

    def finalize(self):
        cnt = {e: 0 for e in self.ENGS}
        for e in self.ENGS:
            for o in self.ops[e]:
                if o.dsem is not None:
                    c = self.dsem_count.get(o.dsem, 0) + 16
                    self.dsem_count[o.dsem] = c
                    o.sigval = c
                elif o.signal:
                    cnt[e] += 1
                    o.sigval = cnt[e]
        return cnt

    def emit(self, eng, handle, esems, dsems):
        seen = {}
        dbg = getattr(self, "dbg", None)
        for o in self.ops[eng]:
            if dbg is not None:
                dbg.append((eng, o.fn.__code__.co_firstlineno, [(d.key, d.sigval) for d in o.deps], o.key, o.sigval))
            need = {}
            for d in o.deps:
                k = d.key
                if d.sigval > need.get(k, 0):
                    need[k] = d.sigval
            for k, v in need.items():
                if v > seen.get(k, 0):
                    seen[k] = v
                    sem = dsems[k[1]] if k[0] == "d" else esems[k[1]]
                    handle.wait_ge(sem, v)
            ins = o.fn(handle)
            if o.dsem is not None:
                ins.then_inc(dsems[o.dsem], 16)
            elif o.signal:
                ins.then_inc(esems[eng], 1)


def _slab_list():
    sl = []
    for i in (0,):
        pass
    def ffn(i):
        for j2 in range(NFC // 2):
            sl.append((f"f{i}_in{j2}", 4096))
        for d in range(NKC):
            sl.append((f"f{i}_out{d}", NFC * 128))
    ffn(0)
    sl.append(("m_qk", 4096))
    sl.append(("m_tok", 4096))
    sl.append(("m_pool", 2048))
    sl.append(("m_poolw", 256))
    sl.append(("m_ua", 4096))
    sl.append(("m_g", 2048))
    for d in range(NKC):
        sl.append((f"m_gate{d}", 4096))
        if d % 4 == 0:
            sl.append((f"m_br{d // 4}", 4096))
    sl.append(("m_out0", 4096))
    sl.append(("m_out1", 4096))
    ffn(1)
    return sl


SLABS = _slab_list()
SLAB_OFF = {}
_o = 0
for _n, _sz in SLABS:
    SLAB_OFF[_n] = (_o, _sz)
    _o += _sz
LAYER_W = _o


def _kc(w, cols):
    k = w.shape[0] // 128
    nch = len(cols) // 128
    a = w[:, cols].reshape(k, 128, nch, 128)
    return a.transpose(1, 2, 0, 3)


def pack_weights(inp):
    W = np.empty((128, DEPTH * LAYER_W), np.float32)
    for l in range(DEPTH):
        def put(name, arr):
            off, sz = SLAB_OFF[name]
            a = np.ascontiguousarray(arr).reshape(128, -1)
            assert a.shape[1] == sz, (name, a.shape, sz)
            W[:, l * LAYER_W + off: l * LAYER_W + off + sz] = a
        for i in range(2):
            w_in = inp["ffn_w_in"][l, i]
            w4 = w_in.reshape(NKC, 128, 2, NFC, 128)
            for j2 in range(NFC // 2):
                a = w4[:, :, :, 2 * j2:2 * j2 + 2, :]
                put(f"f{i}_in{j2}", a.transpose(1, 3, 2, 0, 4))
            w_out = inp["ffn_w_out"][l, i].reshape(NFC, 128, NKC, 128)
            for d in range(NKC):
                put(f"f{i}_out{d}", w_out[:, :, d, :].transpose(1, 0, 2))
        mw = inp["mix_w_in"][l]
        put("m_qk", _kc(mw, np.arange(0, 512)))
        tokc = np.concatenate([np.arange(512, 768), np.arange(1024, 1280)])
        put("m_tok", mw[:, tokc].reshape(NKC, 128, 512).transpose(1, 0, 2))
        put("m_pool", mw[:, 1280:1536].reshape(NKC, 128, 256).transpose(1, 0, 2))
        pw = np.zeros((128, 2, 128), np.float32)
        for g in range(4):
            r = (g % 2) * 64
            pw[r:r + 64, g // 2, r:r + 64] = inp["pool_w"][l, g]
        put("m_poolw", pw)
        put("m_ua", _kc(mw, np.concatenate([np.arange(768, 1024), np.arange(1536, 1792)])))
        put("m_g", _kc(mw, np.arange(1792, 2048)))
        gw = inp["gate_w"][l].reshape(4, NKC, 128, NKC, 128)
        for d in range(NKC):
            put(f"m_gate{d}", gw[:, :, :, d, :].transpose(2, 0, 1, 3))
        bw = inp["branch_w"][l].reshape(4, 2, 128, NKC, 128)
        for h in range(2):
            put(f"m_br{h}", bw[:, :, :, 4 * h:4 * h + 4, :].transpose(2, 3, 0, 1, 4))
        ow = inp["out_w"][l].reshape(NKC, 128, NKC, 128)
        for h in range(2):
            put(f"m_out{h}", ow[:, :, 4 * h:4 * h + 4, :].transpose(1, 2, 0, 3))
    return W


VEC_COLS = {}
_c = 0
def _vc(name, n):
    global _c
    VEC_COLS[name] = _c
    _c += n
for _i in range(3):
    _vc(f"ln_g{_i}", 8); _vc(f"ln_b{_i}", 8)
for _n in range(4):
    _vc(f"gate_b{_n}", 8)
_vc("pool_scale", 2); _vc("conv_b", 2); _vc("conv_ln_g", 2); _vc("conv_ln_b", 2)
_vc("conv_w", 62)
NVEC = _c


def pack_vecs(inp):
    V = np.zeros((128, DEPTH * NVEC), np.float32)
    for l in range(DEPTH):
        def put(name, v):
            c0 = l * NVEC + VEC_COLS[name]
            a = np.asarray(v).reshape(-1, 128).T
            V[:, c0:c0 + a.shape[1]] = a
        for i in range(3):
            put(f"ln_g{i}", inp["ln_g"][l, i]); put(f"ln_b{i}", inp["ln_b"][l, i])
        for n in range(4):
            put(f"gate_b{n}", inp["gate_b"][l, n])
        put("pool_scale", inp["pool_scale"][l]); put("conv_b", inp["conv_b"][l])
        put("conv_ln_g", inp["conv_ln_g"][l]); put("conv_ln_b", inp["conv_ln_b"][l])
        cw = inp["conv_w"][l]
        c0 = l * NVEC + VEC_COLS["conv_w"]
        for c in range(2):
            V[:, c0 + c * 31: c0 + (c + 1) * 31] = cw[:, c * 128:(c + 1) * 128].T
    return V


CB_NTRI, CB_NUSTR, CB_TRILE = 0, 128, 256
CB_BAND, CB_BANDP, CB_BAND0 = 384, 896, 1408
CB_MASK = 1920
NCB = CB_MASK + 2048
CF_SGW, CF_SGB, CF_SGG, CF_SGBE = 0, 512, 1536, 1792
NCF = 2048


def pack_consts(inp=None):
    c = {}
    c["ident"] = np.eye(128, dtype=np.float32)
    j = np.arange(128)[:, None]
    t = np.arange(128)[None, :]
    cb = np.zeros((128, NCB), np.float32)
    cb[:, CB_NTRI:CB_NTRI + 128] = -1.0 * (j >= t)
    cb[:, CB_NUSTR:CB_NUSTR + 128] = -1.0 * (j < t)
    cb[:, CB_TRILE:CB_TRILE + 128] = (j <= t)
    for g, win in enumerate(POOL_WINDOWS):
        band = ((j <= t) & (j > t - win)).astype(np.float32) / win - (j == t)
        cnt = np.minimum(t + 1, win).astype(np.float32)
        band0 = ((j <= t) & (j > t - win)).astype(np.float32) / cnt - (j == t)
        bandp = ((j + 0 - 128 > t - win)).astype(np.float32) / win
        cb[:, CB_BAND + g * 128:CB_BAND + (g + 1) * 128] = band
        cb[:, CB_BANDP + g * 128:CB_BANDP + (g + 1) * 128] = bandp
        cb[:, CB_BAND0 + g * 128:CB_BAND0 + (g + 1) * 128] = band0
    col = np.arange(512)[None, :]
    for d in range(4):
        cb[:, CB_MASK + d * 512:CB_MASK + (d + 1) * 512] = (col > 128 * d + j)
    c["cb"] = cb
    if inp is not None:
        cf = np.zeros((128, DEPTH * NCF), np.float32)
        for l in range(DEPTH):
            o = l * NCF
            sgw = inp["sg_w"][l]
            cf[:, o + CF_SGW:o + CF_SGW + 512] = sgw.transpose(2, 0, 1).reshape(128, 512)
            sgb = inp["sg_b"][l]
            rep = np.empty((128, 2, 4, 128), np.float32)
            for cc in range(2):
                rep[0:64, cc] = sgb[2 * cc][None, None, :]
                rep[64:128, cc] = sgb[2 * cc + 1][None, None, :]
            cf[:, o + CF_SGB:o + CF_SGB + 1024] = rep.reshape(128, 1024)
            cf[:, o + CF_SGG:o + CF_SGG + 256] = inp["sg_ln_g"][l][None, :]
            cf[:, o + CF_SGBE:o + CF_SGBE + 256] = inp["sg_ln_b"][l][None, :]
        c["cf"] = cf
    return c


class Cfg:
    def __init__(self, seq=SEQ, depth=DEPTH, phases=("f0", "mix", "f1"), dbg=None):
        self.seq = seq
        self.depth = depth
        self.phases = phases
        self.nblk = seq // TB
        self.dbg = dbg


def build_program(cfg):
    nc = bass.Bass("TRN2", target_bir_lowering=False)
    S = cfg.seq
    NBLK = cfg.nblk
    x_in = nc.dram_tensor("x", [S, D_MODEL], F32, kind="ExternalInput").ap()
    w_in = nc.dram_tensor("w", [128, DEPTH * LAYER_W], F32, kind="ExternalInput").ap()
    vec_in = nc.dram_tensor("vecs", [128, DEPTH * NVEC], F32, kind="ExternalInput").ap()
    ident_in = nc.dram_tensor("ident", [128, 128], F32, kind="ExternalInput").ap()
    cb_in = nc.dram_tensor("cb", [128, NCB], F32, kind="ExternalInput").ap()
    cf_in = nc.dram_tensor("cf", [128, DEPTH * NCF], F32, kind="ExternalInput").ap()
    out = nc.dram_tensor("out", [S, D_MODEL], F32, kind="ExternalOutput").ap()
    wb = nc.dram_tensor("wb", [128, DEPTH * LAYER_W], BF16, kind="Internal").ap()
    xs = nc.dram_tensor("xs", [NBLK, 128, NKC, TB], F32, kind="Internal").ap()

    P = Prog()
    es = ExitStack()

    def sb(name, shape, dt):
        return es.enter_context(nc.sbuf_tensor("s_" + name, shape, dt))

    def bufs(name, n):
        return [Buf(f"{name}{i}") for i in range(n)]

    with es:
        ident = sb("ident", [128, 128], F32); b_ident = Buf("ident")
        vecs = sb("vecs", [128, DEPTH * NVEC], F32); b_vecs = Buf("vecs")
        ring = sb("ring", [128, NSLOT, SLOT], BF16); b_ring = bufs("ring", NSLOT)
        xT = sb("xT", [128, NKC, TB], F32); b_xT = bufs("xT", NKC)
        xTb = sb("xTb", [128, NKC, TB], BF16); b_xTb = bufs("xTb", NKC)
        xn = sb("xn", [128, NSUB, D_MODEL], F32); b_xn = bufs("xn", NSUB)
        gT = sb("gT", [128, NFC, TB], BF16); b_gT = bufs("gT", NFC)
        S32 = sb("S32", [128, 8, TB], F32); b_S32 = bufs("S32", 8)
        stt = sb("stt", [128, 4, 12], F32); b_stt = bufs("stt", 4)
        mv = sb("mv", [128, 4, 2], F32); b_mv = bufs("mv", 4)
        rs = sb("rs", [128, 4, 4], F32); b_rs = bufs("rs", 4); b_rs0 = bufs("rs0_", 4); b_rs1 = bufs("rs1_", 4)
        psum = [es.enter_context(nc.psum_tensor(f"ps{i}", [128, 512], F32)) for i in range(8)]
        b_ps = [Buf(f"ps{i}", excl=True) for i in range(8)]
        ps_rr = [0]

        def next_ps(subset=range(8)):
            subset = list(subset)
            i = subset[ps_rr[0] % len(subset)]
            ps_rr[0] += 1
            return psum[i], b_ps[i]

        DS_CONST = "const"
        DS_RING = [f"ring{i}" for i in range(NSLOT)]
        DS_PRE = [f"pre{i}" for i in range(8)]
        DS_XIO = ["xio0", "xio1", "xio2", "xio3"]
        DS_XS = "xs"

        P.op("pool", lambda e: e.dma_start(out=ident[:], in_=ident_in[:]), writes=[b_ident], dsem=DS_CONST)
        P.op("pool", lambda e: e.dma_start(out=vecs[:], in_=vec_in[:]), writes=[b_vecs], dsem=DS_CONST)

        b_wb = {}
        npre = [0]
        pre_list = [(l, name, sz) for l in range(cfg.depth) for (name, sz) in SLABS]

        def emit_prepass(n):
            if getattr(cfg, "no_prepass", False):
                return
            for _ in range(n):
                if not pre_list:
                    return
                l, name, sz = pre_list.pop(0)
                off = l * LAYER_W + SLAB_OFF[name][0]
                b = Buf(f"wb{l}{name}")
                b_wb[(l, name)] = b
                P.op("pool",
                     lambda e, off=off, sz=sz: e.dma_start(out=wb[:, off:off + sz], in_=w_in[:, off:off + sz],
                                                            max_dma_last_dim=8192),
                     writes=[b], dsem=DS_PRE[npre[0] % 8])
                npre[0] += 1


        slab_ctr = [0]

        def load_slab(l, name):
            off, sz = SLAB_OFF[name]
            off += l * LAYER_W
            k = slab_ctr[0] % NSLOT
            slab_ctr[0] += 1
            P.op("sp", lambda e, k=k, off=off, sz=sz: e.dma_start(out=ring[:, k, 0:sz], in_=wb[:, off:off + sz]),
                 reads=[b_wb[(l, name)]], writes=[b_ring[k]], dsem=DS_RING[k])
            return ring[:, k, :], b_ring[k]

        def vcol(l, name, j):
            c = l * NVEC + VEC_COLS[name] + j
            return vecs[:, c:c + 1]

        def mm(ps, bps, lhsT, rhs, rd, start, stop, skip=False):
            P.op("pe", lambda e: e.matmul(ps, lhsT=lhsT, rhs=rhs, start=start, stop=stop, skip_group_check=skip),
                 reads=rd, writes=[bps])

        def tr(ps, bps, in_, rd):
            P.op("pe", lambda e: e.transpose(ps, in_, ident[:]), reads=rd + [b_ident], writes=[bps])

        ln_ctr = [0]

        def layer_norm_T(l, gname, bname):
            P.tag = "ln"
            pzs = {}
            for s in range(NSUB + 1):
                if s < NSUB:
                    q = s % 2
                    pz = [next_ps(), next_ps()]
                    pzs[s] = pz
                    for d in range(NKC):
                        ps, bps = pz[d // 4]
                        tr(ps[:, (d % 4) * 128:(d % 4 + 1) * 128], bps, xT[:, d, s * 128:(s + 1) * 128], [b_xT[d]])
                    for h in range(2):
                        ps, bps = pz[h]
                        P.op("dve", lambda e, ps=ps, q=q, h=h: e.bn_stats(out=stt[:, q, h * 6:(h + 1) * 6], in_=ps[:]),
                             reads=[bps], writes=[b_stt[q]])
                    P.op("dve", lambda e, q=q: e.bn_aggr(out=mv[:, q, :], in_=stt[:, q, :]),
                         reads=[b_stt[q]], writes=[b_mv[q]])
                    P.op("act", lambda e, q=q: e.activation(out=rs[:, q, 0:1], in_=mv[:, q, 1:2], func=AF.Sqrt,
                                                            bias=eps_t[:, 0:1], scale=1.0),
                         reads=[b_mv[q], b_eps], writes=[b_rs0[q]])
                if s >= 1:
                    sp_ = s - 1
                    q = sp_ % 2
                    P.op("dve", lambda e, q=q: e.reciprocal(out=rs[:, q, 1:2], in_=rs[:, q, 0:1]),
                         reads=[b_rs0[q]], writes=[b_rs1[q]])
                    P.op("dve", lambda e, q=q: e.scalar_tensor_tensor(out=rs[:, q, 2:3], in0=mv[:, q, 0:1], scalar=-1.0,
                                                                       in1=rs[:, q, 1:2], op0=ALU.mult, op1=ALU.mult),
                         reads=[b_rs1[q], b_mv[q]], writes=[b_rs[q]])
                    for h in range(2):
                        ps, bps = pzs[sp_][h]
                        P.op("act", lambda e, ps=ps, q=q, h=h, sp_=sp_: e.activation(
                            out=xn[:, sp_, h * 512:(h + 1) * 512], in_=ps[:], func=AF.Identity,
                            bias=rs[:, q, 2:3], scale=rs[:, q, 1:2]),
                            reads=[bps, b_rs[q], b_rs1[q]], writes=[b_xn[sp_]])
            tails = []
            for d in range(NKC):
                ps, bps = next_ps()
                for s in range(NSUB):
                    tr(ps[:, s * 128:(s + 1) * 128], bps, xn[:, s, d * 128:(d + 1) * 128], [b_xn[s]])
                g = vcol(l, gname, d)
                b = vcol(l, bname, d)
                tails.append((ps, bps, g, b))
                if d % 2 == 0:
                    P.op("act", lambda e, ps=ps, d=d, g=g, b=b: e.activation(out=xTb[:, d, :], in_=ps[:], func=AF.Identity,
                                                                             bias=b, scale=g),
                         reads=[bps, b_vecs], writes=[b_xTb[d]])
                else:
                    P.op("dve", lambda e, ps=ps, d=d, g=g, b=b: e.tensor_scalar(out=xTb[:, d, :], in0=ps[:], scalar1=g,
                                                                               scalar2=b, op0=ALU.mult, op1=ALU.add),
                         reads=[bps, b_vecs], writes=[b_xTb[d]])
            for d in range(NKC):
                ps, bps, g, b = tails[d]
                if d % 2 == 0:
                    P.op("act", lambda e, ps=ps, d=d, g=g, b=b: e.activation(out=xT[:, d, :], in_=ps[:], func=AF.Identity,
                                                                             bias=b, scale=g),
                         reads=[bps, b_vecs], writes=[b_xT[d]])
                else:
                    P.op("dve", lambda e, ps=ps, d=d, g=g, b=b: e.tensor_scalar(out=xT[:, d, :], in0=ps[:], scalar1=g,
                                                                               scalar2=b, op0=ALU.mult, op1=ALU.add),
                         reads=[bps, b_vecs], writes=[b_xT[d]])

        def ffn_phase(l, i, ln_idx):
            c_res = 0.5 / DN_ALPHA
            P.tag = "ffn_in"
            b_sil = [Buf("silh0"), Buf("silh1")]
            P.op("dve", lambda e: e.memset(xn[:, 0, 0:2], 0.0), writes=[b_xn[0]] + b_sil)
            for j2 in range(NFC // 2):
                slab, bsl = load_slab(l, f"f{i}_in{j2}")
                for jj in range(2):
                    j = 2 * j2 + jj
                    pa = next_ps()
                    pu = next_ps()
                    for t, (ps, bps) in enumerate((pa, pu)):
                        for k in range(NKC):
                            o = ((jj * 2 + t) * NKC + k) * 128
                            mm(ps[:], bps, slab[:, o:o + 128], xTb[:, k, :], [bsl, b_xTb[k]], k == 0, k == NKC - 1)
                    q = j % 2
                    P.op("act", lambda e, ps=pa[0], q=q: e.activation(out=xn[:, 0, q * 512:(q + 1) * 512], in_=ps[:], func=AF.Silu),
                         reads=[pa[1]], writes=[b_sil[q]])
                    P.op("dve", lambda e, ps=pu[0], q=q, j=j: e.tensor_tensor(out=gT[:, j, :], in0=ps[:],
                                                                             in1=xn[:, 0, q * 512:(q + 1) * 512], op=ALU.mult),
                         reads=[pu[1], b_sil[q]], writes=[b_gT[j]])
            P.op("dve", lambda e: e.memset(xn[:, 0, 0:2], 0.0), writes=[b_xn[0]] + b_sil)
            P.tag = "ffn_out"
            P.op("act", lambda e: e.activation(out=acts[:, 0:1], in_=one1[:, 0:1], func=AF.Sqrt), reads=[b_eps], writes=[b_acts])
            for d in range(NKC):
                slab, bsl = load_slab(l, f"f{i}_out{d}")
                ps, bps = next_ps()
                for j in range(NFC):
                    mm(ps[:], bps, slab[:, j * 128:(j + 1) * 128], gT[:, j, :], [bsl, b_gT[j]], j == 0, j == NFC - 1)
                P.op("dve", lambda e, ps=ps, d=d: e.scalar_tensor_tensor(out=xT[:, d, :], in0=ps[:], scalar=c_res,
                                                                         in1=xT[:, d, :], op0=ALU.mult, op1=ALU.add),
                     reads=[bps, b_xT[d]], writes=[b_xT[d]])
            layer_norm_T(l, f"ln_g{ln_idx}", f"ln_b{ln_idx}")

        eps_t = sb("eps_t", [128, 1], F32); b_eps = Buf("eps")
        P.op("dve", lambda e: e.memset(eps_t[:], LN_EPS / (DN_ALPHA * DN_ALPHA)), writes=[b_eps])
        eps1 = sb("eps1", [128, 1], F32)
        one1 = sb("one1", [128, 1], F32)
        acts = sb("acts", [128, 1], F32); b_acts = Buf("acts")
        mhalf = sb("mhalf", [128, 1], F32)
        P.op("dve", lambda e: e.memset(mhalf[:], -0.5), writes=[b_eps])
        P.op("dve", lambda e: e.memset(eps1[:], LN_EPS), writes=[b_eps])
        P.op("dve", lambda e: e.memset(one1[:], 1.0), writes=[b_eps])

        DS_PF = ["pf0", "pf1", "pf2", "pf3"]

        def prefetch_x(blk):
            for s_ in range(NSUB):
                r0 = blk * TB + s_ * 128
                P.op("pool", lambda e, s_=s_, r0=r0: e.dma_start(
                    out=S32[:, 2 * s_:2 * s_ + 2, :], in_=x_in[r0:r0 + 128, :].rearrange("p (a b) -> p a b", a=2)),
                    writes=[b_S32[2 * s_], b_S32[2 * s_ + 1]], dsem=DS_PF[s_])

        def consume_x(blk):
            P.tag = "io"
            tails = []
            for d in range(NKC):
                ps, bps = next_ps()
                for s_ in range(NSUB):
                    sl = 2 * s_ + d // 4
                    tr(ps[:, s_ * 128:(s_ + 1) * 128], bps, S32[:, sl, (d % 4) * 128:(d % 4 + 1) * 128], [b_S32[sl]])
                tails.append((ps, bps))
                if d % 2 == 0:
                    P.op("act", lambda e, ps=ps, d=d: e.activation(out=xTb[:, d, :], in_=ps[:], func=AF.Identity),
                         reads=[bps], writes=[b_xTb[d]])
                else:
                    P.op("dve", lambda e, ps=ps, d=d: e.tensor_copy(out=xTb[:, d, :], in_=ps[:]),
                         reads=[bps], writes=[b_xTb[d]])
            for d in range(NKC):
                ps, bps = tails[d]
                if d % 2 == 0:
                    P.op("act", lambda e, ps=ps, d=d: e.activation(out=xT[:, d, :], in_=ps[:], func=AF.Identity),
                         reads=[bps], writes=[b_xT[d]])
                else:
                    P.op("dve", lambda e, ps=ps, d=d: e.tensor_copy(out=xT[:, d, :], in_=ps[:]),
                         reads=[bps], writes=[b_xT[d]])

        def prefetch_xs(blk):
            P.op("pool", lambda e, blk=blk: e.dma_start(out=S32[:], in_=xs[blk]),
                 reads=[b_xs[blk]], writes=b_S32, dsem=DS_PF[0])

        def consume_xs(blk):
            P.tag = "io"
            for d in range(NKC):
                if d % 2 == 0:
                    P.op("act", lambda e, d=d: e.activation(out=xT[:, d, :], in_=S32[:, d, :], func=AF.Identity),
                         reads=[b_S32[d]], writes=[b_xT[d]])
                    P.op("dve", lambda e, d=d: e.tensor_copy(out=xTb[:, d, :], in_=S32[:, d, :]),
                         reads=[b_S32[d]], writes=[b_xTb[d]])
                else:
                    P.op("dve", lambda e, d=d: e.tensor_copy(out=xT[:, d, :], in_=S32[:, d, :]),
                         reads=[b_S32[d]], writes=[b_xT[d]])
                    P.op("act", lambda e, d=d: e.activation(out=xTb[:, d, :], in_=S32[:, d, :], func=AF.Identity),
                         reads=[b_S32[d]], writes=[b_xTb[d]])

        def store_block_to_out(blk):
            P.tag = "io"
            for s in range(NSUB):
                q = s
                for h in range(2):
                    ps, bps = next_ps()
                    for dd in range(4):
                        d = h * 4 + dd
                        tr(ps[:, dd * 128:(dd + 1) * 128], bps, xT[:, d, s * 128:(s + 1) * 128], [b_xT[d]])
                    eng = "act" if h == 0 else "dve"
                    if eng == "act":
                        P.op("act", lambda e, ps=ps, q=q, h=h: e.activation(out=xn[:, q, h * 512:(h + 1) * 512],
                                                                            in_=ps[:], func=AF.Identity),
                             reads=[bps], writes=[b_xn[q]])
                    else:
                        P.op("dve", lambda e, ps=ps, q=q, h=h: e.tensor_copy(out=xn[:, q, h * 512:(h + 1) * 512],
                                                                             in_=ps[:]),
                             reads=[bps], writes=[b_xn[q]])
                r0 = blk * TB + s * 128
                P.op("pool", lambda e, q=q, r0=r0: e.dma_start(out=out[r0:r0 + 128, :], in_=xn[:, q, :]),
                     reads=[b_xn[q]], dsem=DS_XIO[q])

        def store_block_xs(blk):
            P.op("pool", lambda e, blk=blk: e.dma_start(out=xs[blk], in_=xT[:]),
                 reads=b_xT, writes=[b_xs[blk]], dsem=DS_XS)

        cbt = sb("cbt", [128, NCB], BF16); b_cb = Buf("cb")
        cft = sb("cft", [128, NCF], F32); b_cf = Buf("cf")
        wsm = sb("wsm", [128, 4, 128], BF16); b_wsm = Buf("wsm")
        kTc = sb("kTc", [128, 2, S], BF16); b_kTc = Buf("kTc")
        Vc = sb("Vc", [128, S // 128, BW], BF16); b_Vc = Buf("Vc")
        B16 = sb("B16", [128, 8, TB], BF16); b_B16 = bufs("B16", 8)
        yT4 = sb("yT4", [128, 8, TB], BF16); b_yT4 = bufs("yT4", 8)
        qT = sb("qT", [128, 2, TB], BF16); b_qT = bufs("qT", 2)
        vln = sb("vln", [128, NSUB, BW], BF16); b_vln = bufs("vln", NSUB)
        ptok = sb("ptok", [128, NSUB + 1, BW], BF16); b_ptok = bufs("ptok", NSUB + 1)
        ybuf = sb("ybuf", [128, 2, 30 + TB], BF16); b_ybuf = bufs("ybuf", 2)
        dg = sb("dg", [128, 8, 128], BF16); b_dg = bufs("dg", 8)
        P.op("pool", lambda e: e.dma_start(out=cbt[:], in_=cb_in[:], max_dma_last_dim=8192), writes=[b_cb], dsem="cb")
        ntri = cbt[:, CB_NTRI:CB_NTRI + 128]
        nustr = cbt[:, CB_NUSTR:CB_NUSTR + 128]
        mergedT = gT
        b_merged = b_gT
        dg_ctr = [0]
        st_ctr = [0]

        def tok_stats_a(src_ap, src_buf, eps_ap):
            q = st_ctr[0] % 4
            st_ctr[0] += 1
            P.op("dve", lambda e: e.bn_stats(out=stt[:, q, 0:6], in_=src_ap), reads=[src_buf], writes=[b_stt[q]])
            P.op("dve", lambda e: e.bn_aggr(out=mv[:, q, :], in_=stt[:, q, 0:6]), reads=[b_stt[q]], writes=[b_mv[q]])
            P.op("act", lambda e: e.activation(out=rs[:, q, 0:1], in_=mv[:, q, 1:2], func=AF.Sqrt, bias=eps_ap, scale=1.0),
                 reads=[b_mv[q], b_eps], writes=[b_rs0[q]])
            return q

        def tok_stats_b(q):
            P.op("dve", lambda e: e.reciprocal(out=rs[:, q, 1:2], in_=rs[:, q, 0:1]), reads=[b_rs0[q]], writes=[b_rs1[q]])
            P.op("dve", lambda e: e.scalar_tensor_tensor(out=rs[:, q, 2:3], in0=mv[:, q, 0:1], scalar=-1.0,
                                                          in1=rs[:, q, 1:2], op0=ALU.mult, op1=ALU.mult),
                 reads=[b_rs1[q], b_mv[q]], writes=[b_rs[q]])
            return rs[:, q, 1:2], rs[:, q, 2:3], [b_rs[q], b_rs1[q]]

        def layer_setup(l):
            o = l * NCF
            P.op("pool", lambda e: e.dma_start(out=cft[:], in_=cf_in[:, o:o + NCF]), writes=[b_cf], dsem="cf")
            for g in range(4):
                P.op("dve", lambda e, g=g: e.tensor_tensor(out=wsm[:, g, :], in0=cft[:, CF_SGW + g * 128:CF_SGW + (g + 1) * 128],
                                                          in1=cbt[:, CB_TRILE:CB_TRILE + 128], op=ALU.mult),
                     reads=[b_cf, b_cb], writes=[b_wsm])
            for c in range(2):
                P.op("dve", lambda e, c=c: e.memset(ybuf[:, c, 0:30], 0.0), writes=[b_ybuf[c]])

        def mixer_phase(l, blk):
            t0 = blk * TB
            nt0 = blk * NSUB
            if blk == 0 and l > 0:
                layer_setup(l)
            P.tag = "mix_qk"
            slab, bsl = load_slab(l, "m_qk")
            for cc in range(4):
                ps, bps = next_ps()
                for k in range(NKC):
                    o = (cc * NKC + k) * 128
                    mm(ps[:], bps, slab[:, o:o + 128], xTb[:, k, :], [bsl, b_xTb[k]], k == 0, k == NKC - 1)
                if cc < 2:
                    P.op("act", lambda e, ps=ps, cc=cc: e.activation(out=qT[:, cc, :], in_=ps[:], func=AF.Identity),
                         reads=[bps], writes=[b_qT[cc]])
                else:
                    P.op("dve", lambda e, ps=ps, cc=cc: e.tensor_copy(out=kTc[:, cc - 2, t0:t0 + TB], in_=ps[:]),
                         reads=[bps], writes=[b_kTc])
            P.tag = "mix_tok"
            slab, bsl = load_slab(l, "m_tok")
            slabp, bslp = load_slab(l, "m_pool")
            slabw, bslw = load_slab(l, "m_poolw")
            for s in range(NSUB):
                ps, bps = next_ps()
                for k in range(NKC):
                    mm(ps[:], bps, xTb[:, k, s * 128:(s + 1) * 128], slab[:, k * 512:(k + 1) * 512],
                       [bsl, b_xTb[k]], k == 0, k == NKC - 1)
                P.op("dve", lambda e, ps=ps, s=s: e.tensor_copy(out=Vc[:, nt0 + s, :], in_=ps[:, 0:256]),
                     reads=[bps], writes=[b_Vc])
                P.op("act", lambda e, ps=ps, s=s: e.activation(out=xn[:, s, 256:512], in_=ps[:, 256:512],
                                                               func=AF.Gelu_apprx_tanh),
                     reads=[bps], writes=[b_xn[s]])
                ps2, bps2 = next_ps()
                for k in range(NKC):
                    mm(ps2[:, 0:256], bps2, xTb[:, k, s * 128:(s + 1) * 128], slabp[:, k * 256:(k + 1) * 256],
                       [bslp, b_xTb[k]], k == 0, k == NKC - 1)
                P.op("act", lambda e, ps2=ps2, s=s: e.activation(out=ptok[:, s + 1, :], in_=ps2[:, 0:256], func=AF.Identity),
                     reads=[bps2], writes=[b_ptok[s + 1]])
            P.tag = "mix_sgln"
            qs = [tok_stats_a(xn[:, s, 256:512], b_xn[s], eps1[:, 0:1]) for s in range(NSUB)]
            for s in range(NSUB):
                rstd, nmr, brs = tok_stats_b(qs[s])
                P.op("act", lambda e, s=s, rstd=rstd, nmr=nmr: e.activation(out=xn[:, s, 512:768], in_=xn[:, s, 256:512],
                                                                            func=AF.Identity, bias=nmr, scale=rstd),
                     reads=[b_xn[s]] + brs, writes=[b_xn[s]])
                P.op("dve", lambda e, s=s: e.tensor_tensor(out=xn[:, s, 512:768], in0=xn[:, s, 512:768],
                                                           in1=cft[:, CF_SGG:CF_SGG + 256], op=ALU.mult),
                     reads=[b_xn[s], b_cf], writes=[b_xn[s]])
                P.op("dve", lambda e, s=s: e.tensor_tensor(out=vln[:, s, :], in0=xn[:, s, 512:768],
                                                           in1=cft[:, CF_SGBE:CF_SGBE + 256], op=ALU.add),
                     reads=[b_xn[s], b_cf], writes=[b_vln[s]])
            P.tag = "mix_u"
            slab, bsl = load_slab(l, "m_ua")
            slabg, bslg = load_slab(l, "m_g")
            for c in range(2):
                ps, bps = next_ps()
                for k in range(NKC):
                    o = (c * NKC + k) * 128
                    mm(ps[:], bps, slab[:, o:o + 128], xTb[:, k, :], [bsl, b_xTb[k]], k == 0, k == NKC - 1)
                P.op("act", lambda e, ps=ps, c=c: e.activation(out=S32[:, 2 + c, :], in_=ps[:], func=AF.Gelu_apprx_tanh),
                     reads=[bps], writes=[b_S32[2 + c]])
            P.tag = "mix_conv"
            for c in range(2):
                psa, bpsa = next_ps()
                for k in range(NKC):
                    o = ((2 + c) * NKC + k) * 128
                    mm(psa[:], bpsa, slab[:, o:o + 128], xTb[:, k, :], [bsl, b_xTb[k]], k == 0, k == NKC - 1)
                psg, bpsg = next_ps()
                for k in range(NKC):
                    o = (c * NKC + k) * 128
                    mm(psg[:], bpsg, slabg[:, o:o + 128], xTb[:, k, :], [bslg, b_xTb[k]], k == 0, k == NKC - 1)
                P.op("act", lambda e, psg=psg: e.activation(out=S32[:, 0, :], in_=psg[:], func=AF.Sigmoid),
                     reads=[bpsg], writes=[b_S32[0]])
                P.op("dve", lambda e, psa=psa, c=c: e.tensor_tensor(out=ybuf[:, c, 30:30 + TB], in0=psa[:], in1=S32[:, 0, :],
                                                                    op=ALU.mult),
                     reads=[bpsa, b_S32[0]], writes=[b_ybuf[c]])
            for c in range(2):
                ps, bps = next_ps()
                for k in range(CONV_W):
                    q = dg_ctr[0] % 8
                    dg_ctr[0] += 1
                    wc = vcol(l, "conv_w", c * 31 + k)
                    if k % 2 == 0:
                        P.op("dve", lambda e, q=q, wc=wc: e.tensor_scalar(out=dg[:, q, :], in0=ident[:], scalar1=wc, scalar2=None,
                                                                          op0=ALU.mult),
                             reads=[b_ident, b_vecs], writes=[b_dg[q]])
                    else:
                        P.op("act", lambda e, q=q, wc=wc: e.activation(out=dg[:, q, :], in_=ident[:], func=AF.Identity, scale=wc),
                             reads=[b_ident, b_vecs], writes=[b_dg[q]])
                    mm(ps[:], bps, dg[:, q, :], ybuf[:, c, k:k + TB], [b_dg[q], b_ybuf[c]], k == 0, k == CONV_W - 1)
                cb_ = vcol(l, "conv_b", c)
                P.op("act", lambda e, ps=ps, c=c, cb_=cb_: e.activation(out=S32[:, 6 + c, :], in_=ps[:], func=AF.Identity,
                                                                         bias=cb_, scale=1.0),
                     reads=[bps, b_vecs], writes=[b_S32[6 + c]])
                P.op("dve", lambda e, c=c: e.tensor_copy(out=ybuf[:, c, 0:30], in_=ybuf[:, c, TB:TB + 30]),
                     reads=[b_ybuf[c]], writes=[b_ybuf[c]])
            cps = []
            for s in range(NSUB):
                ps, bps = next_ps(range(4, 8))
                for c in range(2):
                    tr(ps[:, c * 128:(c + 1) * 128], bps, S32[:, 6 + c, s * 128:(s + 1) * 128], [b_S32[6 + c]])
                cps.append((ps, bps, tok_stats_a(ps[:, 0:256], bps, eps1[:, 0:1])))
            for s in range(NSUB):
                ps, bps, q = cps[s]
                rstd, nmr, brs = tok_stats_b(q)
                P.op("act", lambda e, ps=ps, s=s, rstd=rstd, nmr=nmr: e.activation(out=xn[:, s, 0:256], in_=ps[:, 0:256],
                                                                                   func=AF.Identity, bias=nmr, scale=rstd),
                     reads=[bps] + brs, writes=[b_xn[s]])
            P.tag = "mix_pool"
            for c in range(2):
                for par in range(2):
                    g = 2 * c + par
                    r0 = par * 64
                    ps, bps = next_ps(range(4))
                    for s in range(NSUB):
                        osl = ps[:, s * 128:(s + 1) * 128]
                        cur = ptok[:, s + 1, c * 128:(c + 1) * 128]
                        if blk == 0 and s == 0:
                            mm(osl, bps, cur, cbt[:, CB_BAND0 + g * 128:CB_BAND0 + (g + 1) * 128],
                               [b_ptok[s + 1], b_cb], True, True)
                        else:
                            mm(osl, bps, cur, cbt[:, CB_BAND + g * 128:CB_BAND + (g + 1) * 128],
                               [b_ptok[s + 1], b_cb], True, False)
                            mm(osl, bps, ptok[:, s, c * 128:(c + 1) * 128],
                               cbt[:, CB_BANDP + g * 128:CB_BANDP + (g + 1) * 128], [b_ptok[s], b_cb], False, True)
                    P.op("act", lambda e, ps=ps, c=c, r0=r0: e.activation(out=B16[r0:r0 + 64, 4 + c, :], in_=ps[r0:r0 + 64, :],
                                                                          func=AF.Identity),
                         reads=[bps], writes=[b_B16[4 + c]])
            P.op("dve", lambda e: e.tensor_copy(out=ptok[:, 0, :], in_=ptok[:, NSUB, :]), reads=[b_ptok[NSUB]],
                 writes=[b_ptok[0]])
            for c in range(2):
                ps, bps = next_ps(range(4))
                mm(ps[:], bps, slabw[:, c * 128:(c + 1) * 128], B16[:, 4 + c, :], [bslw, b_B16[4 + c]], True, True)
                sc_ = vcol(l, "pool_scale", c)
                P.op("act", lambda e, ps=ps, c=c, sc_=sc_: e.activation(out=yT4[:, 4 + c, :], in_=ps[:], func=AF.Identity,
                                                                        scale=sc_),
                     reads=[bps, b_vecs], writes=[b_yT4[4 + c]])
            P.tag = "mix_sgmix"
            for c in range(2):
                for par in range(2):
                    g = 2 * c + par
                    r0 = par * 64
                    ps, bps = next_ps(range(4))
                    for s in range(NSUB):
                        mm(ps[:, s * 128:(s + 1) * 128], bps, vln[:, s, c * 128:(c + 1) * 128], wsm[:, g, :],
                           [b_vln[s], b_wsm], True, True)
                    P.op("dve", lambda e, ps=ps, c=c, r0=r0: e.tensor_tensor(
                        out=S32[r0:r0 + 64, 4, :], in0=ps[r0:r0 + 64, :],
                        in1=cft[r0:r0 + 64, CF_SGB + c * 512:CF_SGB + (c + 1) * 512], op=ALU.add),
                        reads=[bps, b_cf], writes=[b_S32[4]])
                    P.op("dve", lambda e, c=c, r0=r0: e.tensor_tensor(
                        out=yT4[r0:r0 + 64, 2 + c, :], in0=S32[r0:r0 + 64, 4, :], in1=S32[r0:r0 + 64, 2 + c, :], op=ALU.mult),
                        reads=[b_S32[4], b_S32[2 + c]], writes=[b_yT4[2 + c]])
            P.tag = "mix_conv"
            for c in range(2):
                ps, bps = next_ps(range(4))
                for s in range(NSUB):
                    tr(ps[:, s * 128:(s + 1) * 128], bps, xn[:, s, c * 128:(c + 1) * 128], [b_xn[s]])
                g_ = vcol(l, "conv_ln_g", c)
                b_ = vcol(l, "conv_ln_b", c)
                P.op("act", lambda e, ps=ps, c=c, g_=g_, b_=b_: e.activation(out=yT4[:, 6 + c, :], in_=ps[:], func=AF.Silu,
                                                                             bias=b_, scale=g_),
                     reads=[bps, b_vecs], writes=[b_yT4[6 + c]])
            P.tag = "mix_attn"
            npair = nt0 + NSUB
            a_hi = npair - 1
            b_xnh = [[Buf(f"xnh{s_}{hh}") for hh in range(2)] for s_ in range(NSUB)]
            for s_ in range(NSUB):
                P.op("dve", lambda e, s_=s_: e.memset(xn[:, s_, 0:2], 0.0), writes=[b_xn[s_]] + b_xnh[s_])
            ebuf = []
            for h in range(4):
                ebuf.append([(xn[:, h, 0:512], b_xnh[h][0]), (xn[:, h, 512:1024], b_xnh[h][1]), (S32[:, h, :], b_S32[h])])
            Ebuf = [(S32[:, 4 + h, :], b_S32[4 + h]) for h in range(4)]
            spbuf = [[(gT[:, 8 + h * 3 + i, :], b_gT[8 + h * 3 + i]) for i in range(3)] for h in range(4)]
            wbuf = [[(B16[:, h * 2 + i, :], b_B16[h * 2 + i]) for i in range(2)] for h in range(4)]
            actr = [0]

            def pinfo(p):
                a = a_hi - p
                dd = a - nt0
                c0 = 128 * max(dd, 0)
                return a, dd, c0

            for t in range(npair + 2):
                if t < npair:
                    a, dd, c0 = pinfo(t)
                    pend_ln = None
                    for h in range(4):
                        c = h // 2
                        r0 = (h % 2) * 64
                        psA, bA = psum[actr[0] % 2], b_ps[actr[0] % 2]
                        actr[0] += 1
                        mm(psA[:, c0:TB], bA, kTc[r0:r0 + 64, c, a * 128:(a + 1) * 128], qT[r0:r0 + 64, c, c0:TB],
                           [b_kTc, b_qT[c]], True, True)
                        eap, eb = ebuf[h][t % 3]
                        P.op("act", lambda e, psA=psA, eap=eap, c0=c0: e.activation(out=eap[:, c0:TB], in_=psA[:, c0:TB],
                                                                                  func=AF.Exp, scale=0.125),
                             reads=[bA], writes=[eb])
                        if dd >= 0:
                            P.op("dve", lambda e, eap=eap, dd=dd, c0=c0: e.tensor_tensor(
                                out=eap[:, c0:TB], in0=eap[:, c0:TB], in1=cbt[:, CB_MASK + dd * 512 + c0:CB_MASK + (dd + 1) * 512],
                                op=ALU.mult), reads=[eb, b_cb], writes=[eb])
                        if pend_ln is not None:
                            pend_ln()
                        sap, sbf = spbuf[h][t % 3]

                        def _ln(eap=eap, eb=eb, sap=sap, sbf=sbf, c0=c0):
                            P.op("act", lambda e: e.activation(out=sap[:, c0:TB], in_=eap[:, c0:TB], func=AF.Ln, bias=1.0, scale=1.0),
                                 reads=[eb], writes=[sbf])
                        pend_ln = _ln
                    pend_ln()
                if t >= 2:
                    p = t - 2
                    a, dd, c0 = pinfo(p)
                    for h in range(4):
                        c = h // 2
                        r0 = (h % 2) * 64
                        sap, sbf = spbuf[h][p % 3]
                        wap, wbf = wbuf[h][p % 2]
                        if p < npair - 1:
                            mm(psum[2 + h][:, c0:TB], b_ps[2 + h], nustr, sap[:, c0:TB], [b_cb, sbf], False, True, skip=True)
                        mm(psum[6 + c][r0:r0 + 64, c0:TB], b_ps[6 + c], Vc[:, a, h * 64:(h + 1) * 64], wap[:, c0:TB],
                           [b_Vc, wbf], p == 0, True, skip=(p > 0))
                if 1 <= t <= npair:
                    p = t - 1
                    a, dd, c0 = pinfo(p)
                    for h in range(4):
                        sap, sbf = spbuf[h][p % 3]
                        eap, eb = ebuf[h][p % 3]
                        Eap, Ebf = Ebuf[h]
                        wap, wbf = wbuf[h][p % 2]
                        psB, bB = psum[2 + h], b_ps[2 + h]
                        mm(psB[:, c0:TB], bB, ntri, sap[:, c0:TB], [b_cb, sbf], p == 0, True, skip=(p > 0))
                        P.op("act", lambda e, psB=psB, Eap=Eap, c0=c0: e.activation(out=Eap[:, c0:TB], in_=psB[:, c0:TB], func=AF.Exp),
                             reads=[bB], writes=[Ebf])
                        P.op("dve", lambda e, wap=wap, eap=eap, Eap=Eap, c0=c0: e.tensor_tensor(
                            out=wap[:, c0:TB], in0=eap[:, c0:TB], in1=Eap[:, c0:TB], op=ALU.mult),
                            reads=[eb, Ebf], writes=[wbf])
            for c in range(2):
                P.op("act", lambda e, c=c: e.activation(out=yT4[:, c, :], in_=psum[6 + c][:], func=AF.Identity),
                     reads=[b_ps[6 + c]], writes=[b_yT4[c]])
            for s_ in range(NSUB):
                P.op("dve", lambda e, s_=s_: e.memset(xn[:, s_, 0:2], 0.0), writes=[b_xn[s_]] + b_xnh[s_])
            P.tag = "mix_gates"
            for d in range(NKC):
                slabg_, bslg_ = load_slab(l, f"m_gate{d}")
                if d % 4 == 0:
                    slabb, bslb = load_slab(l, f"m_br{d // 4}")
                for ni, n in enumerate((1, 2, 3, 0)):
                    psg, bpsg = next_ps()
                    for k in range(NKC):
                        o = (n * NKC + k) * 128
                        mm(psg[:], bpsg, slabg_[:, o:o + 128], xTb[:, k, :], [bslg_, b_xTb[k]], k == 0, k == NKC - 1)
                    gb = vcol(l, f"gate_b{n}", d)
                    gi = n % 2
                    P.op("act", lambda e, psg=psg, gi=gi, gb=gb: e.activation(out=S32[:, gi, :], in_=psg[:], func=AF.Sigmoid,
                                                                             bias=gb, scale=1.0),
                         reads=[bpsg, b_vecs], writes=[b_S32[gi]])
                    psb, bpsb = next_ps()
                    for kk in range(2):
                        o = (((d % 4) * 4 + n) * 2 + kk) * 128
                        mm(psb[:], bpsb, slabb[:, o:o + 128], yT4[:, 2 * n + kk, :], [bslb, b_yT4[2 * n + kk]], kk == 0, kk == 1)
                    ai = 4 + d % 2
                    if ni == 0:
                        P.op("dve", lambda e, psb=psb, gi=gi, ai=ai: e.tensor_tensor(out=S32[:, ai, :], in0=psb[:], in1=S32[:, gi, :],
                                                                                    op=ALU.mult),
                             reads=[bpsb, b_S32[gi]], writes=[b_S32[ai]])
                    else:
                        ti = 2 + n % 2
                        P.op("dve", lambda e, psb=psb, gi=gi, ti=ti: e.tensor_tensor(out=S32[:, ti, :], in0=psb[:], in1=S32[:, gi, :],
                                                                                    op=ALU.mult),
                             reads=[bpsb, b_S32[gi]], writes=[b_S32[ti]])
                        if ni < 3:
                            P.op("dve", lambda e, ti=ti, ai=ai: e.tensor_tensor(out=S32[:, ai, :], in0=S32[:, ai, :], in1=S32[:, ti, :],
                                                                               op=ALU.add),
                                 reads=[b_S32[ai], b_S32[ti]], writes=[b_S32[ai]])
                        else:
                            P.op("dve", lambda e, ti=ti, ai=ai, d=d: e.tensor_tensor(out=mergedT[:, d, :], in0=S32[:, ai, :],
                                                                                    in1=S32[:, ti, :], op=ALU.add),
                                 reads=[b_S32[ai], b_S32[ti]], writes=[b_merged[d]])
            P.tag = "mix_out"
            P.op("act", lambda e: e.activation(out=acts[:, 0:1], in_=one1[:, 0:1], func=AF.Sqrt), reads=[b_eps], writes=[b_acts])
            c_res = 1.0 / DN_ALPHA
            for d in range(NKC):
                if d % 4 == 0:
                    slabo, bslo = load_slab(l, f"m_out{d // 4}")
                ps, bps = next_ps()
                for k in range(NKC):
                    o = ((d % 4) * NKC + k) * 128
                    mm(ps[:], bps, slabo[:, o:o + 128], mergedT[:, k, :], [bslo, b_merged[k]], k == 0, k == NKC - 1)
                P.op("dve", lambda e, ps=ps, d=d: e.scalar_tensor_tensor(out=xT[:, d, :], in0=ps[:], scalar=c_res,
                                                                         in1=xT[:, d, :], op0=ALU.mult, op1=ALU.add),
                     reads=[bps, b_xT[d]], writes=[b_xT[d]])
            layer_norm_T(l, "ln_g1", "ln_b1")

        def names_of(ph):
            if ph == "f0":
                return [n for n, _ in SLABS if n.startswith("f0_")]
            if ph == "f1":
                return [n for n, _ in SLABS if n.startswith("f1_")]
            return [n for n, _ in SLABS if n.startswith("m_")]

        def prepass_names(l, names):
            for nm in names:
                for i, (ll, name, sz) in enumerate(pre_list):
                    if ll == l and name == nm:
                        pre_list.insert(0, pre_list.pop(i))
                        emit_prepass(1)
                        break

        phase_fn = {"f0": lambda l, blk: ffn_phase(l, 0, 0), "mix": mixer_phase, "f1": lambda l, blk: ffn_phase(l, 1, 2)}
        order = [ph for ph in ("f0", "mix", "f1") if ph in cfg.phases]
        seq = [(l, blk) for l in range(cfg.depth) for blk in range(NBLK)]

        def prefetch(l, blk):
            if l == 0:
                prefetch_x(blk)
            else:
                prefetch_xs(blk)

        b_xs = bufs("xs", NBLK)
        for idx, (l, blk) in enumerate(seq):
            P.tag = "io"
            if idx == 0:
                if order:
                    prepass_names(0, names_of(order[0])[:4])
                prefetch(l, blk)
                if "mix" in cfg.phases:
                    layer_setup(0)
                emit_prepass(sum(1 for (ll, _n, _s) in pre_list if ll == 0))
            if l > 0 and blk == 0:
                emit_prepass(10 ** 6)
            if l == 0:
                consume_x(blk)
            else:
                consume_xs(blk)
            for pi, ph in enumerate(order):
                P.tag = "io"
                if l == 0 and blk == 0:
                    prepass_names(0, names_of(ph))
                elif l == 0 and cfg.depth > 1:
                    emit_prepass(3)
                if pi == len(order) - 1 and ph != "mix" and idx + 1 < len(seq):
                    prefetch(*seq[idx + 1])
                phase_fn[ph](l, blk)
            if not (order and order[-1] != "mix") and idx + 1 < len(seq):
                prefetch(*seq[idx + 1])
            if l == cfg.depth - 1:
                store_block_to_out(blk)
            else:
                store_block_xs(blk)

        fin = sb("fin", [128, 1], F32)
        b_fin = Buf("fin")
        last_out_ops = [o for o in P.ops["pool"] if o.dsem in DS_XIO]
        tail = P.op("pool", lambda e: e.memset(fin[:], 0.0), writes=[b_fin])
        for q in range(4):
            lo = [o for o in last_out_ops if o.dsem == DS_XIO[q]]
            if lo:
                tail.deps.append(lo[-1])

        P.finalize()
        if getattr(cfg, "dbg", None) is not None:
            P.dbg = cfg.dbg
        if getattr(cfg, "pe_tags", None) is not None:
            cfg.pe_tags.extend(o.tag for o in P.ops["pe"])
        dsem_names = [DS_CONST, "cb", "cf"] + DS_RING + DS_PRE + DS_XIO + [DS_XS] + DS_PF
        esems = {e: es.enter_context(nc.semaphore(f"e_{e}")) for e in Prog.ENGS}
        dsems = {n: es.enter_context(nc.semaphore(f"d_{n}")) for n in dsem_names}
        block = es.enter_context(nc.Block())

        @block.tensor
        def _(e):
            P.emit("pe", e, esems, dsems)

        @block.scalar
        def _(e):
            P.emit("act", e, esems, dsems)

        @block.vector
        def _(e):
            P.emit("dve", e, esems, dsems)

        @block.gpsimd
        def _(e):
            P.emit("pool", e, esems, dsems)

        @block.sync
        def _(e):
            P.emit("sp", e, esems, dsems)

    return nc


_PROGRAM_CACHE = {}


def kernel(**inputs):
    inp = {k: np.asarray(v) for k, v in inputs.items()}
    x = inp["x"].astype(np.float32, copy=False)
    B = x.shape[0]
    W = pack_weights(inp)
    V = pack_vecs(inp)
    C = pack_consts(inp)
    cfg = Cfg()
    nc = build_program(cfg)
    in_maps = [dict(x=np.ascontiguousarray(x[b]), w=W, vecs=V, **C) for b in range(B)]
    res = run_bass_kernel_spmd(nc, in_maps, core_ids=list(range(B)))
    return np.stack([np.asarray(r["out"]) for r in res.results], axis=0).astype(np.float32)
```
